# Optimizing a Trainium2 kernel written in Bass

```python
import jax, jax.numpy as jnp
from jax import lax
import numpy as np

D_MODEL = 1024
BATCH = 2
SEQ = 16384
DEPTH = 4

CHUNK = 64
EPS = 1e-6
D_A = D_MODEL
CONV_A = 3
D_B = D_MODEL
LRU_HEADS = 16
LRU_BW = D_B // LRU_HEADS
CONV_B = 4
LRU_C = 8.0
D_C = D_MODEL
RWKV_HEAD = 64
RWKV_HEADS = D_C // RWKV_HEAD
R_W = 64
R_A = 64
R_V = 32
R_G = 128
LNX_EPS = RWKV_HEAD * 1e-5
D_FF = 4 * D_MODEL
N_BRANCH = 3
COLS_A = 3 * D_A
COLS_B = 2 * D_B
COLS_GATE = N_BRANCH * D_MODEL
COLS_C = 3 * D_C + R_W + R_A + R_G
N_IN = COLS_A + COLS_B + COLS_GATE + COLS_C

kernel_name = "hybrid_conv_rglru_rwkv7_block"


def _split(t, sizes):
    out, o = [], 0
    for s in sizes:
        out.append(t[..., o:o + s])
        o += s
    return out


def rms_norm(x, g):
    xf = x.astype(jnp.float32)
    y = xf * lax.rsqrt(jnp.mean(xf * xf, axis=-1, keepdims=True) + EPS)
    return (y * g.astype(jnp.float32)).astype(x.dtype)


def causal_dwconv(x, w, b=None):
    K = w.shape[0]
    S = x.shape[1]
    xp = jnp.pad(x, ((0, 0), (K - 1, 0), (0, 0)))
    y = xp[:, K - 1:K - 1 + S] * w[K - 1]
    for j in range(K - 1):
        y = y + xp[:, j:j + S] * w[j]
    return y if b is None else y + b


def token_shift(p, mu):
    prev = jnp.pad(p, ((0, 0), (1, 0), (0, 0)))[:, :-1]
    return p + (prev - p) * mu


def rg_lru(x, w_a, b_a, w_i, b_i, a_param):
    Bsz, S, C = x.shape
    f32 = jnp.float32
    xb = x.reshape(Bsz, S, LRU_HEADS, LRU_BW)
    gate_a = jax.nn.sigmoid((jnp.einsum('bshi,hij->bshj', xb, w_a).reshape(Bsz, S, C) + b_a).astype(f32))
    gate_i = jax.nn.sigmoid((jnp.einsum('bshi,hij->bshj', xb, w_i).reshape(Bsz, S, C) + b_i).astype(f32))
    log_a = -LRU_C * gate_a * jax.nn.softplus(a_param.astype(f32))
    a = jnp.exp(log_a)
    mult = jnp.sqrt(-jnp.expm1(2.0 * log_a))
    mult = jnp.where(jnp.arange(S)[None, :, None] == 0, 1.0, mult)
    u = x.astype(f32) * gate_i * mult

    def combine(left, right):
        a_l, u_l = left
        a_r, u_r = right
        return a_l * a_r, a_r * u_l + u_r

    _, h = lax.associative_scan(combine, (a, u), axis=1)
    return h.astype(x.dtype)


def rwkv7_recurrence(r, decay, k, v, kk, a):
    Bsz, S, H, N = r.shape
    nc = S // CHUNK

    def to_chunks(t):
        return t.reshape(Bsz, nc, CHUNK, H, N).transpose(1, 2, 0, 3, 4)

    def step(state, inp):
        r_t, w_t, k_t, v_t, kk_t, a_t = inp
        sa = jnp.einsum('bhvk,bhk->bhv', state, -kk_t)
        state = (state * w_t[:, :, None, :]
                 + sa[..., None] * (kk_t * a_t)[:, :, None, :]
                 + v_t[..., None] * k_t[:, :, None, :])
        y_t = jnp.einsum('bhvk,bhk->bhv', state, r_t)
        return state, y_t

    def chunk_step(state, chunk_inp):
        return lax.scan(step, state, chunk_inp)

    state0 = jnp.zeros((Bsz, H, N, N), jnp.float32)
    inputs = (to_chunks(r), to_chunks(decay), to_chunks(k), to_chunks(v), to_chunks(kk), to_chunks(a))
    _, y = lax.scan(chunk_step, state0, inputs)
    return y.transpose(2, 0, 1, 3, 4).reshape(Bsz, S, H, N)


def rwkv7_mix(pc, h, v_first, w0, w2, a0, a2, g2, k_k, k_a, r_k, lnx_g, lnx_b, vres):
    f32 = jnp.float32
    r, k, v, xw, xa, xg = _split(pc, (D_C, D_C, D_C, R_W, R_A, R_G))
    w_log = -jax.nn.softplus(-(w0 + jnp.tanh(xw) @ w2).astype(f32)) - 0.5
    decay = jnp.exp(-jnp.exp(w_log))
    if vres is None:
        v_first = v
    else:
        v0, v1, v2 = vres
        v = v + (v_first - v) * jax.nn.sigmoid(v0 + (h @ v1) @ v2)
    a = jax.nn.sigmoid(a0 + xa @ a2)
    g = jax.nn.sigmoid(xg) @ g2

    def heads(t):
        return t.reshape(t.shape[:-1] + (RWKV_HEADS, RWKV_HEAD)).astype(f32)

    kk = heads(k * k_k)
    kk = kk / jnp.maximum(jnp.sqrt(jnp.sum(kk * kk, axis=-1, keepdims=True)), 1e-12)
    k = k * (1.0 + (a - 1.0) * k_a)
    rh, kh, vh, ah = heads(r), heads(k), heads(v), heads(a)
    y = rwkv7_recurrence(rh, heads(decay), kh, vh, kk, ah)
    mu = jnp.mean(y, axis=-1, keepdims=True)
    var = jnp.mean(jnp.square(y - mu), axis=-1, keepdims=True)
    y = (y - mu) * lax.rsqrt(var + LNX_EPS)
    y = y + jnp.sum(rh * kh * r_k.astype(f32), axis=-1, keepdims=True) * vh
    y = y.reshape(y.shape[:2] + (D_C,))
    return y, g, v_first


def setup_inputs(seed: int = 0) -> dict:
    key = jax.random.key(seed)
    ks = iter(jax.random.split(key, 48))
    L = DEPTH

    def nrm(shape, scale):
        return jax.random.normal(next(ks), shape, jnp.float32) * scale

    def uni(shape, lo, hi):
        return jax.random.uniform(next(ks), shape, jnp.float32, lo, hi)

    rad = uni((L, D_B), 0.9, 0.999)
    lru_a_param = jnp.log(jnp.expm1(-jnp.log(rad)))
    LV = max(L - 1, 0)
    return {
        "x": nrm((BATCH, SEQ, D_MODEL), 1.0),
        "norm1_g": 1.0 + nrm((L, D_MODEL), 0.05),
        "w_in": nrm((L, D_MODEL, N_IN), D_MODEL ** -0.5),
        "merge_b": nrm((L, COLS_GATE), 0.01),
        "conv_a_w": nrm((L, CONV_A, D_A), CONV_A ** -0.5),
        "lru_conv_w": nrm((L, CONV_B, D_B), CONV_B ** -0.5),
        "lru_conv_b": nrm((L, D_B), 0.01),
        "lru_wa": nrm((L, LRU_HEADS, LRU_BW, LRU_BW), LRU_BW ** -0.5),
        "lru_ba": nrm((L, D_B), 0.01),
        "lru_wi": nrm((L, LRU_HEADS, LRU_BW, LRU_BW), LRU_BW ** -0.5),
        "lru_bi": nrm((L, D_B), 0.01),
        "lru_a_param": lru_a_param,
        "rwkv_mu": uni((L, COLS_C), 0.0, 1.0),
        "rwkv_w0": uni((L, D_C), -6.0, 1.0),
        "rwkv_w2": nrm((L, R_W, D_C), 0.1 * R_W ** -0.5),
        "rwkv_a0": nrm((L, D_C), 0.1),
        "rwkv_a2": nrm((L, R_A, D_C), 0.1 * R_A ** -0.5),
        "rwkv_g2": nrm((L, R_G, D_C), R_G ** -0.5),
        "rwkv_kk": 0.85 + nrm((L, D_C), 0.05),
        "rwkv_ka": 1.0 + nrm((L, D_C), 0.05),
        "rwkv_rk": nrm((L, RWKV_HEADS, RWKV_HEAD), 0.1),
        "rwkv_lnx_g": 1.0 + nrm((L, D_C), 0.05),
        "rwkv_lnx_b": nrm((L, D_C), 0.01),
        "rwkv_v0": 1.0 + nrm((LV, D_C), 0.1),
        "rwkv_v1": nrm((LV, D_MODEL, R_V), D_MODEL ** -0.5),
        "rwkv_v2": nrm((LV, R_V, D_C), 0.1 * R_V ** -0.5),
        "w_out": nrm((L, D_MODEL, D_MODEL), D_MODEL ** -0.5),
        "norm2_g": 1.0 + nrm((L, D_MODEL), 0.05),
        "mlp_w1": nrm((L, D_MODEL, D_FF), D_MODEL ** -0.5),
        "mlp_w2": nrm((L, D_FF, D_MODEL), D_FF ** -0.5),
        "final_g": 1.0 + nrm((D_MODEL,), 0.05),
    }


def reference(x, norm1_g, w_in, merge_b, conv_a_w, lru_conv_w, lru_conv_b, lru_wa, lru_ba, lru_wi,
              lru_bi, lru_a_param, rwkv_mu, rwkv_w0, rwkv_w2, rwkv_a0, rwkv_a2, rwkv_g2, rwkv_kk,
              rwkv_ka, rwkv_rk, rwkv_lnx_g, rwkv_lnx_b, rwkv_v0, rwkv_v1, rwkv_v2, w_out, norm2_g,
              mlp_w1, mlp_w2, final_g):
    v_first = None
    for l in range(DEPTH):
        h = rms_norm(x, norm1_g[l])
        p = h @ w_in[l]
        pa, pb, pg, pc = _split(p, (COLS_A, COLS_B, COLS_GATE, COLS_C))

        b_a, c_a, x_a = _split(pa, (D_A, D_A, D_A))
        y_a = b_a * causal_dwconv(c_a * x_a, conv_a_w[l])

        x_b, g_b = _split(pb, (D_B, D_B))
        u = causal_dwconv(x_b, lru_conv_w[l], lru_conv_b[l])
        y_b = rg_lru(u, lru_wa[l], lru_ba[l], lru_wi[l], lru_bi[l], lru_a_param[l]) * jax.nn.gelu(g_b, approximate=True)

        pc = token_shift(pc, rwkv_mu[l])
        vres = None if l == 0 else (rwkv_v0[l - 1], rwkv_v1[l - 1], rwkv_v2[l - 1])
        f32 = jnp.float32
        r, k, v, xw, xa, xg = _split(pc, (D_C, D_C, D_C, R_W, R_A, R_G))
        w_log = -jax.nn.softplus(-(rwkv_w0[l] + jnp.tanh(xw) @ rwkv_w2[l]).astype(f32)) - 0.5
        decay = jnp.exp(-jnp.exp(w_log))
        if vres is None:
            v_first = v
        else:
            v0, v1, v2 = vres
            v = v + (v_first - v) * jax.nn.sigmoid(v0 + (h @ v1) @ v2)
        a = jax.nn.sigmoid(rwkv_a0[l] + xa @ rwkv_a2[l])
        g_c = jax.nn.sigmoid(xg) @ rwkv_g2[l]
        hs = x.shape[:2] + (RWKV_HEADS, RWKV_HEAD)
        kk = (k * rwkv_kk[l]).astype(f32).reshape(hs)
        kk = kk / jnp.maximum(jnp.sqrt(jnp.sum(kk * kk, axis=-1, keepdims=True)), 1e-12)
        k = k * (1.0 + (a - 1.0) * rwkv_ka[l])
        rh = r.astype(f32).reshape(hs)
        kh = k.astype(f32).reshape(hs)
        vh = v.astype(f32).reshape(hs)
        ah = a.astype(f32).reshape(hs)
        yc = rwkv7_recurrence(rh, decay.reshape(hs), kh, vh, kk, ah)
        mu = jnp.mean(yc, axis=-1, keepdims=True)
        var = jnp.mean(jnp.square(yc - mu), axis=-1, keepdims=True)
        yc = ((yc - mu) * lax.rsqrt(var + LNX_EPS)).reshape(x.shape[:2] + (D_C,))
        yc = yc * rwkv_lnx_g[l].astype(f32) + rwkv_lnx_b[l].astype(f32)
        bonus = jnp.sum(rh * kh * rwkv_rk[l].astype(f32), axis=-1, keepdims=True) * vh
        yc = yc + bonus.reshape(x.shape[:2] + (D_C,))
        y_c = (yc * g_c.astype(f32)).astype(x.dtype)

        gate_a, gate_b, gate_c = _split(jax.nn.sigmoid(pg + merge_b[l]), (D_MODEL, D_MODEL, D_MODEL))
        m = gate_a * y_a + gate_b * y_b + gate_c * y_c
        x = x + m @ w_out[l]

        h2 = rms_norm(x, norm2_g[l])
        x = x + jnp.square(jax.nn.relu(h2 @ mlp_w1[l])) @ mlp_w2[l]
    return rms_norm(x, final_g)
```

```python
import numpy as np
from contextlib import ExitStack
import concourse.bass as bass
import concourse.mybir as mybir
from concourse.bass_utils import run_bass_kernel_spmd

F32 = mybir.dt.float32
BF16 = mybir.dt.bfloat16
AF = mybir.ActivationFunctionType
ALU = mybir.AluOpType

D = 1024
NJ = 8
T = 512
CH = 64
NCH = T // CH
N_IN = 11520
D_FF = 4096
EPS = 1e-6
LNX_EPS = 64 * 1e-5
DECAY_C = 0.6065306597126334
GELU_C = 1.5957691216057308

_VNAMES = [("g1", 8), ("g2n", 8), ("mbA", 8), ("mbB", 8), ("mbC", 8), ("cA0", 8), ("cA1", 8), ("cA2", 8),
           ("cB0", 8), ("cB1", 8), ("cB2", 8), ("cB3", 8), ("cBb", 8), ("ba", 8), ("bi", 8), ("ap", 8),
           ("mu_lr", 2), ("mu_r", 8), ("mu_k", 8), ("mu_v", 8), ("w0", 8), ("a0", 8), ("kkv", 8), ("ka", 8),
           ("rk", 8), ("lg", 8), ("lb", 8), ("v0", 8)]
VOFF = {}
_o = 0
for _n, _w in _VNAMES:
    VOFF[_n] = _o
    _o += _w
NVEC = _o

WOFF = {}
_o = 0
for _n, _w in [("win", 8 * N_IN), ("wout", 8 * D), ("w1", 8 * D_FF), ("w2", 32 * D), ("w2a2", D), ("g2", D),
               ("v1", 8 * 32), ("v2", D), ("wa", 8 * 128), ("wi", 8 * 128)]:
    WOFF[_n] = _o
    _o += _w
WTOT = _o
CVT = 4096

COFF = {}
_o = 0
for _n, _w in [("ident", 128), ("ones_bd", 128), ("mSU", 128), ("mSL", 128), ("mIU", 64), ("cmask", 512),
               ("hmask", 2), ("nhmask", 2), ("ones", 128), ("eps", 1), ("lnxeps", 1)]:
    COFF[_n] = _o
    _o += _w
NCONST = _o


def _cperm():
    c0 = 8192
    perm = list(range(c0 + 3072, c0 + 3328))
    for j in range(NJ):
        s = slice(j * 128, (j + 1) * 128)
        for base in (0, 1024, 2048, 3072, 4096, 5120, 6144, 7168, c0, c0 + 1024, c0 + 2048):
            perm += list(range(base + s.start, base + s.stop))
    return np.array(perm, dtype=np.int64)


def _fm(v):
    return np.ascontiguousarray(np.asarray(v, np.float32).reshape(-1, 128).T)


def _pack_layer_vecs(inp, l, L):
    out = np.zeros((128, NVEC), np.float32)

    def put(name, arr):
        a = _fm(arr)
        out[:, VOFF[name]:VOFF[name] + a.shape[1]] = a

    put("g1", inp["norm1_g"][l]); put("g2n", inp["norm2_g"][l])
    mb = inp["merge_b"][l]
    put("mbA", mb[0:1024]); put("mbB", mb[1024:2048]); put("mbC", mb[2048:3072])
    for t in range(3):
        put("cA%d" % t, inp["conv_a_w"][l, t])
    for t in range(4):
        put("cB%d" % t, inp["lru_conv_w"][l, t])
    put("cBb", inp["lru_conv_b"][l]); put("ba", inp["lru_ba"][l]); put("bi", inp["lru_bi"][l])
    put("ap", inp["lru_a_param"][l])
    mu = inp["rwkv_mu"][l]
    put("mu_lr", mu[3072:3328]); put("mu_r", mu[0:1024]); put("mu_k", mu[1024:2048]); put("mu_v", mu[2048:3072])
    put("w0", inp["rwkv_w0"][l]); put("a0", inp["rwkv_a0"][l]); put("kkv", inp["rwkv_kk"][l])
    put("ka", inp["rwkv_ka"][l]); put("rk", inp["rwkv_rk"][l].reshape(-1))
    put("lg", inp["rwkv_lnx_g"][l]); put("lb", inp["rwkv_lnx_b"][l])
    if l > 0:
        put("v0", inp["rwkv_v0"][l - 1])
    return out


def _pack_layer_w(inp, l, perm):
    out = np.zeros((128, WTOT), np.float32)

    def put(name, a):
        a = np.asarray(a, np.float32).reshape(a.shape[0], -1)
        out[:a.shape[0], WOFF[name]:WOFF[name] + a.shape[1]] = a

    def kc(w):
        K, N = w.shape
        return np.ascontiguousarray(np.asarray(w, np.float32).reshape(K // 128, 128, N).transpose(1, 0, 2))

    put("win", kc(inp["w_in"][l][:, perm]))
    put("wout", kc(inp["w_out"][l]))
    put("w1", kc(inp["mlp_w1"][l]))
    put("w2", kc(inp["mlp_w2"][l]))
    put("w2a2", np.concatenate([inp["rwkv_w2"][l], inp["rwkv_a2"][l]], axis=0))
    put("g2", inp["rwkv_g2"][l])
    if l > 0:
        put("v1", kc(inp["rwkv_v1"][l - 1]))
        put("v2", inp["rwkv_v2"][l - 1])
    for nm, key in (("wa", "lru_wa"), ("wi", "lru_wi")):
        bd = np.zeros((128, 8, 128), np.float32)
        w = np.asarray(inp[key][l], np.float32)
        for j in range(8):
            bd[0:64, j, 0:64] = w[2 * j]
            bd[64:128, j, 64:128] = w[2 * j + 1]
        put(nm, bd)
    return out


def _consts():
    c = np.zeros((128, NCONST), np.float32)
    p = np.arange(128)
    hp, sp = p // 64, p % 64
    c[:, COFF["ident"]:COFF["ident"] + 128] = np.eye(128)
    same = (hp[:, None] == hp[None, :])
    c[:, COFF["ones_bd"]:COFF["ones_bd"] + 128] = same
    c[:, COFF["mSU"]:COFF["mSU"] + 128] = same & (sp[:, None] < sp[None, :])
    c[:, COFF["mSL"]:COFF["mSL"] + 128] = same & (sp[:, None] > sp[None, :])
    c[:, COFF["mIU"]:COFF["mIU"] + 64] = (sp[:, None] <= np.arange(64)[None, :])
    c[:, COFF["cmask"]:COFF["cmask"] + 512] = (np.arange(512) % 64 != 0)[None, :]
    for hh in range(2):
        c[:, COFF["hmask"] + hh] = (hp == hh)
        c[:, COFF["nhmask"] + hh] = -(hp == hh).astype(np.float32)
    c[:, COFF["ones"]:COFF["ones"] + 128] = 1.0
    c[:, COFF["eps"]] = EPS
    c[:, COFF["lnxeps"]] = LNX_EPS
    return c


class _Ctr:
    def __init__(self, S, name):
        self.S, self.name, self.gen = S, name, 0
        self.sid = S.new_sem(name)
        self.v = 0

    def bump(self, inc):
        if self.v + inc > 30000:
            self.gen += 1
            self.sid = self.S.new_sem("%s_%d" % (self.name, self.gen))
            self.v = 0
        self.v += inc
        return self.sid, self.v


class Sched:
    ENG = ("pe", "act", "dve", "pool", "sp")

    def __init__(self, nc, stack):
        self.nc, self.stack = nc, stack
        self.handles = []
        self.streams = {e: [] for e in self.ENG}
        self.ctr = {e: _Ctr(self, "e_" + e) for e in self.ENG if e != "sp"}
        self.waited = {e: {} for e in self.ENG}
        self.lw, self.rd, self.dctr = {}, {}, {}
        self.nops = 0

    def new_sem(self, name):
        h = self.stack.enter_context(self.nc.semaphore(name))
        self.handles.append(h)
        return len(self.handles) - 1

    def _deps(self, eng, reads, writes):
        need = {}

        def add(ev):
            sid, val, src = ev
            if self.waited[eng].get(sid, 0) >= val:
                return
            if need.get(sid, 0) < val:
                need[sid] = val

        for k in reads:
            ev = self.lw.get(k)
            if ev is not None and not (eng == "pe" and ev[2] == "pe"):
                add(ev)
        for k in writes:
            ev = self.lw.get(k)
            if ev is not None and ev[2] != eng:
                add(ev)
            for sid, (val, src) in self.rd.get(k, {}).items():
                if src != eng:
                    add((sid, val, src))
        for sid, val in need.items():
            self.waited[eng][sid] = val
            self.streams[eng].append(("w", sid, val))

    def _record(self, me, reads, writes):
        for k in writes:
            self.lw[k] = me
            self.rd[k] = {}
        for k in reads:
            d = self.rd.setdefault(k, {})
            if d.get(me[0], (0, None))[0] < me[1]:
                d[me[0]] = (me[1], me[2])

    def op(self, eng, fn, reads=(), writes=()):
        self._deps(eng, reads, writes)
        sid, val = self.ctr[eng].bump(1)
        self.streams[eng].append(("o", fn, sid))
        self._record((sid, val, eng), reads, writes)
        self.nops += 1

    def dma(self, out_ap, in_ap, reads=(), writes=(), semkey=None, q="sp"):
        self._deps(q, reads, writes)
        c = self.dctr.get(semkey)
        if c is None:
            c = self.dctr[semkey] = _Ctr(self, "d_" + str(semkey))
        sid, val = c.bump(16)
        self.streams[q].append(("d", out_ap, in_ap, sid))
        self._record((sid, val, "dma"), reads, writes)
        self.nops += 1

    def cc(self, fn, reads=(), writes=()):
        self._deps("pool", reads, writes)
        c = self.dctr.get("cc")
        if c is None:
            c = self.dctr["cc"] = _Ctr(self, "d_cc")
        sid, val = c.bump(1)
        self.streams["pool"].append(("o", fn, sid))
        self._record((sid, val, "dma"), reads, writes)
        self.nops += 1

    def barrier(self):
        evs = []
        for e, c in self.ctr.items():
            if c.v > 0:
                evs.append((c.sid, c.v))
        for c in self.dctr.values():
            if c.v > 0:
                evs.append((c.sid, c.v))
        for eng in self.ENG:
            for sid, val in evs:
                if self.waited[eng].get(sid, 0) < val:
                    self.waited[eng][sid] = val
                    self.streams[eng].append(("w", sid, val))

    def final_wait(self, keys):
        self._deps("sp", list(keys), [])

    def emit(self, block):
        H = self.handles

        def mk(name):
            items = self.streams[name]

            def body(e):
                for it in items:
                    if it[0] == "w":
                        e.wait_ge(H[it[1]], it[2])
                    elif it[0] == "o":
                        it[1](e).then_inc(H[it[2]], 1)
                    else:
                        e.dma_start(out=it[1], in_=it[2]).then_inc(H[it[3]], 16)
            return body

        block.tensor(mk("pe"))
        block.scalar(mk("act"))
        block.vector(mk("dve"))
        block.gpsimd(mk("pool"))
        block.sync(mk("sp"))


class Buf:
    def __init__(self, t, k):
        self.t, self.k = t, k


class Pool:
    def __init__(self, bufs, name):
        self.bufs = bufs
        self.name = name
        self.i = 0

    def get(self):
        b = self.bufs[self.i % len(self.bufs)]
        self.i += 1
        return b


DBG_NAMES = []


NSTAGE = 4
NGROUP = 2


def flag_layout(NTICK):
    off = {"selx": 0, "sel": NTICK, "vres": (1 + NSTAGE) * NTICK, "keep": (1 + NSTAGE) * NTICK + 1,
           "first": (2 + NSTAGE) * NTICK + 1}
    return off, (3 + NSTAGE) * NTICK + 1


def build_program(NT, debug=False):
    DEPTH = 1
    NTOK = NT * T
    FOFF, NFLAG = flag_layout(NT)
    NV_ALL = NVEC + 8 + NFLAG
    nc = bass.Bass("TRN2", target_bir_lowering=False)
    x_in = nc.dram_tensor("x", [128, NJ, NTOK], F32, kind="ExternalInput").ap()
    vecs_d = nc.dram_tensor("vecs", [128, NV_ALL], F32, kind="ExternalInput").ap()
    wts_d = nc.dram_tensor("wts", [DEPTH, 128, WTOT], F32, kind="ExternalInput").ap()
    cst_d = nc.dram_tensor("consts", [128, NCONST], F32, kind="ExternalInput").ap()
    out_d = nc.dram_tensor("out", [128, NJ, NTOK], F32, kind="ExternalOutput").ap()
    snd_t = [nc.dram_tensor("snd%d" % q, [128, 4 * T], F32) for q in range(4)]
    rcv_t = [nc.dram_tensor("rcv%d" % q, [NSTAGE * 128, 4 * T], F32) for q in range(4)]

    def snd_c(c):
        return snd_t[c // 4].ap().rearrange("p (c t) -> p c t", c=4)[:, c % 4, :]

    def rcv_c(c):
        return rcv_t[c // 4].ap().rearrange("(r p) (c t) -> p r c t", r=NSTAGE, c=4)[:, :, c % 4, :]
    wb_d = nc.dram_tensor("wb", [DEPTH, 128, WTOT], BF16).ap()
    dbg_d = nc.dram_tensor("dbg", [24, 128, T], F32, kind="ExternalOutput").ap() if debug else None
    del DBG_NAMES[:]

    with ExitStack() as stack:
        S = Sched(nc, stack)

        def sb(name, shape, dt):
            return stack.enter_context(nc.sbuf_tensor("s_" + name, shape, dt))

        def mkpool(name, n, shape, dt):
            return Pool([Buf(sb("%s%d" % (name, i), shape, dt), (name, i)) for i in range(n)], name)

        def act(out, in_, func, reads, writes, bias=None, scale=None):
            kw = {}
            if bias is not None:
                kw["bias"] = bias
            if scale is not None:
                kw["scale"] = scale
            S.op("act", lambda e: e.activation(out=out, in_=in_, func=func, **kw), reads, writes)

        def tt(out, a, b, op, reads, writes, eng="dve"):
            S.op(eng, lambda e: e.tensor_tensor(out=out, in0=a, in1=b, op=op), reads, writes)

        def ts(out, a, s1, s2, op0, op1, reads, writes, eng="dve"):
            if op1 is None:
                S.op(eng, lambda e: e.tensor_scalar(out=out, in0=a, scalar1=s1, scalar2=None, op0=op0), reads, writes)
            else:
                S.op(eng, lambda e: e.tensor_scalar(out=out, in0=a, scalar1=s1, scalar2=s2, op0=op0, op1=op1),
                     reads, writes)

        def stt(out, a, s, b, op0, op1, reads, writes, eng="dve"):
            eng = "dve"
            S.op(eng, lambda e: e.scalar_tensor_tensor(out=out, in0=a, scalar=s, in1=b, op0=op0, op1=op1),
                 reads, writes)

        def cp(out, in_, reads, writes, eng="dve"):
            S.op(eng, lambda e: e.tensor_copy(out=out, in_=in_), reads, writes)

        def rcp(out, in_, reads, writes):
            S.op("dve", lambda e: e.reciprocal(out=out, in_=in_), reads, writes)

        def mm(out, lhsT, rhs, start, stop, reads, writes):
            S.op("pe", lambda e: e.matmul(out, lhsT=lhsT, rhs=rhs, start=start, stop=stop), reads, writes)

        def tr(out, in_, ident, reads, writes):
            S.op("pe", lambda e: e.transpose(out, in_, ident), reads, writes)

        def dbg(name, ap, key, cond=True):
            if debug and cond:
                n = len(DBG_NAMES)
                DBG_NAMES.append(name)
                S.dma(dbg_d[n], ap, reads=[key], writes=[("dbg", n)], semkey=("dbg", n))

        cst = sb("cst", [128, NCONST], F32)
        cstb = sb("cstb", [128, 384], BF16)
        vecs = sb("vecs", [128, NV_ALL], F32)
        der = sb("der", [128, DEPTH * 16], F32)
        S.dma(cst[:], cst_d, writes=["cst"], semkey="cst")
        S.dma(vecs[:], vecs_d, writes=["vecs"], semkey="vecs")
        cp(cstb[:, 0:256], cst[:, COFF["ident"]:COFF["ident"] + 256], ["cst"], ["cstb"])
        cp(cstb[:, 256:384], cst[:, COFF["ones"]:COFF["ones"] + 128], ["cst"], ["cstb"])
        ident_b = cstb[:, 0:128]
        onesbd_b = cstb[:, 128:256]
        ones_b = cstb[:, 256:384]

        def C(name, w):
            return cst[:, COFF[name]:COFF[name] + w]

        def FL(name, k=0):
            o = NVEC + 8 + FOFF[name] + k
            return vecs[:, o:o + 1]

        def V(l, name, j):
            o = l * NVEC + VOFF[name] + j
            return vecs[:, o:o + 1]

        with ExitStack() as pstack:
            stg = [pstack.enter_context(nc.sbuf_tensor("stg%d" % i, [128, CVT], F32)) for i in range(3)]
            stb = [pstack.enter_context(nc.sbuf_tensor("stb%d" % i, [128, CVT], BF16)) for i in range(3)]
            n = 0
            for l in range(DEPTH):
                o = l * 16
                act(der[:, o:o + 8], vecs[:, l * NVEC + VOFF["ap"]:l * NVEC + VOFF["ap"] + 8], AF.Exp,
                    ["vecs"], ["der"])
                ts(der[:, o:o + 8], der[:, o:o + 8], 1.0, None, ALU.add, None, ["der"], ["der"])
                act(der[:, o:o + 8], der[:, o:o + 8], AF.Ln, ["der"], ["der"])
                ts(der[:, o:o + 8], der[:, o:o + 8], -8.0, None, ALU.mult, None, ["der"], ["der"])
                ts(der[:, o + 8:o + 16], vecs[:, l * NVEC + VOFF["ka"]:l * NVEC + VOFF["ka"] + 8], -1.0, 1.0,
                   ALU.mult, ALU.add, ["vecs"], ["der"])
                for c0 in range(0, WTOT, CVT):
                    w = min(CVT, WTOT - c0)
                    s = n % 3
                    S.dma(stg[s][:, 0:w], wts_d[l][:, c0:c0 + w], writes=[("stg", s)], semkey=("stg", s))
                    if n % 2 == 0:
                        cp(stb[s][:, 0:w], stg[s][:, 0:w], [("stg", s)], [("stb", s)])
                    else:
                        act(stb[s][:, 0:w], stg[s][:, 0:w], AF.Copy, [("stg", s)], [("stb", s)])
                    S.dma(wb_d[l][:, c0:c0 + w], stb[s][:, 0:w], reads=[("stb", s)], writes=["wbd"],
                          semkey=("stb", s))
                    n += 1
        S.barrier()

        xT = sb("xT", [128, NJ, T], F32)
        hT = sb("hT", [128, NJ, T], BF16)
        mb = sb("mb", [128, NJ, T], BF16)
        HG = 8
        hid = mb
        WB = mkpool("wb", 4, [128, 4096], BF16)
        wa_s = sb("wa_s", [128, 8, 128], BF16)
        wi_s = sb("wi_s", [128, 8, 128], BF16)
        w2a2_s = sb("w2a2_s", [128, D], BF16)
        g2_s = sb("g2_s", [128, D], BF16)
        v1_s = sb("v1_s", [128, 8, 32], BF16)
        v2_s = sb("v2_s", [128, D], BF16)
        TMP = mkpool("tmp", 9, [128, T], F32)
        PB = mkpool("pb", 3, [128, T + 3], F32)
        SQ = mkpool("sq", 2, [128, T], BF16)
        TB = mkpool("tb", 3, [128, T], BF16)
        _psb = [Buf(stack.enter_context(nc.psum_tensor("ps%d" % i, [128, 512], F32)), ("ps", i))
                for i in range(8)]
        PS = Pool(_psb[0:5], "ps")
        PQ = Pool(_psb[5:7], "pq")
        PSY = _psb[7]
        cTS = sb("cTS", [128, 26], F32)
        cA = sb("cA", [128, NJ, 2], F32)
        cB = sb("cB", [128, NJ, 3], F32)
        cH = sb("cH", [128, NJ], F32)
        Hf = sb("Hf", [128, NJ, 128], F32)
        Hb = sb("Hb", [128, NJ, 128], BF16)
        xwa = sb("xwa", [128, T], BF16)
        sgx = sb("sgx", [128, T], BF16)
        hv1 = sb("hv1", [32, T], BF16)
        RW = {nm: sb("rw_" + nm, [128, T], F32) for nm in
              ("r", "k", "v", "kk", "b", "ac", "sw", "cum", "epos", "eneg")}
        BD = {nm: sb("bd_" + nm, [128, NCH, 128], BF16) for nm in ("b", "k", "h", "v")}
        BDA = [sb("bd_a%d" % p, [128, NCH, 128], BF16) for p in range(2)]
        rtP = [sb("rt%d" % p, [128, T], BF16) for p in range(2)]
        BhTP = [sb("BhT%d" % p, [128, NCH, 128], BF16) for p in range(2)]
        KhTP = [sb("KhT%d" % p, [128, NCH, 128], BF16) for p in range(2)]
        VTP = [sb("VT%d" % p, [128, NCH, 128], BF16) for p in range(2)]
        GP = [sb("G%d" % p, [128, T], F32) for p in range(2)]
        BpP = [sb("Bp%d" % p, [128, T], F32) for p in range(2)]
        gamP = [sb("gam%d" % p, [128, NCH], F32) for p in range(2)]
        Nq = [[sb("Nq%d_%d" % (g, i), [128, 4, 128], BF16) for i in range(2)] for g in range(2)]
        Mq = [[sb("Mq%d_%d" % (g, i), [128, 4, 128], BF16) for i in range(2)] for g in range(2)]
        Xq = [[sb("Xq%d_%d" % (g, i), [128, 4, 128], BF16) for i in range(2)] for g in range(2)]
        XfP = [sb("Xf%d" % p, [128, NCH, 128], BF16) for p in range(2)]
        AakTP = [sb("AakT%d" % p, [128, NCH, 128], BF16) for p in range(2)]
        ArbTP = [sb("ArbT%d" % p, [128, NCH, 64], BF16) for p in range(2)]
        ArkTP = [sb("ArkT%d" % p, [128, NCH, 64], BF16) for p in range(2)]
        WU = mkpool("wu", 4, [128, 128], BF16)
        vft = sb("vft", [128, T], F32)
        st4 = sb("st4", [128, NSTAGE - 1, T], F32)
        sm1 = sb("sm1", [128, NJ], F32)

        pieces = []
        for l in range(DEPTH):
            win3 = wb_d[l][:, WOFF["win"]:WOFF["win"] + 8 * N_IN].rearrange("p (k c) -> p k c", k=8)
            wout3 = wb_d[l][:, WOFF["wout"]:WOFF["wout"] + 8 * D].rearrange("p (k c) -> p k c", k=8)
            w13 = wb_d[l][:, WOFF["w1"]:WOFF["w1"] + 8 * D_FF].rearrange("p (k c) -> p k c", k=8)
            w23 = wb_d[l][:, WOFF["w2"]:WOFF["w2"] + 32 * D].rearrange("p (k c) -> p k c", k=32)
            for i in range(NT):
                pieces.append((win3[:, :, 0:256], (8, 256)))
                for j in range(NJ):
                    c0 = 256 + j * 1408
                    pieces.append((win3[:, :, c0:c0 + 384], (8, 384)))
                    pieces.append((win3[:, :, c0 + 384:c0 + 768], (8, 384)))
                    pieces.append((win3[:, :, c0 + 768:c0 + 1152], (8, 384)))
                    pieces.append((win3[:, :, c0 + 1152:c0 + 1408], (8, 256)))
                for h in range(2):
                    pieces.append((wout3[:, :, h * 512:(h + 1) * 512], (8, 512)))
                for q in range(4):
                    for h in range(2):
                        pieces.append((w13[:, :, q * 1024 + h * 512:q * 1024 + (h + 1) * 512], (8, 512)))
                    for h in range(2):
                        pieces.append((w23[:, q * 8:(q + 1) * 8, h * 512:(h + 1) * 512], (8, 512)))
        pstate = {"issued": 0, "next": 0, "slots": {}}

        def _issue(n):
            src, (k, c) = pieces[n]
            b = WB.get()
            view = b.t[:, 0:k * c].rearrange("p (k c) -> p k c", k=k)
            S.dma(view, src, reads=["wbd"], writes=[b.k], semkey=b.k)
            pstate["slots"][n] = (view, b.k)

        def next_piece():
            n = pstate["next"]
            pstate["next"] += 1
            while pstate["issued"] < min(len(pieces), n + 4):
                _issue(pstate["issued"])
                pstate["issued"] += 1
            return pstate["slots"].pop(n)

        def XK(j):
            return ("xT", j)

        def allgather(q):
            S.cc(lambda e: e.collective_compute(
                "AllGather", ALU.bypass,
                replica_groups=[list(range(g * NSTAGE, (g + 1) * NSTAGE)) for g in range(NGROUP)],
                ins=[snd_t[q].ap().opt()], outs=[rcv_t[q].ap().opt()]),
                reads=[("snd", c) for c in range(4 * q, 4 * q + 4)], writes=[("rcv", q)])

        def proj(wv, wk, cidx, rhsT, rkeys):
            ps = PS.get()
            for kc in range(8):
                mm(ps.t[:], wv[:, kc, cidx * 128:(cidx + 1) * 128], rhsT[:, kc, :], kc == 0, kc == 7,
                   [wk] + rkeys, [ps.k])
            return ps

        def rmsnorm_to(l, gname, dst, dkeyname):
            ps = PS.get()
            for j in range(NJ):
                sq = SQ.get()
                act(sq.t[:], xT[:, j, :], AF.Square, [XK(j)], [sq.k])
                mm(ps.t[:], ones_b, sq.t[:], j == 0, j == NJ - 1, [sq.k, "cstb"], [ps.k])
            rstd = TMP.get()
            act(rstd.t[:], ps.t[:], AF.Sqrt, [ps.k, "cst"], [rstd.k], bias=C("eps", 1), scale=1.0 / D)
            rcp(rstd.t[:], rstd.t[:], [rstd.k], [rstd.k])
            for j in range(NJ):
                if gname == "final":
                    g = vecs[:, DEPTH * NVEC + j:DEPTH * NVEC + j + 1]
                else:
                    g = V(l, gname, j)
                stt(dst[:, j, :], xT[:, j, :], g, rstd.t[:], ALU.mult, ALU.mult,
                    [XK(j), rstd.k, "vecs"], [(dkeyname, j)])

        def tshift(ps, cidx, mu_ap, out_ap, okeys):
            pb = PB.get()
            cp(pb.t[:, 0:1], cTS[:, cidx:cidx + 1], [("cTS", cidx)], [pb.k])
            act(pb.t[:, 1:T + 1], ps.t[:], AF.Copy, [ps.k], [pb.k])
            cp(cTS[:, cidx:cidx + 1], pb.t[:, T:T + 1], [pb.k], [("cTS", cidx)])
            d = TMP.get()
            tt(d.t[:], pb.t[:, 0:T], pb.t[:, 1:T + 1], ALU.subtract, [pb.k], [d.k], eng="pool")
            stt(out_ap, d.t[:], mu_ap, pb.t[:, 1:T + 1], ALU.mult, ALU.add, [d.k, pb.k, "vecs"], okeys, eng="pool")

        for l in range(DEPTH):
            for buf, key in ((cTS, None), (cA, None), (cB, None), (cH, None), (Hf, None), (Hb, None)):
                pass
            S.op("dve", lambda e: e.memset(cTS[:], 0.0), [], [("cTS", c) for c in range(26)])
            S.op("dve", lambda e: e.memset(cA[:], 0.0), [], [("cA", j) for j in range(NJ)])
            S.op("dve", lambda e: e.memset(cB[:], 0.0), [], [("cB", j) for j in range(NJ)])
            S.op("dve", lambda e: e.memset(cH[:], 0.0), [], [("cH", j) for j in range(NJ)])
            S.op("dve", lambda e: e.memset(Hf[:], 0.0), [], [("Hf", j) for j in range(NJ)])
            S.op("dve", lambda e: e.memset(Hb[:], 0.0), [], [("Hb", j) for j in range(NJ)])
            wl = wb_d[l]
            S.dma(wa_s[:], wl[:, WOFF["wa"]:WOFF["wa"] + 1024].rearrange("p (k c) -> p k c", k=8), ["wbd"],
                  ["wa_s"], semkey="wa_s")
            S.dma(wi_s[:], wl[:, WOFF["wi"]:WOFF["wi"] + 1024].rearrange("p (k c) -> p k c", k=8), ["wbd"],
                  ["wi_s"], semkey="wi_s")
            S.dma(w2a2_s[:], wl[:, WOFF["w2a2"]:WOFF["w2a2"] + D], ["wbd"], ["w2a2_s"], semkey="w2a2_s")
            S.dma(g2_s[:], wl[:, WOFF["g2"]:WOFF["g2"] + D], ["wbd"], ["g2_s"], semkey="g2_s")
            if True:
                S.dma(v1_s[:], wl[:, WOFF["v1"]:WOFF["v1"] + 256].rearrange("p (k c) -> p k c", k=8), ["wbd"],
                      ["v1_s"], semkey="v1_s")
                S.dma(v2_s[:], wl[:, WOFF["v2"]:WOFF["v2"] + D], ["wbd"], ["v2_s"], semkey="v2_s")
            sp8 = lambda j, l=l: der[:, l * 16 + j:l * 16 + j + 1]
            omka = lambda j, l=l: der[:, l * 16 + 8 + j:l * 16 + 8 + j + 1]
            HK = ["hT%d" % 0]
            hkeys = [("hT", j) for j in range(NJ)]

            for i in range(NT):
                t0 = i * T
                keep = FL("keep", i)
                for stt_, kn, n_ in ((cTS[:], "cTS", 26), (cA[:].rearrange("p a b -> p (a b)"), "cA", NJ),
                                     (cB[:].rearrange("p a b -> p (a b)"), "cB", NJ), (cH[:], "cH", NJ),
                                     (Hf[:].rearrange("p a b -> p (a b)"), "Hf", NJ),
                                     (Hb[:].rearrange("p a b -> p (a b)"), "Hb", NJ)):
                    ks_ = [(kn, c) for c in range(n_)]
                    ts(stt_, stt_, keep, None, ALU.mult, None, ks_ + ["vecs"], ks_)
                for j in range(NJ):
                    xi = TMP.get()
                    S.dma(xi.t[:], x_in[:, j, t0:t0 + T], writes=[xi.k], semkey=xi.k)
                    ts(xT[:, j, :], xi.t[:], FL("selx", i), None, ALU.mult, None, [xi.k, "vecs"], [XK(j)])
                    if i > 0:
                        S.dma(st4[:], rcv_c(j)[:, 0:NSTAGE - 1, :], reads=[("rcv", j // 4)], writes=["st4"], semkey="st4")
                        for r in range(NSTAGE - 1):
                            stt(xT[:, j, :], st4[:, r, :], FL("sel", r * NT + i), xT[:, j, :], ALU.mult, ALU.add,
                                ["st4", XK(j), "vecs"], [XK(j)], eng="pool")
                rmsnorm_to(l, "g1", hT, "hT")

                wv, wk = next_piece()
                ps = proj(wv, wk, 0, hT, hkeys)
                xs_t = TMP.get()
                tshift(ps, 0, V(l, "mu_lr", 0), xs_t.t[:], [xs_t.k])
                act(xwa[0:64, :], xs_t.t[0:64, :], AF.Tanh, [xs_t.k], ["xwa"])
                act(xwa[64:128, :], xs_t.t[64:128, :], AF.Copy, [xs_t.k], ["xwa"])
                ps = proj(wv, wk, 1, hT, hkeys)
                xg_t = TMP.get()
                tshift(ps, 1, V(l, "mu_lr", 1), xg_t.t[:], [xg_t.k])
                act(sgx[:], xg_t.t[:], AF.Sigmoid, [xg_t.k], ["sgx"])
                if True:
                    ps = PS.get()
                    for kc in range(8):
                        mm(ps.t[0:32, :], v1_s[:, kc, :], hT[:, kc, :], kc == 0, kc == 7, ["v1_s"] + hkeys, [ps.k])
                    act(hv1[:], ps.t[0:32, :], AF.Copy, [ps.k], ["hv1"])

                PL = "pool"

                def v3(ap):
                    return ap.rearrange("p (c t) -> p c t", c=NCH)

                def pv4(ps):
                    return ps.t[:].rearrange("p (c t) -> p c t", c=4)

                mSU4 = C("mSU", 128).unsqueeze(1).broadcast_to([128, 4, 128])
                mSL4 = C("mSL", 128).unsqueeze(1).broadcast_to([128, 4, 128])
                idn4 = C("ident", 128).unsqueeze(1).broadcast_to([128, 4, 128])
                mIU8 = C("mIU", 64).unsqueeze(1).broadcast_to([128, NCH, 64])

                def prep(j):
                    p = j % 2
                    js = slice(j * 128, (j + 1) * 128)
                    bd_a, rt, BhT, KhT, VT = BDA[p], rtP[p], BhTP[p], KhTP[p], VTP[p]
                    Xf, AakT, ArbT, ArkT = XfP[p], AakTP[p], ArbTP[p], ArkTP[p]
                    G, Bp, gam = GP[p], BpP[p], gamP[p]

                    def K(nm):
                        return (nm, p)
                    wvA, wkA = next_piece()
                    psB = proj(wvA, wkA, 0, hT, hkeys)
                    psC = proj(wvA, wkA, 1, hT, hkeys)
                    psX = proj(wvA, wkA, 2, hT, hkeys)
                    csb = TMP.get()
                    act(csb.t[:], psC.t[:], AF.Copy, [psC.k], [csb.k])
                    cx = PB.get()
                    cp(cx.t[:, 0:2], cA[:, j, :], [("cA", j)], [cx.k])
                    tt(cx.t[:, 2:T + 2], csb.t[:], psX.t[:], ALU.mult, [csb.k, psX.k], [cx.k])
                    cp(cA[:, j, :], cx.t[:, T:T + 2], [cx.k], [("cA", j)])
                    acc = TMP.get()
                    ts(acc.t[:], cx.t[:, 2:T + 2], V(l, "cA2", j), None, ALU.mult, None, [cx.k, "vecs"], [acc.k], eng=PL)
                    stt(acc.t[:], cx.t[:, 1:T + 1], V(l, "cA1", j), acc.t[:], ALU.mult, ALU.add,
                        [cx.k, acc.k], [acc.k], eng=PL)
                    stt(acc.t[:], cx.t[:, 0:T], V(l, "cA0", j), acc.t[:], ALU.mult, ALU.add,
                        [cx.k, acc.k], [acc.k], eng=PL)
                    tt(acc.t[:], acc.t[:], psB.t[:], ALU.mult, [acc.k, psB.k], [acc.k])
                    yield
                    wvA, wkA = next_piece()
                    psg = proj(wvA, wkA, 2, hT, hkeys)
                    gA = TMP.get()
                    act(gA.t[:], psg.t[:], AF.Sigmoid, [psg.k, "vecs"], [gA.k], bias=V(l, "mbA", j))
                    tt(Bp[:], acc.t[:], gA.t[:], ALU.mult, [acc.k, gA.k], [K("Bp")])
                    yield
                    psxb = proj(wvA, wkA, 0, hT, hkeys)
                    psgb = proj(wvA, wkA, 1, hT, hkeys)
                    xbb = PB.get()
                    cp(xbb.t[:, 0:3], cB[:, j, :], [("cB", j)], [xbb.k])
                    act(xbb.t[:, 3:T + 3], psxb.t[:], AF.Copy, [psxb.k], [xbb.k])
                    cp(cB[:, j, :], xbb.t[:, T:T + 3], [xbb.k], [("cB", j)])
                    u = TMP.get()
                    ts(u.t[:], xbb.t[:, 3:T + 3], V(l, "cB3", j), V(l, "cBb", j), ALU.mult, ALU.add,
                       [xbb.k, "vecs"], [u.k], eng=PL)
                    for tap in range(3):
                        stt(u.t[:], xbb.t[:, tap:tap + T], V(l, "cB%d" % tap, j), u.t[:], ALU.mult, ALU.add,
                            [xbb.k, u.k], [u.k], eng=PL)
                    ub = TB.get()
                    act(ub.t[:], u.t[:], AF.Copy, [u.k], [ub.k])
                    yield
                    psga = PS.get()
                    mm(psga.t[:], wa_s[:, j, :], ub.t[:], True, True, ["wa_s", ub.k], [psga.k])
                    psgi = PS.get()
                    mm(psgi.t[:], wi_s[:, j, :], ub.t[:], True, True, ["wi_s", ub.k], [psgi.k])
                    ga = TMP.get()
                    act(ga.t[:], psga.t[:], AF.Sigmoid, [psga.k, "vecs"], [ga.k], bias=V(l, "ba", j))
                    gi = TMP.get()
                    act(gi.t[:], psgi.t[:], AF.Sigmoid, [psgi.k, "vecs"], [gi.k], bias=V(l, "bi", j))
                    aa = TMP.get()
                    act(aa.t[:], ga.t[:], AF.Exp, [ga.k, "der"], [aa.k], scale=sp8(j))
                    mlt = ga
                    stt(mlt.t[:], aa.t[:], -1.0, aa.t[:], ALU.mult, ALU.mult, [aa.k], [mlt.k])
                    act(mlt.t[:], mlt.t[:], AF.Sqrt, [mlt.k, "cst"], [mlt.k], bias=C("ones", 1))
                    ts(sm1[:, j:j + 1], mlt.t[:, 0:1], -1.0, 1.0, ALU.mult, ALU.add, [mlt.k], [("sm1", j)])
                    stt(mlt.t[:, 0:1], sm1[:, j:j + 1], FL("first", i), mlt.t[:, 0:1], ALU.mult, ALU.add,
                        [("sm1", j), mlt.k, "vecs"], [mlt.k])
                    tt(u.t[:], u.t[:], gi.t[:], ALU.mult, [u.k, gi.k], [u.k], eng=PL)
                    tt(u.t[:], u.t[:], mlt.t[:], ALU.mult, [u.k, mlt.k], [u.k])
                    yield
                    hb = gi
                    S.op("dve", lambda e, o=hb.t, a=aa.t, uu=u.t, j=j: e.tensor_tensor_scan(
                        out=o[:], data0=a[:], data1=uu[:], initial=cH[:, j:j + 1], op0=ALU.mult, op1=ALU.add),
                        [aa.k, u.k, ("cH", j)], [hb.k])
                    cp(cH[:, j:j + 1], hb.t[:, T - 1:T], [hb.k], [("cH", j)])
                    gx = TMP.get()
                    act(gx.t[:], psgb.t[:], AF.Copy, [psgb.k], [gx.k])
                    g2t = TMP.get()
                    act(g2t.t[:], psgb.t[:], AF.Square, [psgb.k], [g2t.k])
                    ts(g2t.t[:], g2t.t[:], 0.044715, 1.0, ALU.mult, ALU.add, [g2t.k], [g2t.k], eng=PL)
                    tt(g2t.t[:], g2t.t[:], gx.t[:], ALU.mult, [g2t.k, gx.k], [g2t.k], eng=PL)
                    act(g2t.t[:], g2t.t[:], AF.Sigmoid, [g2t.k], [g2t.k], scale=GELU_C)
                    tt(gx.t[:], gx.t[:], g2t.t[:], ALU.mult, [gx.k, g2t.k], [gx.k], eng=PL)
                    tt(gx.t[:], gx.t[:], hb.t[:], ALU.mult, [gx.k, hb.k], [gx.k])
                    yield
                    wvB, wkB = next_piece()
                    psg = proj(wvB, wkB, 0, hT, hkeys)
                    gB = TMP.get()
                    act(gB.t[:], psg.t[:], AF.Sigmoid, [psg.k, "vecs"], [gB.k], bias=V(l, "mbB", j))
                    tt(gx.t[:], gx.t[:], gB.t[:], ALU.mult, [gx.k, gB.k], [gx.k], eng=PL)
                    tt(Bp[:], Bp[:], gx.t[:], ALU.add, [K("Bp"), gx.k], [K("Bp")], eng=PL)
                    yield
                    psgC = proj(wvB, wkB, 1, hT, hkeys)
                    gCt = TMP.get()
                    act(gCt.t[:], psgC.t[:], AF.Sigmoid, [psgC.k, "vecs"], [gCt.k], bias=V(l, "mbC", j))
                    psr = proj(wvB, wkB, 2, hT, hkeys)
                    tshift(psr, 2 + j, V(l, "mu_r", j), RW["r"][:], ["r"])
                    yield
                    wvB, wkB = next_piece()
                    psk = proj(wvB, wkB, 0, hT, hkeys)
                    tshift(psk, 10 + j, V(l, "mu_k", j), RW["k"][:], ["k"])
                    yield
                    psv = proj(wvB, wkB, 1, hT, hkeys)
                    tshift(psv, 18 + j, V(l, "mu_v", j), RW["v"][:], ["v"])
                    yield
                    if i > 0:
                        S.dma(st4[:], rcv_c(8 + j)[:, 0:NSTAGE - 1, :], reads=[("rcv", 2 + j // 4)], writes=["st4"],
                              semkey="st4")
                        ts(vft[:], st4[:, 0, :], FL("sel", i), None, ALU.mult, None, ["st4", "vecs"], ["vft"], eng=PL)
                        for r in range(1, NSTAGE - 1):
                            stt(vft[:], st4[:, r, :], FL("sel", r * NT + i), vft[:], ALU.mult, ALU.add,
                                ["st4", "vft", "vecs"], ["vft"], eng=PL)
                    else:
                        S.op("dve", lambda e: e.memset(vft[:], 0.0), [], ["vft"])
                    ps = PS.get()
                    mm(ps.t[:], v2_s[0:32, js], hv1[:], True, True, ["v2_s", "hv1"], [ps.k])
                    sgv = TMP.get()
                    act(sgv.t[:], ps.t[:], AF.Sigmoid, [ps.k, "vecs"], [sgv.k], bias=V(l, "v0", j))
                    ts(sgv.t[:], sgv.t[:], FL("vres"), None, ALU.mult, None, [sgv.k, "vecs"], [sgv.k], eng=PL)
                    tt(vft[:], vft[:], RW["v"][:], ALU.subtract, ["vft", "v"], ["vft"], eng=PL)
                    vfo = TMP.get()
                    stt(vfo.t[:], vft[:], FL("vres"), RW["v"][:], ALU.mult, ALU.add, ["vft", "v", "vecs"], [vfo.k], eng=PL)
                    S.dma(snd_c(8 + j), vfo.t[:], reads=[vfo.k], writes=[("snd", 8 + j)], semkey=vfo.k)
                    if j % 4 == 3 and i < NT - 1:
                        allgather(2 + j // 4)
                    tt(vft[:], vft[:], sgv.t[:], ALU.mult, ["vft", sgv.k], ["vft"], eng=PL)
                    tt(RW["v"][:], RW["v"][:], vft[:], ALU.add, ["v", "vft"], ["v"], eng=PL)
                    yield
                    ps = PS.get()
                    mm(ps.t[:], w2a2_s[0:64, js], xwa[0:64, :], True, True, ["w2a2_s", "xwa"], [ps.k])
                    act(RW["sw"][:], ps.t[:], AF.Sigmoid, [ps.k, "vecs"], ["sw"], bias=V(l, "w0", j))
                    ps = PS.get()
                    mm(ps.t[:], w2a2_s[64:128, js], xwa[64:128, :], True, True, ["w2a2_s", "xwa"], [ps.k])
                    act(RW["ac"][:], ps.t[:], AF.Sigmoid, [ps.k, "vecs"], ["ac"], bias=V(l, "a0", j))
                    ps = PS.get()
                    mm(ps.t[:], g2_s[:, js], sgx[:], True, True, ["g2_s", "sgx"], [ps.k])
                    tt(G[:], ps.t[:], gCt.t[:], ALU.mult, [ps.k, gCt.k], [K("G")])
                    yield
                    ts(RW["kk"][:], RW["k"][:], V(l, "kkv", j), None, ALU.mult, None, ["k", "vecs"], ["kk"], eng=PL)
                    sq = SQ.get()
                    act(sq.t[:], RW["kk"][:], AF.Square, ["kk"], [sq.k])
                    ps = PS.get()
                    mm(ps.t[:], onesbd_b, sq.t[:], True, True, ["cstb", sq.k], [ps.k])
                    inv = TMP.get()
                    act(inv.t[:], ps.t[:], AF.Sqrt, [ps.k], [inv.k])
                    ts(inv.t[:], inv.t[:], 1e-12, None, ALU.max, None, [inv.k], [inv.k])
                    rcp(inv.t[:], inv.t[:], [inv.k], [inv.k])
                    tt(RW["kk"][:], RW["kk"][:], inv.t[:], ALU.mult, ["kk", inv.k], ["kk"], eng=PL)
                    km = TMP.get()
                    ts(km.t[:], RW["ac"][:], V(l, "ka", j), omka(j), ALU.mult, ALU.add, ["ac", "vecs", "der"], [km.k], eng=PL)
                    tt(RW["k"][:], RW["k"][:], km.t[:], ALU.mult, ["k", km.k], ["k"], eng=PL)
                    tt(RW["b"][:], RW["kk"][:], RW["ac"][:], ALU.mult, ["kk", "ac"], ["b"], eng=PL)
                    yield
                    rkb = TB.get()
                    stt(rkb.t[:], RW["r"][:], V(l, "rk", j), RW["k"][:], ALU.mult, ALU.mult, ["r", "k", "vecs"], [rkb.k], eng=PL)
                    ps = PS.get()
                    mm(ps.t[:], onesbd_b, rkb.t[:], True, True, ["cstb", rkb.k], [ps.k])
                    bn = TMP.get()
                    tt(bn.t[:], ps.t[:], RW["v"][:], ALU.mult, [ps.k, "v"], [bn.k])
                    tt(bn.t[:], bn.t[:], G[:], ALU.mult, [bn.k, K("G")], [bn.k], eng=PL)
                    tt(Bp[:], Bp[:], bn.t[:], ALU.add, [K("Bp"), bn.k], [K("Bp")], eng=PL)
                    yield
                    S.op("dve", lambda e: e.tensor_tensor_scan(
                        out=RW["cum"][:], data0=C("cmask", 512), data1=RW["sw"][:], initial=0.0,
                        op0=ALU.mult, op1=ALU.add), ["cst", "sw"], ["cum"])
                    act(RW["epos"][:], RW["cum"][:], AF.Exp, ["cum"], ["epos"], scale=-DECAY_C)
                    act(RW["eneg"][:], RW["cum"][:], AF.Exp, ["cum"], ["eneg"], scale=DECAY_C)
                    cp(gam[:].unsqueeze(2), v3(RW["epos"][:])[:, :, CH - 1:CH], ["epos"], [K("gam")])
                    epv = TMP.get()
                    tt(epv.t[:], RW["cum"][:], RW["sw"][:], ALU.subtract, ["cum", "sw"], [epv.k], eng=PL)
                    act(epv.t[:], epv.t[:], AF.Exp, [epv.k], [epv.k], scale=-DECAY_C)
                    eht = TMP.get()
                    cum3 = v3(RW["cum"][:])
                    tt(v3(eht.t[:]), cum3[:, :, CH - 1:CH].broadcast_to([128, NCH, CH]),
                       cum3, ALU.subtract, ["cum"], [eht.k], eng=PL)
                    act(eht.t[:], eht.t[:], AF.Exp, [eht.k], [eht.k], scale=-DECAY_C)
                    yield

                    def bdv(t_, hh):
                        return t_[:].rearrange("p c (h t) -> p c h t", h=2)[:, :, hh, :]
                    for hh in range(2):
                        hm = cst[:, COFF["hmask"] + hh:COFF["hmask"] + hh + 1]
                        nhm = cst[:, COFF["nhmask"] + hh:COFF["nhmask"] + hh + 1]
                        stt(bdv(bd_a, hh), v3(RW["kk"][:]), nhm, v3(epv.t[:]), ALU.mult, ALU.mult,
                            ["kk", epv.k, "cst"], [K("bd_a")], eng=PL)
                        stt(bdv(BD["b"], hh), v3(RW["b"][:]), hm, v3(RW["eneg"][:]), ALU.mult, ALU.mult,
                            ["b", "eneg", "cst"], ["bd_b"], eng=PL)
                        stt(bdv(BD["k"], hh), v3(RW["k"][:]), hm, v3(RW["eneg"][:]), ALU.mult, ALU.mult,
                            ["k", "eneg", "cst"], ["bd_k"])
                        ts(bdv(BD["v"], hh), v3(RW["v"][:]), hm, None, ALU.mult, None, ["v", "cst"], ["bd_v"], eng=PL)
                    tt(rt[:], RW["r"][:], RW["epos"][:], ALU.mult, ["r", "epos"], [K("rt")])
                    yield

                    def transp(srcnm, dst, dkey):
                        ps = PS.get()
                        pv = ps.t[:].bitcast(BF16).rearrange("p (c t) -> p c t", c=NCH)
                        for c in range(NCH):
                            tr(pv[:, c, :], BD[srcnm][:, c, :], ident_b, ["bd_" + srcnm, "cstb"], [ps.k])
                        act(dst[:], pv, AF.Copy, [ps.k], [dkey])
                    for hh in range(2):
                        hm = cst[:, COFF["hmask"] + hh:COFF["hmask"] + hh + 1]
                        stt(bdv(BD["h"], hh), v3(RW["b"][:]), hm, v3(eht.t[:]), ALU.mult, ALU.mult,
                            ["b", eht.k, "cst"], ["bd_h"], eng=PL)
                    transp("h", BhT, K("BhT"))
                    yield
                    for hh in range(2):
                        hm = cst[:, COFF["hmask"] + hh:COFF["hmask"] + hh + 1]
                        stt(bdv(BD["h"], hh), v3(RW["k"][:]), hm, v3(eht.t[:]), ALU.mult, ALU.mult,
                            ["k", eht.k, "cst"], ["bd_h"])
                    transp("h", KhT, K("KhT"))
                    yield
                    transp("v", VT, K("VT"))
                    yield
                    for g in range(2):
                        ps = PS.get()
                        for cc in range(4):
                            c = g * 4 + cc
                            mm(pv4(ps)[:, cc, :], BD["k"][:, c, :], bd_a[:, c, :], True, True, ["bd_k", K("bd_a")], [ps.k])
                        tt(AakT[:, g * 4:(g + 1) * 4, :], pv4(ps), mSU4, ALU.mult, [ps.k, "cst"], [K("AakT")])
                    yield
                    for g in range(2):
                        psN = PS.get()
                        for cc in range(4):
                            c = g * 4 + cc
                            mm(pv4(psN)[:, cc, :], BD["b"][:, c, :], bd_a[:, c, :], True, True, ["bd_b", K("bd_a")], [psN.k])
                        tt(Nq[g][0][:], pv4(psN), mSU4, ALU.mult, [psN.k, "cst"], [("Nq", g, 0)])
                        psM = PS.get()
                        for cc in range(4):
                            c = g * 4 + cc
                            mm(pv4(psM)[:, cc, :], bd_a[:, c, :], BD["b"][:, c, :], True, True, ["bd_b", K("bd_a")], [psM.k])
                        tt(Mq[g][0][:], pv4(psM), mSL4, ALU.mult, [psM.k, "cst"], [("Mq", g, 0)])
                        tt(Xq[g][0][:], Nq[g][0][:], idn4, ALU.add, [("Nq", g, 0), "cst"], [("Xq", g, 0)], eng=PL)
                    yield
                    for lev in range(1, 6):
                        s_, d_ = (lev - 1) % 2, lev % 2
                        for g in range(2):
                            if lev < 5:
                                psN = PS.get()
                                for cc in range(4):
                                    mm(pv4(psN)[:, cc, :], Mq[g][s_][:, cc, :], Nq[g][s_][:, cc, :], True, True,
                                       [("Mq", g, s_), ("Nq", g, s_)], [psN.k])
                                act(Nq[g][d_][:], pv4(psN), AF.Copy, [psN.k], [("Nq", g, d_)])
                            psM = PS.get()
                            for cc in range(4):
                                mm(pv4(psM)[:, cc, :], Nq[g][s_][:, cc, :], Mq[g][s_][:, cc, :], True, True,
                                   [("Mq", g, s_), ("Nq", g, s_)], [psM.k])
                            act(Mq[g][d_][:], pv4(psM), AF.Copy, [psM.k], [("Mq", g, d_)])
                        yield
                        for g in range(2):
                            psX = PS.get()
                            for cc in range(4):
                                mm(pv4(psX)[:, cc, :], Mq[g][d_][:, cc, :], Xq[g][s_][:, cc, :], True, True,
                                   [("Mq", g, d_), ("Xq", g, s_)], [psX.k])
                            if lev < 5:
                                tt(Xq[g][d_][:], pv4(psX), Xq[g][s_][:], ALU.add, [psX.k, ("Xq", g, s_)], [("Xq", g, d_)])
                            else:
                                tt(Xf[:, g * 4:(g + 1) * 4, :], pv4(psX), Xq[g][s_][:], ALU.add,
                                   [psX.k, ("Xq", g, s_)], [K("Xf")])
                        yield
                    ps = PS.get()
                    pv8 = ps.t[:].rearrange("p (c t) -> p c t", c=NCH)
                    for c in range(NCH):
                        mm(pv8[:, c, :], BD["b"][:, c, :], rt[:, c * CH:(c + 1) * CH], True, True, ["bd_b", K("rt")], [ps.k])
                    tt(ArbT[:], pv8, mIU8, ALU.mult, [ps.k, "cst"], [K("ArbT")])
                    ps = PS.get()
                    pv8 = ps.t[:].rearrange("p (c t) -> p c t", c=NCH)
                    for c in range(NCH):
                        mm(pv8[:, c, :], BD["k"][:, c, :], rt[:, c * CH:(c + 1) * CH], True, True, ["bd_k", K("rt")], [ps.k])
                    tt(ArkT[:], pv8, mIU8, ALU.mult, [ps.k, "cst"], [K("ArkT")])
                    yield

                def advance(gen, n):
                    if gen is None:
                        return None
                    for _ in range(n):
                        try:
                            next(gen)
                        except StopIteration:
                            return None
                    return gen

                def chain(j, gen):
                    p = j % 2
                    bd_a, rt, BhT, KhT, VT = BDA[p], rtP[p], BhTP[p], KhTP[p], VTP[p]
                    Xf, AakT, ArbT, ArkT = XfP[p], AakTP[p], ArbTP[p], ArkTP[p]
                    G, Bp, gam = GP[p], BpP[p], gamP[p]

                    def K(nm):
                        return (nm, p)
                    psY = PSY
                    HfK, HbK = ("Hf", j), ("Hb", j)
                    for c in range(NCH):
                        cs = slice(c * CH, (c + 1) * CH)
                        psW = PQ.get()
                        mm(psW.t[:, 0:128], bd_a[:, c, :], Hb[:, j, :], True, False, [K("bd_a"), HbK], [psW.k])
                        mm(psW.t[:, 0:128], AakT[:, c, :], VT[:, c, :], False, True, [K("AakT"), K("VT")], [psW.k])
                        wb_ = WU.get()
                        act(wb_.t[:], psW.t[:, 0:128], AF.Copy, [psW.k], [wb_.k])
                        psU = PQ.get()
                        mm(psU.t[:, 0:128], Xf[:, c, :], wb_.t[:], True, True, [K("Xf"), wb_.k], [psU.k])
                        ub_ = WU.get()
                        act(ub_.t[:], psU.t[:, 0:128], AF.Copy, [psU.k], [ub_.k])
                        mm(psY.t[:, cs], Hb[:, j, :], rt[:, cs], True, False, [HbK, K("rt")], [psY.k])
                        mm(psY.t[:, cs], ub_.t[:], ArbT[:, c, :], False, False, [ub_.k, K("ArbT")], [psY.k])
                        mm(psY.t[:, cs], VT[:, c, :], ArkT[:, c, :], False, True, [K("VT"), K("ArkT")], [psY.k])
                        psH = PQ.get()
                        mm(psH.t[:, 0:128], BhT[:, c, :], ub_.t[:], True, False, [K("BhT"), ub_.k], [psH.k])
                        mm(psH.t[:, 0:128], KhT[:, c, :], VT[:, c, :], False, True, [K("KhT"), K("VT")], [psH.k])
                        stt(Hf[:, j, :], Hf[:, j, :], gam[:, c:c + 1], psH.t[:, 0:128],
                            ALU.mult, ALU.add, [HfK, K("gam"), psH.k], [HfK])
                        act(Hb[:, j, :], Hf[:, j, :], AF.Copy, [HfK], [HbK])
                        gen = advance(gen, 4)
                    while gen is not None:
                        gen = advance(gen, 8)
                    ysb = TMP.get()
                    act(ysb.t[:], psY.t[:], AF.Copy, [psY.k], [ysb.k])
                    ysq = TMP.get()
                    act(ysq.t[:], psY.t[:], AF.Square, [psY.k], [ysq.k])
                    psm = PS.get()
                    mm(psm.t[:], C("ones_bd", 128), ysb.t[:], True, True, ["cst", ysb.k], [psm.k])
                    pss = PS.get()
                    mm(pss.t[:], C("ones_bd", 128), ysq.t[:], True, True, ["cst", ysq.k], [pss.k])
                    mu = TMP.get()
                    ts(mu.t[:], psm.t[:], 1.0 / 64, None, ALU.mult, None, [psm.k], [mu.k])
                    var = ysq
                    stt(var.t[:], mu.t[:], -1.0, mu.t[:], ALU.mult, ALU.mult, [mu.k], [var.k], eng=PL)
                    stt(var.t[:], pss.t[:], 1.0 / 64, var.t[:], ALU.mult, ALU.add, [pss.k, var.k], [var.k])
                    act(var.t[:], var.t[:], AF.Sqrt, [var.k, "cst"], [var.k], bias=C("lnxeps", 1))
                    rcp(var.t[:], var.t[:], [var.k], [var.k])
                    tt(ysb.t[:], ysb.t[:], mu.t[:], ALU.subtract, [ysb.k, mu.k], [ysb.k], eng=PL)
                    tt(ysb.t[:], ysb.t[:], var.t[:], ALU.mult, [ysb.k, var.k], [ysb.k])
                    ts(ysb.t[:], ysb.t[:], V(l, "lg", j), V(l, "lb", j), ALU.mult, ALU.add, [ysb.k, "vecs"], [ysb.k], eng=PL)
                    tt(ysb.t[:], ysb.t[:], G[:], ALU.mult, [ysb.k, K("G")], [ysb.k])
                    tt(mb[:, j, :], Bp[:], ysb.t[:], ALU.add, [K("Bp"), ysb.k], [("mb", j)], eng=PL)

                g0 = prep(0)
                while advance(g0, 8) is not None:
                    pass
                for j in range(NJ):
                    chain(j, prep(j + 1) if j + 1 < NJ else None)

                mkeys = [("mb", j) for j in range(NJ)]
                for h in range(2):
                    wv, wk = next_piece()
                    for jo in range(4 * h, 4 * h + 4):
                        ps = proj(wv, wk, jo - 4 * h, mb, mkeys)
                        tt(xT[:, jo, :], xT[:, jo, :], ps.t[:], ALU.add, [XK(jo), ps.k], [XK(jo)])
                dbg("x_attn", xT[:, 0, :], XK(0), l == 0 and i == 0)
                dbg("x_attn7", xT[:, 7, :], XK(7), l == 0 and i == 0)
                rmsnorm_to(l, "g2n", hT, "hT")
                for q in range(4):
                    for h in range(2):
                        wv1, wk1 = next_piece()
                        for f in range(4 * h, 4 * h + 4):
                            ps = proj(wv1, wk1, f - 4 * h, hT, hkeys)
                            rl = TMP.get()
                            act(rl.t[:], ps.t[:], AF.Relu, [ps.k], [rl.k])
                            tt(hid[:, f, :], rl.t[:], rl.t[:], ALU.mult, [rl.k], [("mb", f)], eng="pool")
                    for jo in range(NJ):
                        if jo % 4 == 0:
                            wv2, wk2 = next_piece()
                        ps = PS.get()
                        for f in range(HG):
                            mm(ps.t[:], wv2[:, f, (jo % 4) * 128:(jo % 4 + 1) * 128], hid[:, f, :], f == 0, f == HG - 1,
                               [wk2, ("mb", f)], [ps.k])
                        tt(xT[:, jo, :], xT[:, jo, :], ps.t[:], ALU.add, [XK(jo), ps.k], [XK(jo)])
                dbg("x_mlp", xT[:, 0, :], XK(0), l == 0 and i == 0)
                for q in range(2):
                    S.dma(snd_t[q].ap().rearrange("p (c t) -> p c t", c=4), xT[:, 4 * q:4 * q + 4, :],
                          reads=[XK(j) for j in range(4 * q, 4 * q + 4)],
                          writes=[("snd", j) for j in range(4 * q, 4 * q + 4)], semkey=("xT_st", q))
                ps = PS.get()
                for j in range(NJ):
                    sq = SQ.get()
                    act(sq.t[:], xT[:, j, :], AF.Square, [XK(j)], [sq.k])
                    mm(ps.t[:], ones_b, sq.t[:], j == 0, j == NJ - 1, [sq.k, "cstb"], [ps.k])
                rstd = TMP.get()
                act(rstd.t[:], ps.t[:], AF.Sqrt, [ps.k, "cst"], [rstd.k], bias=C("eps", 1), scale=1.0 / D)
                rcp(rstd.t[:], rstd.t[:], [rstd.k], [rstd.k])
                for j in range(NJ):
                    g = vecs[:, NVEC + j:NVEC + j + 1]
                    o = TMP.get()
                    stt(o.t[:], xT[:, j, :], g, rstd.t[:], ALU.mult, ALU.mult, [XK(j), rstd.k, "vecs"], [o.k])
                    S.dma(out_d[:, j, t0:t0 + T], o.t[:], reads=[o.k], writes=[("od", i, j)], semkey=o.k)
                if i < NT - 1:
                    allgather(0)
                    allgather(1)
        S.final_wait([("od", i, j) for i in range(NT) for j in range(NJ)] + [("dbg", n) for n in range(len(DBG_NAMES))])
        with nc.Block() as block:
            S.emit(block)
        print("program: ops=%d sems=%d sbuf_left=%d" % (S.nops, len(S.handles), nc.sbuf_bytes_remaining))
    return nc


_PROG_CACHE = {}


def run_model(inputs, S_total, DEPTH, debug=False):
    assert DEPTH == NSTAGE
    x = np.asarray(inputs["x"], np.float32)
    B = x.shape[0]
    assert B == NGROUP
    NTS = S_total // T
    NTICK = NTS + NSTAGE - 1
    FOFF, NFLAG = flag_layout(NTICK)
    perm = _cperm()
    cst = _consts()
    nc = build_program(NTICK, debug=debug)
    lw = [_pack_layer_w(inputs, l, perm)[None] for l in range(DEPTH)]
    lv = [_pack_layer_vecs(inputs, l, DEPTH) for l in range(DEPTH)]
    fg = _fm(inputs["final_g"])
    in_maps = []
    for c in range(NGROUP * NSTAGE):
        b, st = c // NSTAGE, c % NSTAGE
        xfm = x[b].reshape(S_total, NJ, 128).transpose(2, 1, 0)
        reps = -(-NTICK * T // S_total)
        xb = np.ascontiguousarray(np.concatenate([xfm] * reps, axis=2)[:, :, 0:NTICK * T])
        fl = np.zeros((128, NFLAG), np.float32)
        for k in range(NTICK):
            own = (st == 0) or (k < st)
            fl[:, FOFF["selx"] + k] = 1.0 if own else 0.0
            if not own:
                fl[:, FOFF["sel"] + (st - 1) * NTICK + k] = 1.0
        if st > 0:
            fl[:, FOFF["vres"]] = 1.0
        fl[:, FOFF["keep"]:FOFF["keep"] + NTICK] = 1.0
        fl[:, FOFF["keep"] + st] = 0.0
        fl[:, FOFF["first"] + st] = 1.0
        vec_all = np.concatenate([lv[st], fg, fl], axis=1)
        in_maps.append({"x": xb, "vecs": np.ascontiguousarray(vec_all), "wts": lw[st], "consts": cst})
    res = run_bass_kernel_spmd(nc, in_maps, core_ids=list(range(NGROUP * NSTAGE)))
    _PROG_CACHE["res"] = res if debug else None
    outs = []
    for b in range(B):
        o = np.asarray(res.results[b * NSTAGE + NSTAGE - 1]["out"])
        o = o[:, :, (NSTAGE - 1) * T:(NSTAGE - 1) * T + S_total]
        outs.append(o.transpose(2, 1, 0).reshape(S_total, D))
    if debug:
        return np.stack(outs, axis=0).astype(np.float32), {n: np.asarray(res.results[0]["dbg"])[i]
                                                           for i, n in enumerate(DBG_NAMES)}
    return np.stack(outs, axis=0).astype(np.float32)


def kernel(**inputs):
    return run_model(inputs, 16384, 4)
```

```python
import numpy as np
from contextlib import ExitStack
import concourse.bass as bass
import concourse.mybir as mybir
from concourse.bass_utils import run_bass_kernel_spmd

F32 = mybir.dt.float32
BF16 = mybir.dt.bfloat16
AF = mybir.ActivationFunctionType
ALU = mybir.AluOpType

D = 1024
NJ = 8
T = 512
CH = 64
NCH = T // CH
N_IN = 11520
D_FF = 4096
EPS = 1e-6
LNX_EPS = 64 * 1e-5
DECAY_C = 0.6065306597126334
GELU_C = 1.5957691216057308

_VNAMES = [("g1", 8), ("g2n", 8), ("mbA", 8), ("mbB", 8), ("mbC", 8), ("cA0", 8), ("cA1", 8), ("cA2", 8),
           ("cB0", 8), ("cB1", 8), ("cB2", 8), ("cB3", 8), ("cBb", 8), ("ba", 8), ("bi", 8), ("ap", 8),
           ("mu_lr", 2), ("mu_r", 8), ("mu_k", 8), ("mu_v", 8), ("w0", 8), ("a0", 8), ("kkv", 8), ("ka", 8),
           ("rk", 8), ("lg", 8), ("lb", 8), ("v0", 8)]
VOFF = {}
_o = 0
for _n, _w in _VNAMES:
    VOFF[_n] = _o
    _o += _w
NVEC = _o

WOFF = {}
_o = 0
for _n, _w in [("win", 8 * N_IN), ("wout", 8 * D), ("w1", 8 * D_FF), ("w2", 32 * D), ("w2a2", D), ("g2", D),
               ("v1", 8 * 32), ("v2", D), ("wa", 8 * 128), ("wi", 8 * 128)]:
    WOFF[_n] = _o
    _o += _w
WTOT = _o
CVT = 4096

COFF = {}
_o = 0
for _n, _w in [("ident", 128), ("ones_bd", 128), ("mSU", 128), ("mSL", 128), ("mIU", 64), ("cmask", 512),
               ("hmask", 2), ("nhmask", 2), ("ones", 128), ("eps", 1), ("lnxeps", 1)]:
    COFF[_n] = _o
    _o += _w
NCONST = _o


def _cperm():
    c0 = 8192
    perm = list(range(c0 + 3072, c0 + 3328))
    for j in range(NJ):
        s = slice(j * 128, (j + 1) * 128)
        for base in (0, 1024, 2048, 3072, 4096, 5120, 6144, 7168, c0, c0 + 1024, c0 + 2048):
            perm += list(range(base + s.start, base + s.stop))
    return np.array(perm, dtype=np.int64)


def _fm(v):
    return np.ascontiguousarray(np.asarray(v, np.float32).reshape(-1, 128).T)


def _pack_layer_vecs(inp, l, L):
    out = np.zeros((128, NVEC), np.float32)

    def put(name, arr):
        a = _fm(arr)
        out[:, VOFF[name]:VOFF[name] + a.shape[1]] = a

    put("g1", inp["norm1_g"][l]); put("g2n", inp["norm2_g"][l])
    mb = inp["merge_b"][l]
    put("mbA", mb[0:1024]); put("mbB", mb[1024:2048]); put("mbC", mb[2048:3072])
    for t in range(3):
        put("cA%d" % t, inp["conv_a_w"][l, t])
    for t in range(4):
        put("cB%d" % t, inp["lru_conv_w"][l, t])
    put("cBb", inp["lru_conv_b"][l]); put("ba", inp["lru_ba"][l]); put("bi", inp["lru_bi"][l])
    put("ap", inp["lru_a_param"][l])
    mu = inp["rwkv_mu"][l]
    put("mu_lr", mu[3072:3328]); put("mu_r", mu[0:1024]); put("mu_k", mu[1024:2048]); put("mu_v", mu[2048:3072])
    put("w0", inp["rwkv_w0"][l]); put("a0", inp["rwkv_a0"][l]); put("kkv", inp["rwkv_kk"][l])
    put("ka", inp["rwkv_ka"][l]); put("rk", inp["rwkv_rk"][l].reshape(-1))
    put("lg", inp["rwkv_lnx_g"][l]); put("lb", inp["rwkv_lnx_b"][l])
    if l > 0:
        put("v0", inp["rwkv_v0"][l - 1])
    return out


def _pack_layer_w(inp, l, perm):
    out = np.zeros((128, WTOT), np.float32)

    def put(name, a):
        a = np.asarray(a, np.float32).reshape(a.shape[0], -1)
        out[:a.shape[0], WOFF[name]:WOFF[name] + a.shape[1]] = a

    def kc(w):
        K, N = w.shape
        return np.ascontiguousarray(np.asarray(w, np.float32).reshape(K // 128, 128, N).transpose(1, 0, 2))

    put("win", kc(inp["w_in"][l][:, perm]))
    put("wout", kc(inp["w_out"][l]))
    put("w1", kc(inp["mlp_w1"][l]))
    put("w2", kc(inp["mlp_w2"][l]))
    put("w2a2", np.concatenate([inp["rwkv_w2"][l], inp["rwkv_a2"][l]], axis=0))
    put("g2", inp["rwkv_g2"][l])
    if l > 0:
        put("v1", kc(inp["rwkv_v1"][l - 1]))
        put("v2", inp["rwkv_v2"][l - 1])
    for nm, key in (("wa", "lru_wa"), ("wi", "lru_wi")):
        bd = np.zeros((128, 8, 128), np.float32)
        w = np.asarray(inp[key][l], np.float32)
        for j in range(8):
            bd[0:64, j, 0:64] = w[2 * j]
            bd[64:128, j, 64:128] = w[2 * j + 1]
        put(nm, bd)
    return out


def _consts():
    c = np.zeros((128, NCONST), np.float32)
    p = np.arange(128)
    hp, sp = p // 64, p % 64
    c[:, COFF["ident"]:COFF["ident"] + 128] = np.eye(128)
    same = (hp[:, None] == hp[None, :])
    c[:, COFF["ones_bd"]:COFF["ones_bd"] + 128] = same
    c[:, COFF["mSU"]:COFF["mSU"] + 128] = same & (sp[:, None] < sp[None, :])
    c[:, COFF["mSL"]:COFF["mSL"] + 128] = same & (sp[:, None] > sp[None, :])
    c[:, COFF["mIU"]:COFF["mIU"] + 64] = (sp[:, None] <= np.arange(64)[None, :])
    c[:, COFF["cmask"]:COFF["cmask"] + 512] = (np.arange(512) % 64 != 0)[None, :]
    for hh in range(2):
        c[:, COFF["hmask"] + hh] = (hp == hh)
        c[:, COFF["nhmask"] + hh] = -(hp == hh).astype(np.float32)
    c[:, COFF["ones"]:COFF["ones"] + 128] = 1.0
    c[:, COFF["eps"]] = EPS
    c[:, COFF["lnxeps"]] = LNX_EPS
    return c


class _Ctr:
    def __init__(self, S, name):
        self.S, self.name, self.gen = S, name, 0
        self.sid = S.new_sem(name)
        self.v = 0

    def bump(self, inc):
        if self.v + inc > 30000:
            self.gen += 1
            self.sid = self.S.new_sem("%s_%d" % (self.name, self.gen))
            self.v = 0
        self.v += inc
        return self.sid, self.v


class Sched:
    ENG = ("pe", "act", "dve", "pool", "sp")

    def __init__(self, nc, stack):
        self.nc, self.stack = nc, stack
        self.handles = []
        self.streams = {e: [] for e in self.ENG}
        self.ctr = {e: _Ctr(self, "e_" + e) for e in self.ENG if e != "sp"}
        self.waited = {e: {} for e in self.ENG}
        self.lw, self.rd, self.dctr = {}, {}, {}
        self.nops = 0

    def new_sem(self, name):
        h = self.stack.enter_context(self.nc.semaphore(name))
        self.handles.append(h)
        return len(self.handles) - 1

    def _deps(self, eng, reads, writes):
        need = {}

        def add(ev):
            sid, val, src = ev
            if self.waited[eng].get(sid, 0) >= val:
                return
            if need.get(sid, 0) < val:
                need[sid] = val

        for k in reads:
            ev = self.lw.get(k)
            if ev is not None and not (eng == "pe" and ev[2] == "pe"):
                add(ev)
        for k in writes:
            ev = self.lw.get(k)
            if ev is not None and ev[2] != eng:
                add(ev)
            for sid, (val, src) in self.rd.get(k, {}).items():
                if src != eng:
                    add((sid, val, src))
        for sid, val in need.items():
            self.waited[eng][sid] = val
            self.streams[eng].append(("w", sid, val))

    def _record(self, me, reads, writes):
        for k in writes:
            self.lw[k] = me
            self.rd[k] = {}
        for k in reads:
            d = self.rd.setdefault(k, {})
            if d.get(me[0], (0, None))[0] < me[1]:
                d[me[0]] = (me[1], me[2])

    def op(self, eng, fn, reads=(), writes=()):
        self._deps(eng, reads, writes)
        sid, val = self.ctr[eng].bump(1)
        self.streams[eng].append(("o", fn, sid))
        self._record((sid, val, eng), reads, writes)
        self.nops += 1

    def dma(self, out_ap, in_ap, reads=(), writes=(), semkey=None, q="sp"):
        self._deps(q, reads, writes)
        c = self.dctr.get(semkey)
        if c is None:
            c = self.dctr[semkey] = _Ctr(self, "d_" + str(semkey))
        sid, val = c.bump(16)
        self.streams[q].append(("d", out_ap, in_ap, sid))
        self._record((sid, val, "dma"), reads, writes)
        self.nops += 1

    def cc(self, fn, reads=(), writes=()):
        self._deps("pool", reads, writes)
        c = self.dctr.get("cc")
        if c is None:
            c = self.dctr["cc"] = _Ctr(self, "d_cc")
        sid, val = c.bump(1)
        self.streams["pool"].append(("o", fn, sid))
        self._record((sid, val, "dma"), reads, writes)
        self.nops += 1

    def barrier(self):
        evs = []
        for e, c in self.ctr.items():
            if c.v > 0:
                evs.append((c.sid, c.v))
        for c in self.dctr.values():
            if c.v > 0:
                evs.append((c.sid, c.v))
        for eng in self.ENG:
            for sid, val in evs:
                if self.waited[eng].get(sid, 0) < val:
                    self.waited[eng][sid] = val
                    self.streams[eng].append(("w", sid, val))

    def final_wait(self, keys):
        self._deps("sp", list(keys), [])

    def emit(self, block):
        H = self.handles

        def mk(name):
            items = self.streams[name]

            def body(e):
                for it in items:
                    if it[0] == "w":
                        e.wait_ge(H[it[1]], it[2])
                    elif it[0] == "o":
                        it[1](e).then_inc(H[it[2]], 1)
                    else:
                        e.dma_start(out=it[1], in_=it[2]).then_inc(H[it[3]], 16)
            return body

        block.tensor(mk("pe"))
        block.scalar(mk("act"))
        block.vector(mk("dve"))
        block.gpsimd(mk("pool"))
        block.sync(mk("sp"))


class Buf:
    def __init__(self, t, k):
        self.t, self.k = t, k


class Pool:
    def __init__(self, bufs, name):
        self.bufs = bufs
        self.name = name
        self.i = 0

    def get(self):
        b = self.bufs[self.i % len(self.bufs)]
        self.i += 1
        return b


DBG_NAMES = []


NSTAGE = 4
NGROUP = 2


def flag_layout(NTICK):
    off = {"selx": 0, "sel": NTICK, "vres": (1 + NSTAGE) * NTICK, "keep": (1 + NSTAGE) * NTICK + 1,
           "first": (2 + NSTAGE) * NTICK + 1}
    return off, (3 + NSTAGE) * NTICK + 1


def build_program(NT, debug=False):
    DEPTH = 1
    NTOK = NT * T
    FOFF, NFLAG = flag_layout(NT)
    NV_ALL = NVEC + 8 + NFLAG
    nc = bass.Bass("TRN2", target_bir_lowering=False)
    x_in = nc.dram_tensor("x", [128, NJ, NTOK], F32, kind="ExternalInput").ap()
    vecs_d = nc.dram_tensor("vecs", [128, NV_ALL], F32, kind="ExternalInput").ap()
    wts_d = nc.dram_tensor("wts", [DEPTH, 128, WTOT], F32, kind="ExternalInput").ap()
    cst_d = nc.dram_tensor("consts", [128, NCONST], F32, kind="ExternalInput").ap()
    out_d = nc.dram_tensor("out", [128, NJ, NTOK], F32, kind="ExternalOutput").ap()
    snd_t = [nc.dram_tensor("snd%d" % q, [128, 4 * T], F32) for q in range(4)]
    rcv_t = [nc.dram_tensor("rcv%d" % q, [NSTAGE * 128, 4 * T], F32) for q in range(4)]

    def snd_c(c):
        return snd_t[c // 4].ap().rearrange("p (c t) -> p c t", c=4)[:, c % 4, :]

    def rcv_c(c):
        return rcv_t[c // 4].ap().rearrange("(r p) (c t) -> p r c t", r=NSTAGE, c=4)[:, :, c % 4, :]
    wb_d = nc.dram_tensor("wb", [DEPTH, 128, WTOT], BF16).ap()
    dbg_d = nc.dram_tensor("dbg", [24, 128, T], F32, kind="ExternalOutput").ap() if debug else None
    del DBG_NAMES[:]

    with ExitStack() as stack:
        S = Sched(nc, stack)

        def sb(name, shape, dt):
            return stack.enter_context(nc.sbuf_tensor("s_" + name, shape, dt))

        def mkpool(name, n, shape, dt):
            return Pool([Buf(sb("%s%d" % (name, i), shape, dt), (name, i)) for i in range(n)], name)

        def act(out, in_, func, reads, writes, bias=None, scale=None):
            kw = {}
            if bias is not None:
                kw["bias"] = bias
            if scale is not None:
                kw["scale"] = scale
            S.op("act", lambda e: e.activation(out=out, in_=in_, func=func, **kw), reads, writes)

        def tt(out, a, b, op, reads, writes, eng="dve"):
            S.op(eng, lambda e: e.tensor_tensor(out=out, in0=a, in1=b, op=op), reads, writes)

        def ts(out, a, s1, s2, op0, op1, reads, writes, eng="dve"):
            eng = "dve"
            if op1 is None:
                S.op(eng, lambda e: e.tensor_scalar(out=out, in0=a, scalar1=s1, scalar2=None, op0=op0), reads, writes)
            else:
                S.op(eng, lambda e: e.tensor_scalar(out=out, in0=a, scalar1=s1, scalar2=s2, op0=op0, op1=op1),
                     reads, writes)

        def stt(out, a, s, b, op0, op1, reads, writes, eng="dve"):
            eng = "dve"
            S.op(eng, lambda e: e.scalar_tensor_tensor(out=out, in0=a, scalar=s, in1=b, op0=op0, op1=op1),
                 reads, writes)

        def cp(out, in_, reads, writes, eng="dve"):
            S.op(eng, lambda e: e.tensor_copy(out=out, in_=in_), reads, writes)

        def rcp(out, in_, reads, writes):
            S.op("dve", lambda e: e.reciprocal(out=out, in_=in_), reads, writes)

        def mm(out, lhsT, rhs, start, stop, reads, writes):
            S.op("pe", lambda e: e.matmul(out, lhsT=lhsT, rhs=rhs, start=start, stop=stop), reads, writes)

        def tr(out, in_, ident, reads, writes):
            S.op("pe", lambda e: e.transpose(out, in_, ident), reads, writes)

        def dbg(name, ap, key, cond=True):
            if debug and cond:
                n = len(DBG_NAMES)
                DBG_NAMES.append(name)
                S.dma(dbg_d[n], ap, reads=[key], writes=[("dbg", n)], semkey=("dbg", n))

        cst = sb("cst", [128, NCONST], F32)
        cstb = sb("cstb", [128, 384], BF16)
        vecs = sb("vecs", [128, NV_ALL], F32)
        der = sb("der", [128, DEPTH * 16], F32)
        S.dma(cst[:], cst_d, writes=["cst"], semkey="cst")
        S.dma(vecs[:], vecs_d, writes=["vecs"], semkey="vecs")
        cp(cstb[:, 0:256], cst[:, COFF["ident"]:COFF["ident"] + 256], ["cst"], ["cstb"])
        cp(cstb[:, 256:384], cst[:, COFF["ones"]:COFF["ones"] + 128], ["cst"], ["cstb"])
        ident_b = cstb[:, 0:128]
        onesbd_b = cstb[:, 128:256]
        ones_b = cstb[:, 256:384]

        def C(name, w):
            return cst[:, COFF[name]:COFF[name] + w]

        def FL(name, k=0):
            o = NVEC + 8 + FOFF[name] + k
            return vecs[:, o:o + 1]

        def V(l, name, j):
            o = l * NVEC + VOFF[name] + j
            return vecs[:, o:o + 1]

        with ExitStack() as pstack:
            stg = [pstack.enter_context(nc.sbuf_tensor("stg%d" % i, [128, CVT], F32)) for i in range(3)]
            stb = [pstack.enter_context(nc.sbuf_tensor("stb%d" % i, [128, CVT], BF16)) for i in range(3)]
            n = 0
            for l in range(DEPTH):
                o = l * 16
                act(der[:, o:o + 8], vecs[:, l * NVEC + VOFF["ap"]:l * NVEC + VOFF["ap"] + 8], AF.Exp,
                    ["vecs"], ["der"])
                ts(der[:, o:o + 8], der[:, o:o + 8], 1.0, None, ALU.add, None, ["der"], ["der"])
                act(der[:, o:o + 8], der[:, o:o + 8], AF.Ln, ["der"], ["der"])
                ts(der[:, o:o + 8], der[:, o:o + 8], -8.0, None, ALU.mult, None, ["der"], ["der"])
                ts(der[:, o + 8:o + 16], vecs[:, l * NVEC + VOFF["ka"]:l * NVEC + VOFF["ka"] + 8], -1.0, 1.0,
                   ALU.mult, ALU.add, ["vecs"], ["der"])
                for c0 in range(0, WTOT, CVT):
                    w = min(CVT, WTOT - c0)
                    s = n % 3
                    S.dma(stg[s][:, 0:w], wts_d[l][:, c0:c0 + w], writes=[("stg", s)], semkey=("stg", s))
                    if n % 2 == 0:
                        cp(stb[s][:, 0:w], stg[s][:, 0:w], [("stg", s)], [("stb", s)])
                    else:
                        act(stb[s][:, 0:w], stg[s][:, 0:w], AF.Copy, [("stg", s)], [("stb", s)])
                    S.dma(wb_d[l][:, c0:c0 + w], stb[s][:, 0:w], reads=[("stb", s)], writes=["wbd"],
                          semkey=("stb", s))
                    n += 1
        S.barrier()

        xT = sb("xT", [128, NJ, T], F32)
        hT = sb("hT", [128, NJ, T], BF16)
        mb = sb("mb", [128, NJ, T], BF16)
        HG = 8
        hid = mb
        WB = mkpool("wb", 4, [128, 4096], BF16)
        wa_s = sb("wa_s", [128, 8, 128], BF16)
        wi_s = sb("wi_s", [128, 8, 128], BF16)
        w2a2_s = sb("w2a2_s", [128, D], BF16)
        g2_s = sb("g2_s", [128, D], BF16)
        v1_s = sb("v1_s", [128, 8, 32], BF16)
        v2_s = sb("v2_s", [128, D], BF16)
        TMP = mkpool("tmp", 9, [128, T], F32)
        PB = mkpool("pb", 3, [128, T + 3], F32)
        SQ = mkpool("sq", 2, [128, T], BF16)
        TB = mkpool("tb", 3, [128, T], BF16)
        _psb = [Buf(stack.enter_context(nc.psum_tensor("ps%d" % i, [128, 512], F32)), ("ps", i))
                for i in range(8)]
        PS = Pool(_psb[0:5], "ps")
        PQ = Pool(_psb[5:7], "pq")
        PSY = _psb[7]
        cTS = sb("cTS", [128, 26], F32)
        cA = sb("cA", [128, NJ, 2], F32)
        cB = sb("cB", [128, NJ, 3], F32)
        cH = sb("cH", [128, NJ], F32)
        Hf = sb("Hf", [128, NJ, 128], F32)
        Hb = sb("Hb", [128, NJ, 128], BF16)
        xwa = sb("xwa", [128, T], BF16)
        sgx = sb("sgx", [128, T], BF16)
        hv1 = sb("hv1", [32, T], BF16)
        RW = {nm: sb("rw_" + nm, [128, T], F32) for nm in
              ("r", "k", "v", "kk", "b", "ac", "sw", "cum", "epos", "eneg")}
        BD = {nm: sb("bd_" + nm, [128, NCH, 128], BF16) for nm in ("b", "k", "h", "v")}
        BDA = [sb("bd_a%d" % p, [128, NCH, 128], BF16) for p in range(2)]
        rtP = [sb("rt%d" % p, [128, T], BF16) for p in range(2)]
        BhTP = [sb("BhT%d" % p, [128, NCH, 128], BF16) for p in range(2)]
        KhTP = [sb("KhT%d" % p, [128, NCH, 128], BF16) for p in range(2)]
        VTP = [sb("VT%d" % p, [128, NCH, 128], BF16) for p in range(2)]
        GP = [sb("G%d" % p, [128, T], F32) for p in range(2)]
        BpP = [sb("Bp%d" % p, [128, T], F32) for p in range(2)]
        gamP = [sb("gam%d" % p, [128, NCH], F32) for p in range(2)]
        Nq = [[sb("Nq%d_%d" % (g, i), [128, 4, 128], BF16) for i in range(2)] for g in range(2)]
        Mq = [[sb("Mq%d_%d" % (g, i), [128, 4, 128], BF16) for i in range(2)] for g in range(2)]
        Xq = [[sb("Xq%d_%d" % (g, i), [128, 4, 128], BF16) for i in range(2)] for g in range(2)]
        XfP = [sb("Xf%d" % p, [128, NCH, 128], BF16) for p in range(2)]
        AakTP = [sb("AakT%d" % p, [128, NCH, 128], BF16) for p in range(2)]
        ArbTP = [sb("ArbT%d" % p, [128, NCH, 64], BF16) for p in range(2)]
        ArkTP = [sb("ArkT%d" % p, [128, NCH, 64], BF16) for p in range(2)]
        WU = mkpool("wu", 4, [128, 128], BF16)
        vft = sb("vft", [128, T], F32)
        st4 = sb("st4", [128, NSTAGE - 1, T], F32)
        sm1 = sb("sm1", [128, NJ], F32)

        pieces = []
        for l in range(DEPTH):
            win3 = wb_d[l][:, WOFF["win"]:WOFF["win"] + 8 * N_IN].rearrange("p (k c) -> p k c", k=8)
            wout3 = wb_d[l][:, WOFF["wout"]:WOFF["wout"] + 8 * D].rearrange("p (k c) -> p k c", k=8)
            w13 = wb_d[l][:, WOFF["w1"]:WOFF["w1"] + 8 * D_FF].rearrange("p (k c) -> p k c", k=8)
            w23 = wb_d[l][:, WOFF["w2"]:WOFF["w2"] + 32 * D].rearrange("p (k c) -> p k c", k=32)
            for i in range(NT):
                pieces.append((win3[:, :, 0:256], (8, 256)))
                for j in range(NJ):
                    c0 = 256 + j * 1408
                    pieces.append((win3[:, :, c0:c0 + 384], (8, 384)))
                    pieces.append((win3[:, :, c0 + 384:c0 + 768], (8, 384)))
                    pieces.append((win3[:, :, c0 + 768:c0 + 1152], (8, 384)))
                    pieces.append((win3[:, :, c0 + 1152:c0 + 1408], (8, 256)))
                for h in range(2):
                    pieces.append((wout3[:, :, h * 512:(h + 1) * 512], (8, 512)))
                for q in range(4):
                    for h in range(2):
                        pieces.append((w13[:, :, q * 1024 + h * 512:q * 1024 + (h + 1) * 512], (8, 512)))
                    for h in range(2):
                        pieces.append((w23[:, q * 8:(q + 1) * 8, h * 512:(h + 1) * 512], (8, 512)))
        pstate = {"issued": 0, "next": 0, "slots": {}}

        def _issue(n):
            src, (k, c) = pieces[n]
            b = WB.get()
            view = b.t[:, 0:k * c].rearrange("p (k c) -> p k c", k=k)
            S.dma(view, src, reads=["wbd"], writes=[b.k], semkey=b.k)
            pstate["slots"][n] = (view, b.k)

        def next_piece():
            n = pstate["next"]
            pstate["next"] += 1
            while pstate["issued"] < min(len(pieces), n + 4):
                _issue(pstate["issued"])
                pstate["issued"] += 1
            return pstate["slots"].pop(n)

        def XK(j):
            return ("xT", j)

        def allgather(q):
            S.cc(lambda e: e.collective_compute(
                "AllGather", ALU.bypass,
                replica_groups=[list(range(g * NSTAGE, (g + 1) * NSTAGE)) for g in range(NGROUP)],
                ins=[snd_t[q].ap().opt()], outs=[rcv_t[q].ap().opt()]),
                reads=[("snd", c) for c in range(4 * q, 4 * q + 4)], writes=[("rcv", q)])

        def proj(wv, wk, cidx, rhsT, rkeys):
            ps = PS.get()
            for kc in range(8):
                mm(ps.t[:], wv[:, kc, cidx * 128:(cidx + 1) * 128], rhsT[:, kc, :], kc == 0, kc == 7,
                   [wk] + rkeys, [ps.k])
            return ps

        def rmsnorm_to(l, gname, dst, dkeyname):
            ps = PS.get()
            for j in range(NJ):
                sq = SQ.get()
                act(sq.t[:], xT[:, j, :], AF.Square, [XK(j)], [sq.k])
                mm(ps.t[:], ones_b, sq.t[:], j == 0, j == NJ - 1, [sq.k, "cstb"], [ps.k])
            rstd = TMP.get()
            act(rstd.t[:], ps.t[:], AF.Ln, [ps.k, "cst"], [rstd.k], bias=C("eps", 1), scale=1.0 / D)
            act(rstd.t[:], rstd.t[:], AF.Exp, [rstd.k], [rstd.k], scale=-0.5)
            for j in range(NJ):
                if gname == "final":
                    g = vecs[:, DEPTH * NVEC + j:DEPTH * NVEC + j + 1]
                else:
                    g = V(l, gname, j)
                stt(dst[:, j, :], xT[:, j, :], g, rstd.t[:], ALU.mult, ALU.mult,
                    [XK(j), rstd.k, "vecs"], [(dkeyname, j)])

        def tshift(ps, cidx, mu_ap, out_ap, okeys):
            pb = PB.get()
            cp(pb.t[:, 0:1], cTS[:, cidx:cidx + 1], [("cTS", cidx)], [pb.k])
            act(pb.t[:, 1:T + 1], ps.t[:], AF.Copy, [ps.k], [pb.k])
            cp(cTS[:, cidx:cidx + 1], pb.t[:, T:T + 1], [pb.k], [("cTS", cidx)])
            d = TMP.get()
            tt(d.t[:], pb.t[:, 0:T], pb.t[:, 1:T + 1], ALU.subtract, [pb.k], [d.k], eng="pool")
            stt(out_ap, d.t[:], mu_ap, pb.t[:, 1:T + 1], ALU.mult, ALU.add, [d.k, pb.k, "vecs"], okeys, eng="pool")

        for l in range(DEPTH):
            for buf, key in ((cTS, None), (cA, None), (cB, None), (cH, None), (Hf, None), (Hb, None)):
                pass
            S.op("dve", lambda e: e.memset(cTS[:], 0.0), [], [("cTS", c) for c in range(26)])
            S.op("dve", lambda e: e.memset(cA[:], 0.0), [], [("cA", j) for j in range(NJ)])
            S.op("dve", lambda e: e.memset(cB[:], 0.0), [], [("cB", j) for j in range(NJ)])
            S.op("dve", lambda e: e.memset(cH[:], 0.0), [], [("cH", j) for j in range(NJ)])
            S.op("dve", lambda e: e.memset(Hf[:], 0.0), [], [("Hf", j) for j in range(NJ)])
            S.op("dve", lambda e: e.memset(Hb[:], 0.0), [], [("Hb", j) for j in range(NJ)])
            wl = wb_d[l]
            S.dma(wa_s[:], wl[:, WOFF["wa"]:WOFF["wa"] + 1024].rearrange("p (k c) -> p k c", k=8), ["wbd"],
                  ["wa_s"], semkey="wa_s")
            S.dma(wi_s[:], wl[:, WOFF["wi"]:WOFF["wi"] + 1024].rearrange("p (k c) -> p k c", k=8), ["wbd"],
                  ["wi_s"], semkey="wi_s")
            S.dma(w2a2_s[:], wl[:, WOFF["w2a2"]:WOFF["w2a2"] + D], ["wbd"], ["w2a2_s"], semkey="w2a2_s")
            S.dma(g2_s[:], wl[:, WOFF["g2"]:WOFF["g2"] + D], ["wbd"], ["g2_s"], semkey="g2_s")
            if True:
                S.dma(v1_s[:], wl[:, WOFF["v1"]:WOFF["v1"] + 256].rearrange("p (k c) -> p k c", k=8), ["wbd"],
                      ["v1_s"], semkey="v1_s")
                S.dma(v2_s[:], wl[:, WOFF["v2"]:WOFF["v2"] + D], ["wbd"], ["v2_s"], semkey="v2_s")
            sp8 = lambda j, l=l: der[:, l * 16 + j:l * 16 + j + 1]
            omka = lambda j, l=l: der[:, l * 16 + 8 + j:l * 16 + 8 + j + 1]
            HK = ["hT%d" % 0]
            hkeys = [("hT", j) for j in range(NJ)]

            for i in range(NT):
                t0 = i * T
                keep = FL("keep", i)
                for stt_, kn, n_ in ((cTS[:], "cTS", 26), (cA[:].rearrange("p a b -> p (a b)"), "cA", NJ),
                                     (cB[:].rearrange("p a b -> p (a b)"), "cB", NJ), (cH[:], "cH", NJ),
                                     (Hf[:].rearrange("p a b -> p (a b)"), "Hf", NJ),
                                     (Hb[:].rearrange("p a b -> p (a b)"), "Hb", NJ)):
                    ks_ = [(kn, c) for c in range(n_)]
                    ts(stt_, stt_, keep, None, ALU.mult, None, ks_ + ["vecs"], ks_)
                for j in range(NJ):
                    xi = TMP.get()
                    S.dma(xi.t[:], x_in[:, j, t0:t0 + T], writes=[xi.k], semkey=xi.k)
                    ts(xT[:, j, :], xi.t[:], FL("selx", i), None, ALU.mult, None, [xi.k, "vecs"], [XK(j)])
                    if i > 0:
                        S.dma(st4[:], rcv_c(j)[:, 0:NSTAGE - 1, :], reads=[("rcv", j // 4)], writes=["st4"], semkey="st4")
                        for r in range(NSTAGE - 1):
                            stt(xT[:, j, :], st4[:, r, :], FL("sel", r * NT + i), xT[:, j, :], ALU.mult, ALU.add,
                                ["st4", XK(j), "vecs"], [XK(j)], eng="pool")
                rmsnorm_to(l, "g1", hT, "hT")

                wv, wk = next_piece()
                ps = proj(wv, wk, 0, hT, hkeys)
                xs_t = TMP.get()
                tshift(ps, 0, V(l, "mu_lr", 0), xs_t.t[:], [xs_t.k])
                act(xwa[0:64, :], xs_t.t[0:64, :], AF.Tanh, [xs_t.k], ["xwa"])
                act(xwa[64:128, :], xs_t.t[64:128, :], AF.Copy, [xs_t.k], ["xwa"])
                ps = proj(wv, wk, 1, hT, hkeys)
                xg_t = TMP.get()
                tshift(ps, 1, V(l, "mu_lr", 1), xg_t.t[:], [xg_t.k])
                act(sgx[:], xg_t.t[:], AF.Sigmoid, [xg_t.k], ["sgx"])
                if True:
                    ps = PS.get()
                    for kc in range(8):
                        mm(ps.t[0:32, :], v1_s[:, kc, :], hT[:, kc, :], kc == 0, kc == 7, ["v1_s"] + hkeys, [ps.k])
                    act(hv1[:], ps.t[0:32, :], AF.Copy, [ps.k], ["hv1"])

                PL = "pool"

                def v3(ap):
                    return ap.rearrange("p (c t) -> p c t", c=NCH)

                def pv4(ps):
                    return ps.t[:].rearrange("p (c t) -> p c t", c=4)

                mSU4 = C("mSU", 128).unsqueeze(1).broadcast_to([128, 4, 128])
                mSL4 = C("mSL", 128).unsqueeze(1).broadcast_to([128, 4, 128])
                idn4 = C("ident", 128).unsqueeze(1).broadcast_to([128, 4, 128])
                mIU8 = C("mIU", 64).unsqueeze(1).broadcast_to([128, NCH, 64])

                def prep(j):
                    p = j % 2
                    js = slice(j * 128, (j + 1) * 128)
                    bd_a, rt, BhT, KhT, VT = BDA[p], rtP[p], BhTP[p], KhTP[p], VTP[p]
                    Xf, AakT, ArbT, ArkT = XfP[p], AakTP[p], ArbTP[p], ArkTP[p]
                    G, Bp, gam = GP[p], BpP[p], gamP[p]

                    def K(nm):
                        return (nm, p)
                    wvA, wkA = next_piece()
                    psB = proj(wvA, wkA, 0, hT, hkeys)
                    psC = proj(wvA, wkA, 1, hT, hkeys)
                    psX = proj(wvA, wkA, 2, hT, hkeys)
                    csb = TMP.get()
                    act(csb.t[:], psC.t[:], AF.Copy, [psC.k], [csb.k])
                    cx = PB.get()
                    cp(cx.t[:, 0:2], cA[:, j, :], [("cA", j)], [cx.k])
                    tt(cx.t[:, 2:T + 2], csb.t[:], psX.t[:], ALU.mult, [csb.k, psX.k], [cx.k])
                    cp(cA[:, j, :], cx.t[:, T:T + 2], [cx.k], [("cA", j)])
                    acc = TMP.get()
                    ts(acc.t[:], cx.t[:, 2:T + 2], V(l, "cA2", j), None, ALU.mult, None, [cx.k, "vecs"], [acc.k], eng=PL)
                    stt(acc.t[:], cx.t[:, 1:T + 1], V(l, "cA1", j), acc.t[:], ALU.mult, ALU.add,
                        [cx.k, acc.k], [acc.k], eng=PL)
                    stt(acc.t[:], cx.t[:, 0:T], V(l, "cA0", j), acc.t[:], ALU.mult, ALU.add,
                        [cx.k, acc.k], [acc.k], eng=PL)
                    tt(acc.t[:], acc.t[:], psB.t[:], ALU.mult, [acc.k, psB.k], [acc.k])
                    yield
                    wvA, wkA = next_piece()
                    psg = proj(wvA, wkA, 2, hT, hkeys)
                    gA = TMP.get()
                    act(gA.t[:], psg.t[:], AF.Sigmoid, [psg.k, "vecs"], [gA.k], bias=V(l, "mbA", j))
                    tt(Bp[:], acc.t[:], gA.t[:], ALU.mult, [acc.k, gA.k], [K("Bp")])
                    yield
                    psxb = proj(wvA, wkA, 0, hT, hkeys)
                    psgb = proj(wvA, wkA, 1, hT, hkeys)
                    xbb = PB.get()
                    cp(xbb.t[:, 0:3], cB[:, j, :], [("cB", j)], [xbb.k])
                    act(xbb.t[:, 3:T + 3], psxb.t[:], AF.Copy, [psxb.k], [xbb.k])
                    cp(cB[:, j, :], xbb.t[:, T:T + 3], [xbb.k], [("cB", j)])
                    u = TMP.get()
                    ts(u.t[:], xbb.t[:, 3:T + 3], V(l, "cB3", j), V(l, "cBb", j), ALU.mult, ALU.add,
                       [xbb.k, "vecs"], [u.k], eng=PL)
                    for tap in range(3):
                        stt(u.t[:], xbb.t[:, tap:tap + T], V(l, "cB%d" % tap, j), u.t[:], ALU.mult, ALU.add,
                            [xbb.k, u.k], [u.k], eng=PL)
                    ub = TB.get()
                    act(ub.t[:], u.t[:], AF.Copy, [u.k], [ub.k])
                    yield
                    psga = PS.get()
                    mm(psga.t[:], wa_s[:, j, :], ub.t[:], True, True, ["wa_s", ub.k], [psga.k])
                    psgi = PS.get()
                    mm(psgi.t[:], wi_s[:, j, :], ub.t[:], True, True, ["wi_s", ub.k], [psgi.k])
                    ga = TMP.get()
                    act(ga.t[:], psga.t[:], AF.Sigmoid, [psga.k, "vecs"], [ga.k], bias=V(l, "ba", j))
                    gi = TMP.get()
                    act(gi.t[:], psgi.t[:], AF.Sigmoid, [psgi.k, "vecs"], [gi.k], bias=V(l, "bi", j))
                    aa = TMP.get()
                    act(aa.t[:], ga.t[:], AF.Exp, [ga.k, "der"], [aa.k], scale=sp8(j))
                    mlt = ga
                    stt(mlt.t[:], aa.t[:], -1.0, aa.t[:], ALU.mult, ALU.mult, [aa.k], [mlt.k])
                    act(mlt.t[:], mlt.t[:], AF.Sqrt, [mlt.k, "cst"], [mlt.k], bias=C("ones", 1))
                    ts(sm1[:, j:j + 1], mlt.t[:, 0:1], -1.0, 1.0, ALU.mult, ALU.add, [mlt.k], [("sm1", j)])
                    stt(mlt.t[:, 0:1], sm1[:, j:j + 1], FL("first", i), mlt.t[:, 0:1], ALU.mult, ALU.add,
                        [("sm1", j), mlt.k, "vecs"], [mlt.k])
                    tt(u.t[:], u.t[:], gi.t[:], ALU.mult, [u.k, gi.k], [u.k], eng=PL)
                    tt(u.t[:], u.t[:], mlt.t[:], ALU.mult, [u.k, mlt.k], [u.k])
                    yield
                    hb = gi
                    S.op("dve", lambda e, o=hb.t, a=aa.t, uu=u.t, j=j: e.tensor_tensor_scan(
                        out=o[:], data0=a[:], data1=uu[:], initial=cH[:, j:j + 1], op0=ALU.mult, op1=ALU.add),
                        [aa.k, u.k, ("cH", j)], [hb.k])
                    cp(cH[:, j:j + 1], hb.t[:, T - 1:T], [hb.k], [("cH", j)])
                    gx = TMP.get()
                    act(gx.t[:], psgb.t[:], AF.Copy, [psgb.k], [gx.k])
                    g2t = TMP.get()
                    act(g2t.t[:], psgb.t[:], AF.Square, [psgb.k], [g2t.k])
                    ts(g2t.t[:], g2t.t[:], 0.044715, 1.0, ALU.mult, ALU.add, [g2t.k], [g2t.k], eng=PL)
                    tt(g2t.t[:], g2t.t[:], gx.t[:], ALU.mult, [g2t.k, gx.k], [g2t.k], eng=PL)
                    act(g2t.t[:], g2t.t[:], AF.Sigmoid, [g2t.k], [g2t.k], scale=GELU_C)
                    tt(gx.t[:], gx.t[:], g2t.t[:], ALU.mult, [gx.k, g2t.k], [gx.k], eng=PL)
                    tt(gx.t[:], gx.t[:], hb.t[:], ALU.mult, [gx.k, hb.k], [gx.k])
                    yield
                    wvB, wkB = next_piece()
                    psg = proj(wvB, wkB, 0, hT, hkeys)
                    gB = TMP.get()
                    act(gB.t[:], psg.t[:], AF.Sigmoid, [psg.k, "vecs"], [gB.k], bias=V(l, "mbB", j))
                    tt(gx.t[:], gx.t[:], gB.t[:], ALU.mult, [gx.k, gB.k], [gx.k], eng=PL)
                    tt(Bp[:], Bp[:], gx.t[:], ALU.add, [K("Bp"), gx.k], [K("Bp")], eng=PL)
                    yield
                    psgC = proj(wvB, wkB, 1, hT, hkeys)
                    gCt = TMP.get()
                    act(gCt.t[:], psgC.t[:], AF.Sigmoid, [psgC.k, "vecs"], [gCt.k], bias=V(l, "mbC", j))
                    psr = proj(wvB, wkB, 2, hT, hkeys)
                    tshift(psr, 2 + j, V(l, "mu_r", j), RW["r"][:], ["r"])
                    yield
                    wvB, wkB = next_piece()
                    psk = proj(wvB, wkB, 0, hT, hkeys)
                    tshift(psk, 10 + j, V(l, "mu_k", j), RW["k"][:], ["k"])
                    yield
                    psv = proj(wvB, wkB, 1, hT, hkeys)
                    tshift(psv, 18 + j, V(l, "mu_v", j), RW["v"][:], ["v"])
                    yield
                    if i > 0:
                        S.dma(st4[:], rcv_c(8 + j)[:, 0:NSTAGE - 1, :], reads=[("rcv", 2 + j // 4)], writes=["st4"],
                              semkey="st4")
                        ts(vft[:], st4[:, 0, :], FL("sel", i), None, ALU.mult, None, ["st4", "vecs"], ["vft"], eng=PL)
                        for r in range(1, NSTAGE - 1):
                            stt(vft[:], st4[:, r, :], FL("sel", r * NT + i), vft[:], ALU.mult, ALU.add,
                                ["st4", "vft", "vecs"], ["vft"], eng=PL)
                    else:
                        S.op("dve", lambda e: e.memset(vft[:], 0.0), [], ["vft"])
                    ps = PS.get()
                    mm(ps.t[:], v2_s[0:32, js], hv1[:], True, True, ["v2_s", "hv1"], [ps.k])
                    sgv = TMP.get()
                    act(sgv.t[:], ps.t[:], AF.Sigmoid, [ps.k, "vecs"], [sgv.k], bias=V(l, "v0", j))
                    ts(sgv.t[:], sgv.t[:], FL("vres"), None, ALU.mult, None, [sgv.k, "vecs"], [sgv.k], eng=PL)
                    tt(vft[:], vft[:], RW["v"][:], ALU.subtract, ["vft", "v"], ["vft"], eng=PL)
                    vfo = TMP.get()
                    stt(vfo.t[:], vft[:], FL("vres"), RW["v"][:], ALU.mult, ALU.add, ["vft", "v", "vecs"], [vfo.k], eng=PL)
                    S.dma(snd_c(8 + j), vfo.t[:], reads=[vfo.k], writes=[("snd", 8 + j)], semkey=vfo.k)
                    if j % 4 == 3 and i < NT - 1:
                        allgather(2 + j // 4)
                    tt(vft[:], vft[:], sgv.t[:], ALU.mult, ["vft", sgv.k], ["vft"], eng=PL)
                    tt(RW["v"][:], RW["v"][:], vft[:], ALU.add, ["v", "vft"], ["v"], eng=PL)
                    yield
                    ps = PS.get()
                    mm(ps.t[:], w2a2_s[0:64, js], xwa[0:64, :], True, True, ["w2a2_s", "xwa"], [ps.k])
                    act(RW["sw"][:], ps.t[:], AF.Sigmoid, [ps.k, "vecs"], ["sw"], bias=V(l, "w0", j))
                    ps = PS.get()
                    mm(ps.t[:], w2a2_s[64:128, js], xwa[64:128, :], True, True, ["w2a2_s", "xwa"], [ps.k])
                    act(RW["ac"][:], ps.t[:], AF.Sigmoid, [ps.k, "vecs"], ["ac"], bias=V(l, "a0", j))
                    ps = PS.get()
                    mm(ps.t[:], g2_s[:, js], sgx[:], True, True, ["g2_s", "sgx"], [ps.k])
                    tt(G[:], ps.t[:], gCt.t[:], ALU.mult, [ps.k, gCt.k], [K("G")])
                    yield
                    ts(RW["kk"][:], RW["k"][:], V(l, "kkv", j), None, ALU.mult, None, ["k", "vecs"], ["kk"], eng=PL)
                    sq = SQ.get()
                    act(sq.t[:], RW["kk"][:], AF.Square, ["kk"], [sq.k])
                    ps = PS.get()
                    mm(ps.t[:], onesbd_b, sq.t[:], True, True, ["cstb", sq.k], [ps.k])
                    inv = TMP.get()
                    ts(inv.t[:], ps.t[:], 1e-24, None, ALU.max, None, [ps.k], [inv.k])
                    act(inv.t[:], inv.t[:], AF.Ln, [inv.k], [inv.k])
                    act(inv.t[:], inv.t[:], AF.Exp, [inv.k], [inv.k], scale=-0.5)
                    tt(RW["kk"][:], RW["kk"][:], inv.t[:], ALU.mult, ["kk", inv.k], ["kk"], eng=PL)
                    km = TMP.get()
                    ts(km.t[:], RW["ac"][:], V(l, "ka", j), omka(j), ALU.mult, ALU.add, ["ac", "vecs", "der"], [km.k], eng=PL)
                    tt(RW["k"][:], RW["k"][:], km.t[:], ALU.mult, ["k", km.k], ["k"], eng=PL)
                    tt(RW["b"][:], RW["kk"][:], RW["ac"][:], ALU.mult, ["kk", "ac"], ["b"], eng=PL)
                    yield
                    rkb = TB.get()
                    stt(rkb.t[:], RW["r"][:], V(l, "rk", j), RW["k"][:], ALU.mult, ALU.mult, ["r", "k", "vecs"], [rkb.k], eng=PL)
                    ps = PS.get()
                    mm(ps.t[:], onesbd_b, rkb.t[:], True, True, ["cstb", rkb.k], [ps.k])
                    bn = TMP.get()
                    tt(bn.t[:], ps.t[:], RW["v"][:], ALU.mult, [ps.k, "v"], [bn.k])
                    tt(bn.t[:], bn.t[:], G[:], ALU.mult, [bn.k, K("G")], [bn.k], eng=PL)
                    tt(Bp[:], Bp[:], bn.t[:], ALU.add, [K("Bp"), bn.k], [K("Bp")], eng=PL)
                    yield
                    S.op("dve", lambda e: e.tensor_tensor_scan(
                        out=RW["cum"][:], data0=C("cmask", 512), data1=RW["sw"][:], initial=0.0,
                        op0=ALU.mult, op1=ALU.add), ["cst", "sw"], ["cum"])
                    act(RW["epos"][:], RW["cum"][:], AF.Exp, ["cum"], ["epos"], scale=-DECAY_C)
                    act(RW["eneg"][:], RW["cum"][:], AF.Exp, ["cum"], ["eneg"], scale=DECAY_C)
                    cp(gam[:].unsqueeze(2), v3(RW["epos"][:])[:, :, CH - 1:CH], ["epos"], [K("gam")])
                    epv = TMP.get()
                    tt(epv.t[:], RW["cum"][:], RW["sw"][:], ALU.subtract, ["cum", "sw"], [epv.k], eng=PL)
                    act(epv.t[:], epv.t[:], AF.Exp, [epv.k], [epv.k], scale=-DECAY_C)
                    eht = TMP.get()
                    cum3 = v3(RW["cum"][:])
                    tt(v3(eht.t[:]), cum3[:, :, CH - 1:CH].broadcast_to([128, NCH, CH]),
                       cum3, ALU.subtract, ["cum"], [eht.k], eng=PL)
                    act(eht.t[:], eht.t[:], AF.Exp, [eht.k], [eht.k], scale=-DECAY_C)
                    yield

                    def bdv(t_, hh):
                        return t_[:].rearrange("p c (h t) -> p c h t", h=2)[:, :, hh, :]
                    for hh in range(2):
                        hm = cst[:, COFF["hmask"] + hh:COFF["hmask"] + hh + 1]
                        nhm = cst[:, COFF["nhmask"] + hh:COFF["nhmask"] + hh + 1]
                        stt(bdv(bd_a, hh), v3(RW["kk"][:]), nhm, v3(epv.t[:]), ALU.mult, ALU.mult,
                            ["kk", epv.k, "cst"], [K("bd_a")], eng=PL)
                        stt(bdv(BD["b"], hh), v3(RW["b"][:]), hm, v3(RW["eneg"][:]), ALU.mult, ALU.mult,
                            ["b", "eneg", "cst"], ["bd_b"], eng=PL)
                        stt(bdv(BD["k"], hh), v3(RW["k"][:]), hm, v3(RW["eneg"][:]), ALU.mult, ALU.mult,
                            ["k", "eneg", "cst"], ["bd_k"])
                        ts(bdv(BD["v"], hh), v3(RW["v"][:]), hm, None, ALU.mult, None, ["v", "cst"], ["bd_v"], eng=PL)
                    tt(rt[:], RW["r"][:], RW["epos"][:], ALU.mult, ["r", "epos"], [K("rt")])
                    yield

                    def transp(srcnm, dst, dkey):
                        ps = PS.get()
                        pv = ps.t[:].bitcast(BF16).rearrange("p (c t) -> p c t", c=NCH)
                        for c in range(NCH):
                            tr(pv[:, c, :], BD[srcnm][:, c, :], ident_b, ["bd_" + srcnm, "cstb"], [ps.k])
                        act(dst[:], pv, AF.Copy, [ps.k], [dkey])
                    for hh in range(2):
                        hm = cst[:, COFF["hmask"] + hh:COFF["hmask"] + hh + 1]
                        stt(bdv(BD["h"], hh), v3(RW["b"][:]), hm, v3(eht.t[:]), ALU.mult, ALU.mult,
                            ["b", eht.k, "cst"], ["bd_h"], eng=PL)
                    transp("h", BhT, K("BhT"))
                    yield
                    for hh in range(2):
                        hm = cst[:, COFF["hmask"] + hh:COFF["hmask"] + hh + 1]
                        stt(bdv(BD["h"], hh), v3(RW["k"][:]), hm, v3(eht.t[:]), ALU.mult, ALU.mult,
                            ["k", eht.k, "cst"], ["bd_h"])
                    transp("h", KhT, K("KhT"))
                    yield
                    transp("v", VT, K("VT"))
                    yield
                    for g in range(2):
                        ps = PS.get()
                        for cc in range(4):
                            c = g * 4 + cc
                            mm(pv4(ps)[:, cc, :], BD["k"][:, c, :], bd_a[:, c, :], True, True, ["bd_k", K("bd_a")], [ps.k])
                        tt(AakT[:, g * 4:(g + 1) * 4, :], pv4(ps), mSU4, ALU.mult, [ps.k, "cst"], [K("AakT")])
                    yield
                    for g in range(2):
                        psN = PS.get()
                        for cc in range(4):
                            c = g * 4 + cc
                            mm(pv4(psN)[:, cc, :], BD["b"][:, c, :], bd_a[:, c, :], True, True, ["bd_b", K("bd_a")], [psN.k])
                        tt(Nq[g][0][:], pv4(psN), mSU4, ALU.mult, [psN.k, "cst"], [("Nq", g, 0)])
                        psM = PS.get()
                        for cc in range(4):
                            c = g * 4 + cc
                            mm(pv4(psM)[:, cc, :], bd_a[:, c, :], BD["b"][:, c, :], True, True, ["bd_b", K("bd_a")], [psM.k])
                        tt(Mq[g][0][:], pv4(psM), mSL4, ALU.mult, [psM.k, "cst"], [("Mq", g, 0)])
                        tt(Xq[g][0][:], Nq[g][0][:], idn4, ALU.add, [("Nq", g, 0), "cst"], [("Xq", g, 0)], eng=PL)
                    yield
                    for lev in range(1, 6):
                        s_, d_ = (lev - 1) % 2, lev % 2
                        for g in range(2):
                            if lev < 5:
                                psN = PS.get()
                                for cc in range(4):
                                    mm(pv4(psN)[:, cc, :], Mq[g][s_][:, cc, :], Nq[g][s_][:, cc, :], True, True,
                                       [("Mq", g, s_), ("Nq", g, s_)], [psN.k])
                                act(Nq[g][d_][:], pv4(psN), AF.Copy, [psN.k], [("Nq", g, d_)])
                            psM = PS.get()
                            for cc in range(4):
                                mm(pv4(psM)[:, cc, :], Nq[g][s_][:, cc, :], Mq[g][s_][:, cc, :], True, True,
                                   [("Mq", g, s_), ("Nq", g, s_)], [psM.k])
                            act(Mq[g][d_][:], pv4(psM), AF.Copy, [psM.k], [("Mq", g, d_)])
                        yield
                        for g in range(2):
                            psX = PS.get()
                            for cc in range(4):
                                mm(pv4(psX)[:, cc, :], Mq[g][d_][:, cc, :], Xq[g][s_][:, cc, :], True, True,
                                   [("Mq", g, d_), ("Xq", g, s_)], [psX.k])
                            if lev < 5:
                                tt(Xq[g][d_][:], pv4(psX), Xq[g][s_][:], ALU.add, [psX.k, ("Xq", g, s_)], [("Xq", g, d_)])
                            else:
                                tt(Xf[:, g * 4:(g + 1) * 4, :], pv4(psX), Xq[g][s_][:], ALU.add,
                                   [psX.k, ("Xq", g, s_)], [K("Xf")])
                        yield
                    ps = PS.get()
                    pv8 = ps.t[:].rearrange("p (c t) -> p c t", c=NCH)
                    for c in range(NCH):
                        mm(pv8[:, c, :], BD["b"][:, c, :], rt[:, c * CH:(c + 1) * CH], True, True, ["bd_b", K("rt")], [ps.k])
                    tt(ArbT[:], pv8, mIU8, ALU.mult, [ps.k, "cst"], [K("ArbT")])
                    ps = PS.get()
                    pv8 = ps.t[:].rearrange("p (c t) -> p c t", c=NCH)
                    for c in range(NCH):
                        mm(pv8[:, c, :], BD["k"][:, c, :], rt[:, c * CH:(c + 1) * CH], True, True, ["bd_k", K("rt")], [ps.k])
                    tt(ArkT[:], pv8, mIU8, ALU.mult, [ps.k, "cst"], [K("ArkT")])
                    yield

                def advance(gen, n):
                    if gen is None:
                        return None
                    for _ in range(n):
                        try:
                            next(gen)
                        except StopIteration:
                            return None
                    return gen

                def chain(j, gen):
                    p = j % 2
                    bd_a, rt, BhT, KhT, VT = BDA[p], rtP[p], BhTP[p], KhTP[p], VTP[p]
                    Xf, AakT, ArbT, ArkT = XfP[p], AakTP[p], ArbTP[p], ArkTP[p]
                    G, Bp, gam = GP[p], BpP[p], gamP[p]

                    def K(nm):
                        return (nm, p)
                    psY = PSY
                    HfK, HbK = ("Hf", j), ("Hb", j)
                    for c in range(NCH):
                        cs = slice(c * CH, (c + 1) * CH)
                        psW = PQ.get()
                        mm(psW.t[:, 0:128], bd_a[:, c, :], Hb[:, j, :], True, False, [K("bd_a"), HbK], [psW.k])
                        mm(psW.t[:, 0:128], AakT[:, c, :], VT[:, c, :], False, True, [K("AakT"), K("VT")], [psW.k])
                        wb_ = WU.get()
                        act(wb_.t[:], psW.t[:, 0:128], AF.Copy, [psW.k], [wb_.k])
                        gen = advance(gen, 1)
                        psU = PQ.get()
                        mm(psU.t[:, 0:128], Xf[:, c, :], wb_.t[:], True, True, [K("Xf"), wb_.k], [psU.k])
                        ub_ = WU.get()
                        act(ub_.t[:], psU.t[:, 0:128], AF.Copy, [psU.k], [ub_.k])
                        gen = advance(gen, 1)
                        mm(psY.t[:, cs], Hb[:, j, :], rt[:, cs], True, False, [HbK, K("rt")], [psY.k])
                        mm(psY.t[:, cs], ub_.t[:], ArbT[:, c, :], False, False, [ub_.k, K("ArbT")], [psY.k])
                        mm(psY.t[:, cs], VT[:, c, :], ArkT[:, c, :], False, True, [K("VT"), K("ArkT")], [psY.k])
                        psH = PQ.get()
                        mm(psH.t[:, 0:128], BhT[:, c, :], ub_.t[:], True, False, [K("BhT"), ub_.k], [psH.k])
                        mm(psH.t[:, 0:128], KhT[:, c, :], VT[:, c, :], False, True, [K("KhT"), K("VT")], [psH.k])
                        stt(Hf[:, j, :], Hf[:, j, :], gam[:, c:c + 1], psH.t[:, 0:128],
                            ALU.mult, ALU.add, [HfK, K("gam"), psH.k], [HfK])
                        act(Hb[:, j, :], Hf[:, j, :], AF.Copy, [HfK], [HbK])
                        gen = advance(gen, 2)
                    while gen is not None:
                        gen = advance(gen, 8)
                    ysb = TMP.get()
                    act(ysb.t[:], psY.t[:], AF.Copy, [psY.k], [ysb.k])
                    ysq = TMP.get()
                    act(ysq.t[:], psY.t[:], AF.Square, [psY.k], [ysq.k])
                    psm = PS.get()
                    mm(psm.t[:], C("ones_bd", 128), ysb.t[:], True, True, ["cst", ysb.k], [psm.k])
                    pss = PS.get()
                    mm(pss.t[:], C("ones_bd", 128), ysq.t[:], True, True, ["cst", ysq.k], [pss.k])
                    mu = TMP.get()
                    ts(mu.t[:], psm.t[:], 1.0 / 64, None, ALU.mult, None, [psm.k], [mu.k])
                    var = ysq
                    stt(var.t[:], mu.t[:], -1.0, mu.t[:], ALU.mult, ALU.mult, [mu.k], [var.k], eng=PL)
                    stt(var.t[:], pss.t[:], 1.0 / 64, var.t[:], ALU.mult, ALU.add, [pss.k, var.k], [var.k])
                    act(var.t[:], var.t[:], AF.Ln, [var.k, "cst"], [var.k], bias=C("lnxeps", 1))
                    act(var.t[:], var.t[:], AF.Exp, [var.k], [var.k], scale=-0.5)
                    tt(ysb.t[:], ysb.t[:], mu.t[:], ALU.subtract, [ysb.k, mu.k], [ysb.k], eng=PL)
                    tt(ysb.t[:], ysb.t[:], var.t[:], ALU.mult, [ysb.k, var.k], [ysb.k])
                    ts(ysb.t[:], ysb.t[:], V(l, "lg", j), V(l, "lb", j), ALU.mult, ALU.add, [ysb.k, "vecs"], [ysb.k], eng=PL)
                    tt(ysb.t[:], ysb.t[:], G[:], ALU.mult, [ysb.k, K("G")], [ysb.k])
                    tt(mb[:, j, :], Bp[:], ysb.t[:], ALU.add, [K("Bp"), ysb.k], [("mb", j)], eng=PL)

                g0 = prep(0)
                while advance(g0, 8) is not None:
                    pass
                for j in range(NJ):
                    chain(j, prep(j + 1) if j + 1 < NJ else None)

                mkeys = [("mb", j) for j in range(NJ)]
                for h in range(2):
                    wv, wk = next_piece()
                    for jo in range(4 * h, 4 * h + 4):
                        ps = proj(wv, wk, jo - 4 * h, mb, mkeys)
                        tt(xT[:, jo, :], xT[:, jo, :], ps.t[:], ALU.add, [XK(jo), ps.k], [XK(jo)])
                dbg("x_attn", xT[:, 0, :], XK(0), l == 0 and i == 0)
                dbg("x_attn7", xT[:, 7, :], XK(7), l == 0 and i == 0)
                rmsnorm_to(l, "g2n", hT, "hT")
                for q in range(4):
                    for h in range(2):
                        wv1, wk1 = next_piece()
                        for f in range(4 * h, 4 * h + 4):
                            ps = proj(wv1, wk1, f - 4 * h, hT, hkeys)
                            rl = TMP.get()
                            act(rl.t[:], ps.t[:], AF.Relu, [ps.k], [rl.k])
                            tt(hid[:, f, :], rl.t[:], rl.t[:], ALU.mult, [rl.k], [("mb", f)], eng="pool")
                    for jo in range(NJ):
                        if jo % 4 == 0:
                            wv2, wk2 = next_piece()
                        ps = PS.get()
                        for f in range(HG):
                            mm(ps.t[:], wv2[:, f, (jo % 4) * 128:(jo % 4 + 1) * 128], hid[:, f, :], f == 0, f == HG - 1,
                               [wk2, ("mb", f)], [ps.k])
                        tt(xT[:, jo, :], xT[:, jo, :], ps.t[:], ALU.add, [XK(jo), ps.k], [XK(jo)])
                dbg("x_mlp", xT[:, 0, :], XK(0), l == 0 and i == 0)
                for q in range(2):
                    S.dma(snd_t[q].ap().rearrange("p (c t) -> p c t", c=4), xT[:, 4 * q:4 * q + 4, :],
                          reads=[XK(j) for j in range(4 * q, 4 * q + 4)],
                          writes=[("snd", j) for j in range(4 * q, 4 * q + 4)], semkey=("xT_st", q))
                ps = PS.get()
                for j in range(NJ):
                    sq = SQ.get()
                    act(sq.t[:], xT[:, j, :], AF.Square, [XK(j)], [sq.k])
                    mm(ps.t[:], ones_b, sq.t[:], j == 0, j == NJ - 1, [sq.k, "cstb"], [ps.k])
                rstd = TMP.get()
                act(rstd.t[:], ps.t[:], AF.Ln, [ps.k, "cst"], [rstd.k], bias=C("eps", 1), scale=1.0 / D)
                act(rstd.t[:], rstd.t[:], AF.Exp, [rstd.k], [rstd.k], scale=-0.5)
                for j in range(NJ):
                    g = vecs[:, NVEC + j:NVEC + j + 1]
                    o = TMP.get()
                    stt(o.t[:], xT[:, j, :], g, rstd.t[:], ALU.mult, ALU.mult, [XK(j), rstd.k, "vecs"], [o.k])
                    S.dma(out_d[:, j, t0:t0 + T], o.t[:], reads=[o.k], writes=[("od", i, j)], semkey=o.k)
                if i < NT - 1:
                    allgather(0)
                    allgather(1)
        S.final_wait([("od", i, j) for i in range(NT) for j in range(NJ)] + [("dbg", n) for n in range(len(DBG_NAMES))])
        with nc.Block() as block:
            S.emit(block)
        print("program: ops=%d sems=%d sbuf_left=%d" % (S.nops, len(S.handles), nc.sbuf_bytes_remaining))
    return nc


_PROG_CACHE = {}


def run_model(inputs, S_total, DEPTH, debug=False):
    assert DEPTH == NSTAGE
    x = np.asarray(inputs["x"], np.float32)
    B = x.shape[0]
    assert B == NGROUP
    NTS = S_total // T
    NTICK = NTS + NSTAGE - 1
    FOFF, NFLAG = flag_layout(NTICK)
    perm = _cperm()
    cst = _consts()
    nc = build_program(NTICK, debug=debug)
    lw = [_pack_layer_w(inputs, l, perm)[None] for l in range(DEPTH)]
    lv = [_pack_layer_vecs(inputs, l, DEPTH) for l in range(DEPTH)]
    fg = _fm(inputs["final_g"])
    in_maps = []
    for c in range(NGROUP * NSTAGE):
        b, st = c // NSTAGE, c % NSTAGE
        xfm = x[b].reshape(S_total, NJ, 128).transpose(2, 1, 0)
        reps = -(-NTICK * T // S_total)
        xb = np.ascontiguousarray(np.concatenate([xfm] * reps, axis=2)[:, :, 0:NTICK * T])
        fl = np.zeros((128, NFLAG), np.float32)
        for k in range(NTICK):
            own = (st == 0) or (k < st)
            fl[:, FOFF["selx"] + k] = 1.0 if own else 0.0
            if not own:
                fl[:, FOFF["sel"] + (st - 1) * NTICK + k] = 1.0
        if st > 0:
            fl[:, FOFF["vres"]] = 1.0
        fl[:, FOFF["keep"]:FOFF["keep"] + NTICK] = 1.0
        fl[:, FOFF["keep"] + st] = 0.0
        fl[:, FOFF["first"] + st] = 1.0
        vec_all = np.concatenate([lv[st], fg, fl], axis=1)
        in_maps.append({"x": xb, "vecs": np.ascontiguousarray(vec_all), "wts": lw[st], "consts": cst})
    res = run_bass_kernel_spmd(nc, in_maps, core_ids=list(range(NGROUP * NSTAGE)))
    _PROG_CACHE["res"] = res if debug else None
    outs = []
    for b in range(B):
        o = np.asarray(res.results[b * NSTAGE + NSTAGE - 1]["out"])
        o = o[:, :, (NSTAGE - 1) * T:(NSTAGE - 1) * T + S_total]
        outs.append(o.transpose(2, 1, 0).reshape(S_total, D))
    if debug:
        return np.stack(outs, axis=0).astype(np.float32), {n: np.asarray(res.results[0]["dbg"])[i]
                                                           for i, n in enumerate(DBG_NAMES)}
    return np.stack(outs, axis=0).astype(np.float32)


def kernel(**inputs):
    return run_model(inputs, 16384, 4)
```

```python
import numpy as np
from contextlib import ExitStack
import concourse.bass as bass
import concourse.mybir as mybir
from concourse.bass_utils import run_bass_kernel_spmd

F32 = mybir.dt.float32
BF16 = mybir.dt.bfloat16
AF = mybir.ActivationFunctionType
ALU = mybir.AluOpType

D = 1024
NJ = 8
T = 512
CH = 64
NCH = T // CH
N_IN = 11520
D_FF = 4096
EPS = 1e-6
LNX_EPS = 64 * 1e-5
DECAY_C = 0.6065306597126334
GELU_C = 1.5957691216057308

_VNAMES = [("g1", 8), ("g2n", 8), ("mbA", 8), ("mbB", 8), ("mbC", 8), ("cA0", 8), ("cA1", 8), ("cA2", 8),
           ("cB0", 8), ("cB1", 8), ("cB2", 8), ("cB3", 8), ("cBb", 8), ("ba", 8), ("bi", 8), ("ap", 8),
           ("mu_lr", 2), ("mu_r", 8), ("mu_k", 8), ("mu_v", 8), ("w0", 8), ("a0", 8), ("kkv", 8), ("ka", 8),
           ("rk", 8), ("lg", 8), ("lb", 8), ("v0", 8)]
VOFF = {}
_o = 0
for _n, _w in _VNAMES:
    VOFF[_n] = _o
    _o += _w
NVEC = _o

WOFF = {}
_o = 0
for _n, _w in [("win", 8 * N_IN), ("wout", 8 * D), ("w1", 8 * D_FF), ("w2", 32 * D), ("w2a2", D), ("g2", D),
               ("v1", 8 * 32), ("v2", D), ("wa", 8 * 128), ("wi", 8 * 128)]:
    WOFF[_n] = _o
    _o += _w
WTOT = _o
CVT = 4096

COFF = {}
_o = 0
for _n, _w in [("ident", 128), ("ones_bd", 128), ("mSU", 128), ("mSL", 128), ("mIU", 64), ("cmask", 512),
               ("hmask", 2), ("nhmask", 2), ("ones", 128), ("eps", 1), ("lnxeps", 1)]:
    COFF[_n] = _o
    _o += _w
NCONST = _o


def _cperm():
    c0 = 8192
    perm = list(range(c0 + 3072, c0 + 3328))
    for j in range(NJ):
        s = slice(j * 128, (j + 1) * 128)
        for base in (0, 1024, 2048, 3072, 4096, 5120, 6144, 7168, c0, c0 + 1024, c0 + 2048):
            perm += list(range(base + s.start, base + s.stop))
    return np.array(perm, dtype=np.int64)


def _fm(v):
    return np.ascontiguousarray(np.asarray(v, np.float32).reshape(-1, 128).T)


def _pack_layer_vecs(inp, l, L):
    out = np.zeros((128, NVEC), np.float32)

    def put(name, arr):
        a = _fm(arr)
        out[:, VOFF[name]:VOFF[name] + a.shape[1]] = a

    put("g1", inp["norm1_g"][l]); put("g2n", inp["norm2_g"][l])
    mb = inp["merge_b"][l]
    put("mbA", mb[0:1024]); put("mbB", mb[1024:2048]); put("mbC", mb[2048:3072])
    for t in range(3):
        put("cA%d" % t, inp["conv_a_w"][l, t])
    for t in range(4):
        put("cB%d" % t, inp["lru_conv_w"][l, t])
    put("cBb", inp["lru_conv_b"][l]); put("ba", inp["lru_ba"][l]); put("bi", inp["lru_bi"][l])
    put("ap", inp["lru_a_param"][l])
    mu = inp["rwkv_mu"][l]
    put("mu_lr", mu[3072:3328]); put("mu_r", mu[0:1024]); put("mu_k", mu[1024:2048]); put("mu_v", mu[2048:3072])
    put("w0", inp["rwkv_w0"][l]); put("a0", inp["rwkv_a0"][l]); put("kkv", inp["rwkv_kk"][l])
    put("ka", inp["rwkv_ka"][l]); put("rk", inp["rwkv_rk"][l].reshape(-1))
    put("lg", inp["rwkv_lnx_g"][l]); put("lb", inp["rwkv_lnx_b"][l])
    if l > 0:
        put("v0", inp["rwkv_v0"][l - 1])
    return out


def _pack_layer_w(inp, l, perm):
    out = np.zeros((128, WTOT), np.float32)

    def put(name, a):
        a = np.asarray(a, np.float32).reshape(a.shape[0], -1)
        out[:a.shape[0], WOFF[name]:WOFF[name] + a.shape[1]] = a

    def kc(w):
        K, N = w.shape
        return np.ascontiguousarray(np.asarray(w, np.float32).reshape(K // 128, 128, N).transpose(1, 0, 2))

    put("win", kc(inp["w_in"][l][:, perm]))
    put("wout", kc(inp["w_out"][l]))
    put("w1", kc(inp["mlp_w1"][l]))
    put("w2", kc(inp["mlp_w2"][l]))
    put("w2a2", np.concatenate([inp["rwkv_w2"][l], inp["rwkv_a2"][l]], axis=0))
    put("g2", inp["rwkv_g2"][l])
    if l > 0:
        put("v1", kc(inp["rwkv_v1"][l - 1]))
        put("v2", inp["rwkv_v2"][l - 1])
    for nm, key in (("wa", "lru_wa"), ("wi", "lru_wi")):
        bd = np.zeros((128, 8, 128), np.float32)
        w = np.asarray(inp[key][l], np.float32)
        for j in range(8):
            bd[0:64, j, 0:64] = w[2 * j]
            bd[64:128, j, 64:128] = w[2 * j + 1]
        put(nm, bd)
    return out


def _consts():
    c = np.zeros((128, NCONST), np.float32)
    p = np.arange(128)
    hp, sp = p // 64, p % 64
    c[:, COFF["ident"]:COFF["ident"] + 128] = np.eye(128)
    same = (hp[:, None] == hp[None, :])
    c[:, COFF["ones_bd"]:COFF["ones_bd"] + 128] = same
    c[:, COFF["mSU"]:COFF["mSU"] + 128] = same & (sp[:, None] < sp[None, :])
    c[:, COFF["mSL"]:COFF["mSL"] + 128] = same & (sp[:, None] > sp[None, :])
    c[:, COFF["mIU"]:COFF["mIU"] + 64] = (sp[:, None] <= np.arange(64)[None, :])
    c[:, COFF["cmask"]:COFF["cmask"] + 512] = (np.arange(512) % 64 != 0)[None, :]
    for hh in range(2):
        c[:, COFF["hmask"] + hh] = (hp == hh)
        c[:, COFF["nhmask"] + hh] = -(hp == hh).astype(np.float32)
    c[:, COFF["ones"]:COFF["ones"] + 128] = 1.0
    c[:, COFF["eps"]] = EPS
    c[:, COFF["lnxeps"]] = LNX_EPS
    return c


class _Ctr:
    def __init__(self, S, name):
        self.S, self.name, self.gen = S, name, 0
        self.sid = S.new_sem(name)
        self.v = 0

    def bump(self, inc):
        if self.v + inc > 30000:
            self.gen += 1
            self.sid = self.S.new_sem("%s_%d" % (self.name, self.gen))
            self.v = 0
        self.v += inc
        return self.sid, self.v


class Sched:
    ENG = ("pe", "act", "dve", "pool", "sp")

    def __init__(self, nc, stack):
        self.nc, self.stack = nc, stack
        self.handles = []
        self.streams = {e: [] for e in self.ENG}
        self.ctr = {e: _Ctr(self, "e_" + e) for e in self.ENG if e != "sp"}
        self.waited = {e: {} for e in self.ENG}
        self.lw, self.rd, self.dctr = {}, {}, {}
        self.nops = 0

    def new_sem(self, name):
        h = self.stack.enter_context(self.nc.semaphore(name))
        self.handles.append(h)
        return len(self.handles) - 1

    def _deps(self, eng, reads, writes):
        need = {}

        def add(ev):
            sid, val, src = ev
            if self.waited[eng].get(sid, 0) >= val:
                return
            if need.get(sid, 0) < val:
                need[sid] = val

        for k in reads:
            ev = self.lw.get(k)
            if ev is not None and not (eng == "pe" and ev[2] == "pe"):
                add(ev)
        for k in writes:
            ev = self.lw.get(k)
            if ev is not None and ev[2] != eng:
                add(ev)
            for sid, (val, src) in self.rd.get(k, {}).items():
                if src != eng:
                    add((sid, val, src))
        for sid, val in need.items():
            self.waited[eng][sid] = val
            self.streams[eng].append(("w", sid, val))

    def _record(self, me, reads, writes):
        for k in writes:
            self.lw[k] = me
            self.rd[k] = {}
        for k in reads:
            d = self.rd.setdefault(k, {})
            if d.get(me[0], (0, None))[0] < me[1]:
                d[me[0]] = (me[1], me[2])

    def op(self, eng, fn, reads=(), writes=()):
        self._deps(eng, reads, writes)
        sid, val = self.ctr[eng].bump(1)
        self.streams[eng].append(("o", fn, sid))
        self._record((sid, val, eng), reads, writes)
        self.nops += 1

    def dma(self, out_ap, in_ap, reads=(), writes=(), semkey=None, q="sp"):
        self._deps(q, reads, writes)
        c = self.dctr.get(semkey)
        if c is None:
            c = self.dctr[semkey] = _Ctr(self, "d_" + str(semkey))
        sid, val = c.bump(16)
        self.streams[q].append(("d", out_ap, in_ap, sid))
        self._record((sid, val, "dma"), reads, writes)
        self.nops += 1

    def cc(self, fn, reads=(), writes=()):
        self._deps("pool", reads, writes)
        c = self.dctr.get("cc")
        if c is None:
            c = self.dctr["cc"] = _Ctr(self, "d_cc")
        sid, val = c.bump(1)
        self.streams["pool"].append(("o", fn, sid))
        self._record((sid, val, "dma"), reads, writes)
        self.nops += 1

    def barrier(self):
        evs = []
        for e, c in self.ctr.items():
            if c.v > 0:
                evs.append((c.sid, c.v))
        for c in self.dctr.values():
            if c.v > 0:
                evs.append((c.sid, c.v))
        for eng in self.ENG:
            for sid, val in evs:
                if self.waited[eng].get(sid, 0) < val:
                    self.waited[eng][sid] = val
                    self.streams[eng].append(("w", sid, val))

    def final_wait(self, keys):
        self._deps("sp", list(keys), [])

    def emit(self, block):
        H = self.handles

        def mk(name):
            items = self.streams[name]

            def body(e):
                for it in items:
                    if it[0] == "w":
                        e.wait_ge(H[it[1]], it[2])
                    elif it[0] == "o":
                        it[1](e).then_inc(H[it[2]], 1)
                    else:
                        e.dma_start(out=it[1], in_=it[2]).then_inc(H[it[3]], 16)
            return body

        block.tensor(mk("pe"))
        block.scalar(mk("act"))
        block.vector(mk("dve"))
        block.gpsimd(mk("pool"))
        block.sync(mk("sp"))


class Buf:
    def __init__(self, t, k):
        self.t, self.k = t, k


class Pool:
    def __init__(self, bufs, name):
        self.bufs = bufs
        self.name = name
        self.i = 0

    def get(self):
        b = self.bufs[self.i % len(self.bufs)]
        self.i += 1
        return b


DBG_NAMES = []


NSTAGE = 4
NGROUP = 2


def flag_layout(NTICK):
    off = {"selx": 0, "sel": NTICK, "vres": (1 + NSTAGE) * NTICK, "keep": (1 + NSTAGE) * NTICK + 1,
           "first": (2 + NSTAGE) * NTICK + 1}
    return off, (3 + NSTAGE) * NTICK + 1


def build_program(NT, debug=False):
    DEPTH = 1
    NTOK = NT * T
    FOFF, NFLAG = flag_layout(NT)
    NV_ALL = NVEC + 8 + NFLAG
    nc = bass.Bass("TRN2", target_bir_lowering=False)
    x_in = nc.dram_tensor("x", [128, NJ, NTOK], F32, kind="ExternalInput").ap()
    vecs_d = nc.dram_tensor("vecs", [128, NV_ALL], F32, kind="ExternalInput").ap()
    wts_d = nc.dram_tensor("wts", [DEPTH, 128, WTOT], F32, kind="ExternalInput").ap()
    cst_d = nc.dram_tensor("consts", [128, NCONST], F32, kind="ExternalInput").ap()
    out_d = nc.dram_tensor("out", [128, NJ, NTOK], F32, kind="ExternalOutput").ap()
    snd_t = [nc.dram_tensor("snd%d" % q, [128, 4 * T], F32) for q in range(4)]
    rcv_t = [nc.dram_tensor("rcv%d" % q, [NSTAGE * 128, 4 * T], F32) for q in range(4)]

    def snd_c(c):
        return snd_t[c // 4].ap().rearrange("p (c t) -> p c t", c=4)[:, c % 4, :]

    def rcv_c(c):
        return rcv_t[c // 4].ap().rearrange("(r p) (c t) -> p r c t", r=NSTAGE, c=4)[:, :, c % 4, :]
    wb_d = nc.dram_tensor("wb", [DEPTH, 128, WTOT], BF16).ap()
    dbg_d = nc.dram_tensor("dbg", [24, 128, T], F32, kind="ExternalOutput").ap() if debug else None
    del DBG_NAMES[:]

    with ExitStack() as stack:
        S = Sched(nc, stack)

        def sb(name, shape, dt):
            return stack.enter_context(nc.sbuf_tensor("s_" + name, shape, dt))

        def mkpool(name, n, shape, dt):
            return Pool([Buf(sb("%s%d" % (name, i), shape, dt), (name, i)) for i in range(n)], name)

        def act(out, in_, func, reads, writes, bias=None, scale=None):
            kw = {}
            if bias is not None:
                kw["bias"] = bias
            if scale is not None:
                kw["scale"] = scale
            S.op("act", lambda e: e.activation(out=out, in_=in_, func=func, **kw), reads, writes)

        def tt(out, a, b, op, reads, writes, eng="dve"):
            S.op(eng, lambda e: e.tensor_tensor(out=out, in0=a, in1=b, op=op), reads, writes)

        def ts(out, a, s1, s2, op0, op1, reads, writes, eng="dve"):
            eng = "dve"
            if op1 is None:
                S.op(eng, lambda e: e.tensor_scalar(out=out, in0=a, scalar1=s1, scalar2=None, op0=op0), reads, writes)
            else:
                S.op(eng, lambda e: e.tensor_scalar(out=out, in0=a, scalar1=s1, scalar2=s2, op0=op0, op1=op1),
                     reads, writes)

        def stt(out, a, s, b, op0, op1, reads, writes, eng="dve"):
            eng = "dve"
            S.op(eng, lambda e: e.scalar_tensor_tensor(out=out, in0=a, scalar=s, in1=b, op0=op0, op1=op1),
                 reads, writes)

        def cp(out, in_, reads, writes, eng="dve"):
            S.op(eng, lambda e: e.tensor_copy(out=out, in_=in_), reads, writes)

        def rcp(out, in_, reads, writes):
            S.op("dve", lambda e: e.reciprocal(out=out, in_=in_), reads, writes)

        def mm(out, lhsT, rhs, start, stop, reads, writes):
            S.op("pe", lambda e: e.matmul(out, lhsT=lhsT, rhs=rhs, start=start, stop=stop), reads, writes)

        def tr(out, in_, ident, reads, writes):
            S.op("pe", lambda e: e.transpose(out, in_, ident), reads, writes)

        def dbg(name, ap, key, cond=True):
            if debug and cond:
                n = len(DBG_NAMES)
                DBG_NAMES.append(name)
                S.dma(dbg_d[n], ap, reads=[key], writes=[("dbg", n)], semkey=("dbg", n))

        cst = sb("cst", [128, NCONST], F32)
        cstb = sb("cstb", [128, 384], BF16)
        vecs = sb("vecs", [128, NV_ALL], F32)
        der = sb("der", [128, DEPTH * 16], F32)
        S.dma(cst[:], cst_d, writes=["cst"], semkey="cst")
        S.dma(vecs[:], vecs_d, writes=["vecs"], semkey="vecs")
        cp(cstb[:, 0:256], cst[:, COFF["ident"]:COFF["ident"] + 256], ["cst"], ["cstb"])
        cp(cstb[:, 256:384], cst[:, COFF["ones"]:COFF["ones"] + 128], ["cst"], ["cstb"])
        ident_b = cstb[:, 0:128]
        onesbd_b = cstb[:, 128:256]
        ones_b = cstb[:, 256:384]

        def C(name, w):
            return cst[:, COFF[name]:COFF[name] + w]

        def FL(name, k=0):
            o = NVEC + 8 + FOFF[name] + k
            return vecs[:, o:o + 1]

        def V(l, name, j):
            o = l * NVEC + VOFF[name] + j
            return vecs[:, o:o + 1]

        with ExitStack() as pstack:
            stg = [pstack.enter_context(nc.sbuf_tensor("stg%d" % i, [128, CVT], F32)) for i in range(3)]
            stb = [pstack.enter_context(nc.sbuf_tensor("stb%d" % i, [128, CVT], BF16)) for i in range(3)]
            n = 0
            for l in range(DEPTH):
                o = l * 16
                act(der[:, o:o + 8], vecs[:, l * NVEC + VOFF["ap"]:l * NVEC + VOFF["ap"] + 8], AF.Exp,
                    ["vecs"], ["der"])
                ts(der[:, o:o + 8], der[:, o:o + 8], 1.0, None, ALU.add, None, ["der"], ["der"])
                act(der[:, o:o + 8], der[:, o:o + 8], AF.Ln, ["der"], ["der"])
                ts(der[:, o:o + 8], der[:, o:o + 8], -8.0, None, ALU.mult, None, ["der"], ["der"])
                ts(der[:, o + 8:o + 16], vecs[:, l * NVEC + VOFF["ka"]:l * NVEC + VOFF["ka"] + 8], -1.0, 1.0,
                   ALU.mult, ALU.add, ["vecs"], ["der"])
                for c0 in range(0, WTOT, CVT):
                    w = min(CVT, WTOT - c0)
                    s = n % 3
                    S.dma(stg[s][:, 0:w], wts_d[l][:, c0:c0 + w], writes=[("stg", s)], semkey=("stg", s))
                    if n % 2 == 0:
                        cp(stb[s][:, 0:w], stg[s][:, 0:w], [("stg", s)], [("stb", s)])
                    else:
                        act(stb[s][:, 0:w], stg[s][:, 0:w], AF.Copy, [("stg", s)], [("stb", s)])
                    S.dma(wb_d[l][:, c0:c0 + w], stb[s][:, 0:w], reads=[("stb", s)], writes=["wbd"],
                          semkey=("stb", s))
                    n += 1
        S.barrier()

        xT = sb("xT", [128, NJ, T], F32)
        hT = sb("hT", [128, NJ, T], BF16)
        mb = sb("mb", [128, NJ, T], BF16)
        HG = 8
        hid = mb
        WB = mkpool("wb", 4, [128, 4096], BF16)
        wa_s = sb("wa_s", [128, 8, 128], BF16)
        wi_s = sb("wi_s", [128, 8, 128], BF16)
        w2a2_s = sb("w2a2_s", [128, D], BF16)
        g2_s = sb("g2_s", [128, D], BF16)
        v1_s = sb("v1_s", [128, 8, 32], BF16)
        v2_s = sb("v2_s", [128, D], BF16)
        TMP = mkpool("tmp", 9, [128, T], F32)
        PB = mkpool("pb", 3, [128, T + 3], F32)
        SQ = mkpool("sq", 2, [128, T], BF16)
        TB = mkpool("tb", 3, [128, T], BF16)
        _psb = [Buf(stack.enter_context(nc.psum_tensor("ps%d" % i, [128, 512], F32)), ("ps", i))
                for i in range(8)]
        PS = Pool(_psb[0:5], "ps")
        PQ = Pool(_psb[5:7], "pq")
        PSY = _psb[7]
        cTS = sb("cTS", [128, 26], F32)
        cA = sb("cA", [128, NJ, 2], F32)
        cB = sb("cB", [128, NJ, 3], F32)
        cH = sb("cH", [128, NJ], F32)
        Hf = sb("Hf", [128, NJ, 128], F32)
        Hb = sb("Hb", [128, NJ, 128], BF16)
        xwa = sb("xwa", [128, T], BF16)
        sgx = sb("sgx", [128, T], BF16)
        hv1 = sb("hv1", [32, T], BF16)
        RW = {nm: sb("rw_" + nm, [128, T], F32) for nm in
              ("r", "k", "v", "kk", "b", "ac", "sw", "cum", "epos", "eneg")}
        BD = {nm: sb("bd_" + nm, [128, NCH, 128], BF16) for nm in ("b", "k", "h", "v")}
        BDA = [sb("bd_a%d" % p, [128, NCH, 128], BF16) for p in range(2)]
        rtP = [sb("rt%d" % p, [128, T], BF16) for p in range(2)]
        BhTP = [sb("BhT%d" % p, [128, NCH, 128], BF16) for p in range(2)]
        KhTP = [sb("KhT%d" % p, [128, NCH, 128], BF16) for p in range(2)]
        VTP = [sb("VT%d" % p, [128, NCH, 128], BF16) for p in range(2)]
        GP = [sb("G%d" % p, [128, T], F32) for p in range(2)]
        BpP = [sb("Bp%d" % p, [128, T], F32) for p in range(2)]
        gamP = [sb("gam%d" % p, [128, NCH], F32) for p in range(2)]
        Nq = [[sb("Nq%d_%d" % (g, i), [128, 4, 128], BF16) for i in range(2)] for g in range(2)]
        Mq = [[sb("Mq%d_%d" % (g, i), [128, 4, 128], BF16) for i in range(2)] for g in range(2)]
        Xq = [[sb("Xq%d_%d" % (g, i), [128, 4, 128], BF16) for i in range(2)] for g in range(2)]
        XfP = [sb("Xf%d" % p, [128, NCH, 128], BF16) for p in range(2)]
        AakTP = [sb("AakT%d" % p, [128, NCH, 128], BF16) for p in range(2)]
        ArbTP = [sb("ArbT%d" % p, [128, NCH, 64], BF16) for p in range(2)]
        ArkTP = [sb("ArkT%d" % p, [128, NCH, 64], BF16) for p in range(2)]
        WU = mkpool("wu", 4, [128, 128], BF16)
        vft = sb("vft", [128, T], F32)
        st4 = sb("st4", [128, NSTAGE - 1, T], F32)
        sm1 = sb("sm1", [128, NJ], F32)

        pieces = []
        for l in range(DEPTH):
            win3 = wb_d[l][:, WOFF["win"]:WOFF["win"] + 8 * N_IN].rearrange("p (k c) -> p k c", k=8)
            wout3 = wb_d[l][:, WOFF["wout"]:WOFF["wout"] + 8 * D].rearrange("p (k c) -> p k c", k=8)
            w13 = wb_d[l][:, WOFF["w1"]:WOFF["w1"] + 8 * D_FF].rearrange("p (k c) -> p k c", k=8)
            w23 = wb_d[l][:, WOFF["w2"]:WOFF["w2"] + 32 * D].rearrange("p (k c) -> p k c", k=32)
            for i in range(NT):
                pieces.append((win3[:, :, 0:256], (8, 256)))
                for j in range(NJ):
                    c0 = 256 + j * 1408
                    pieces.append((win3[:, :, c0:c0 + 384], (8, 384)))
                    pieces.append((win3[:, :, c0 + 384:c0 + 768], (8, 384)))
                    pieces.append((win3[:, :, c0 + 768:c0 + 1152], (8, 384)))
                    pieces.append((win3[:, :, c0 + 1152:c0 + 1408], (8, 256)))
                for h in range(2):
                    pieces.append((wout3[:, :, h * 512:(h + 1) * 512], (8, 512)))
                for q in range(4):
                    for h in range(2):
                        pieces.append((w13[:, :, q * 1024 + h * 512:q * 1024 + (h + 1) * 512], (8, 512)))
                    for h in range(2):
                        pieces.append((w23[:, q * 8:(q + 1) * 8, h * 512:(h + 1) * 512], (8, 512)))
        pstate = {"issued": 0, "next": 0, "slots": {}}

        def _issue(n):
            src, (k, c) = pieces[n]
            b = WB.get()
            view = b.t[:, 0:k * c].rearrange("p (k c) -> p k c", k=k)
            S.dma(view, src, reads=["wbd"], writes=[b.k], semkey=b.k)
            pstate["slots"][n] = (view, b.k)

        def next_piece():
            n = pstate["next"]
            pstate["next"] += 1
            while pstate["issued"] < min(len(pieces), n + 4):
                _issue(pstate["issued"])
                pstate["issued"] += 1
            return pstate["slots"].pop(n)

        def XK(j):
            return ("xT", j)

        def allgather(q):
            S.cc(lambda e: e.collective_compute(
                "AllGather", ALU.bypass,
                replica_groups=[list(range(g * NSTAGE, (g + 1) * NSTAGE)) for g in range(NGROUP)],
                ins=[snd_t[q].ap().opt()], outs=[rcv_t[q].ap().opt()]),
                reads=[("snd", c) for c in range(4 * q, 4 * q + 4)], writes=[("rcv", q)])

        def proj(wv, wk, cidx, rhsT, rkeys):
            ps = PS.get()
            for kc in range(8):
                mm(ps.t[:], wv[:, kc, cidx * 128:(cidx + 1) * 128], rhsT[:, kc, :], kc == 0, kc == 7,
                   [wk] + rkeys, [ps.k])
            return ps

        def rmsnorm_to(l, gname, dst, dkeyname):
            ps = PS.get()
            for j in range(NJ):
                sq = SQ.get()
                act(sq.t[:], xT[:, j, :], AF.Square, [XK(j)], [sq.k])
                mm(ps.t[:], ones_b, sq.t[:], j == 0, j == NJ - 1, [sq.k, "cstb"], [ps.k])
            rstd = TMP.get()
            act(rstd.t[:], ps.t[:], AF.Ln, [ps.k, "cst"], [rstd.k], bias=C("eps", 1), scale=1.0 / D)
            act(rstd.t[:], rstd.t[:], AF.Exp, [rstd.k], [rstd.k], scale=-0.5)
            for j in range(NJ):
                if gname == "final":
                    g = vecs[:, DEPTH * NVEC + j:DEPTH * NVEC + j + 1]
                else:
                    g = V(l, gname, j)
                stt(dst[:, j, :], xT[:, j, :], g, rstd.t[:], ALU.mult, ALU.mult,
                    [XK(j), rstd.k, "vecs"], [(dkeyname, j)])

        def tshift(ps, cidx, mu_ap, out_ap, okeys):
            pb = PB.get()
            cp(pb.t[:, 0:1], cTS[:, cidx:cidx + 1], [("cTS", cidx)], [pb.k])
            act(pb.t[:, 1:T + 1], ps.t[:], AF.Copy, [ps.k], [pb.k])
            cp(cTS[:, cidx:cidx + 1], pb.t[:, T:T + 1], [pb.k], [("cTS", cidx)])
            d = TMP.get()
            tt(d.t[:], pb.t[:, 0:T], pb.t[:, 1:T + 1], ALU.subtract, [pb.k], [d.k], eng="pool")
            stt(out_ap, d.t[:], mu_ap, pb.t[:, 1:T + 1], ALU.mult, ALU.add, [d.k, pb.k, "vecs"], okeys, eng="pool")

        for l in range(DEPTH):
            for buf, key in ((cTS, None), (cA, None), (cB, None), (cH, None), (Hf, None), (Hb, None)):
                pass
            S.op("dve", lambda e: e.memset(cTS[:], 0.0), [], [("cTS", c) for c in range(26)])
            S.op("dve", lambda e: e.memset(cA[:], 0.0), [], [("cA", j) for j in range(NJ)])
            S.op("dve", lambda e: e.memset(cB[:], 0.0), [], [("cB", j) for j in range(NJ)])
            S.op("dve", lambda e: e.memset(cH[:], 0.0), [], [("cH", j) for j in range(NJ)])
            S.op("dve", lambda e: e.memset(Hf[:], 0.0), [], [("Hf", j) for j in range(NJ)])
            S.op("dve", lambda e: e.memset(Hb[:], 0.0), [], [("Hb", j) for j in range(NJ)])
            wl = wb_d[l]
            S.dma(wa_s[:], wl[:, WOFF["wa"]:WOFF["wa"] + 1024].rearrange("p (k c) -> p k c", k=8), ["wbd"],
                  ["wa_s"], semkey="wa_s")
            S.dma(wi_s[:], wl[:, WOFF["wi"]:WOFF["wi"] + 1024].rearrange("p (k c) -> p k c", k=8), ["wbd"],
                  ["wi_s"], semkey="wi_s")
            S.dma(w2a2_s[:], wl[:, WOFF["w2a2"]:WOFF["w2a2"] + D], ["wbd"], ["w2a2_s"], semkey="w2a2_s")
            S.dma(g2_s[:], wl[:, WOFF["g2"]:WOFF["g2"] + D], ["wbd"], ["g2_s"], semkey="g2_s")
            if True:
                S.dma(v1_s[:], wl[:, WOFF["v1"]:WOFF["v1"] + 256].rearrange("p (k c) -> p k c", k=8), ["wbd"],
                      ["v1_s"], semkey="v1_s")
                S.dma(v2_s[:], wl[:, WOFF["v2"]:WOFF["v2"] + D], ["wbd"], ["v2_s"], semkey="v2_s")
            sp8 = lambda j, l=l: der[:, l * 16 + j:l * 16 + j + 1]
            omka = lambda j, l=l: der[:, l * 16 + 8 + j:l * 16 + 8 + j + 1]
            HK = ["hT%d" % 0]
            hkeys = [("hT", j) for j in range(NJ)]

            for i in range(NT):
                t0 = i * T
                keep = FL("keep", i)
                for stt_, kn, n_ in ((cTS[:], "cTS", 26), (cA[:].rearrange("p a b -> p (a b)"), "cA", NJ),
                                     (cB[:].rearrange("p a b -> p (a b)"), "cB", NJ), (cH[:], "cH", NJ),
                                     (Hf[:].rearrange("p a b -> p (a b)"), "Hf", NJ),
                                     (Hb[:].rearrange("p a b -> p (a b)"), "Hb", NJ)):
                    ks_ = [(kn, c) for c in range(n_)]
                    ts(stt_, stt_, keep, None, ALU.mult, None, ks_ + ["vecs"], ks_)
                for j in range(NJ):
                    xi = TMP.get()
                    S.dma(xi.t[:], x_in[:, j, t0:t0 + T], writes=[xi.k], semkey=xi.k)
                    ts(xT[:, j, :], xi.t[:], FL("selx", i), None, ALU.mult, None, [xi.k, "vecs"], [XK(j)])
                    if i > 0:
                        S.dma(st4[:], rcv_c(j)[:, 0:NSTAGE - 1, :], reads=[("rcv", j // 4)], writes=["st4"], semkey="st4")
                        for r in range(NSTAGE - 1):
                            stt(xT[:, j, :], st4[:, r, :], FL("sel", r * NT + i), xT[:, j, :], ALU.mult, ALU.add,
                                ["st4", XK(j), "vecs"], [XK(j)], eng="pool")
                rmsnorm_to(l, "g1", hT, "hT")

                wv, wk = next_piece()
                ps = proj(wv, wk, 0, hT, hkeys)
                xs_t = TMP.get()
                tshift(ps, 0, V(l, "mu_lr", 0), xs_t.t[:], [xs_t.k])
                act(xwa[0:64, :], xs_t.t[0:64, :], AF.Tanh, [xs_t.k], ["xwa"])
                act(xwa[64:128, :], xs_t.t[64:128, :], AF.Copy, [xs_t.k], ["xwa"])
                ps = proj(wv, wk, 1, hT, hkeys)
                xg_t = TMP.get()
                tshift(ps, 1, V(l, "mu_lr", 1), xg_t.t[:], [xg_t.k])
                act(sgx[:], xg_t.t[:], AF.Sigmoid, [xg_t.k], ["sgx"])
                if True:
                    ps = PS.get()
                    for kc in range(8):
                        mm(ps.t[0:32, :], v1_s[:, kc, :], hT[:, kc, :], kc == 0, kc == 7, ["v1_s"] + hkeys, [ps.k])
                    act(hv1[:], ps.t[0:32, :], AF.Copy, [ps.k], ["hv1"])

                PL = "pool"

                def v3(ap):
                    return ap.rearrange("p (c t) -> p c t", c=NCH)

                def pv4(ps):
                    return ps.t[:].rearrange("p (c t) -> p c t", c=4)

                mSU4 = C("mSU", 128).unsqueeze(1).broadcast_to([128, 4, 128])
                mSL4 = C("mSL", 128).unsqueeze(1).broadcast_to([128, 4, 128])
                idn4 = C("ident", 128).unsqueeze(1).broadcast_to([128, 4, 128])
                mIU8 = C("mIU", 64).unsqueeze(1).broadcast_to([128, NCH, 64])

                def prep(j):
                    p = j % 2
                    js = slice(j * 128, (j + 1) * 128)
                    bd_a, rt, BhT, KhT, VT = BDA[p], rtP[p], BhTP[p], KhTP[p], VTP[p]
                    Xf, AakT, ArbT, ArkT = XfP[p], AakTP[p], ArbTP[p], ArkTP[p]
                    G, Bp, gam = GP[p], BpP[p], gamP[p]

                    def K(nm):
                        return (nm, p)
                    wvA, wkA = next_piece()
                    psB = proj(wvA, wkA, 0, hT, hkeys)
                    psC = proj(wvA, wkA, 1, hT, hkeys)
                    psX = proj(wvA, wkA, 2, hT, hkeys)
                    csb = TMP.get()
                    act(csb.t[:], psC.t[:], AF.Copy, [psC.k], [csb.k])
                    cx = PB.get()
                    cp(cx.t[:, 0:2], cA[:, j, :], [("cA", j)], [cx.k])
                    tt(cx.t[:, 2:T + 2], csb.t[:], psX.t[:], ALU.mult, [csb.k, psX.k], [cx.k])
                    cp(cA[:, j, :], cx.t[:, T:T + 2], [cx.k], [("cA", j)])
                    acc = TMP.get()
                    ts(acc.t[:], cx.t[:, 2:T + 2], V(l, "cA2", j), None, ALU.mult, None, [cx.k, "vecs"], [acc.k], eng=PL)
                    stt(acc.t[:], cx.t[:, 1:T + 1], V(l, "cA1", j), acc.t[:], ALU.mult, ALU.add,
                        [cx.k, acc.k], [acc.k], eng=PL)
                    stt(acc.t[:], cx.t[:, 0:T], V(l, "cA0", j), acc.t[:], ALU.mult, ALU.add,
                        [cx.k, acc.k], [acc.k], eng=PL)
                    tt(acc.t[:], acc.t[:], psB.t[:], ALU.mult, [acc.k, psB.k], [acc.k])
                    yield
                    wvA, wkA = next_piece()
                    psg = proj(wvA, wkA, 2, hT, hkeys)
                    gA = TMP.get()
                    act(gA.t[:], psg.t[:], AF.Sigmoid, [psg.k, "vecs"], [gA.k], bias=V(l, "mbA", j))
                    tt(Bp[:], acc.t[:], gA.t[:], ALU.mult, [acc.k, gA.k], [K("Bp")])
                    yield
                    psxb = proj(wvA, wkA, 0, hT, hkeys)
                    psgb = proj(wvA, wkA, 1, hT, hkeys)
                    xbb = PB.get()
                    cp(xbb.t[:, 0:3], cB[:, j, :], [("cB", j)], [xbb.k])
                    act(xbb.t[:, 3:T + 3], psxb.t[:], AF.Copy, [psxb.k], [xbb.k])
                    cp(cB[:, j, :], xbb.t[:, T:T + 3], [xbb.k], [("cB", j)])
                    u = TMP.get()
                    ts(u.t[:], xbb.t[:, 3:T + 3], V(l, "cB3", j), V(l, "cBb", j), ALU.mult, ALU.add,
                       [xbb.k, "vecs"], [u.k], eng=PL)
                    for tap in range(3):
                        stt(u.t[:], xbb.t[:, tap:tap + T], V(l, "cB%d" % tap, j), u.t[:], ALU.mult, ALU.add,
                            [xbb.k, u.k], [u.k], eng=PL)
                    ub = TB.get()
                    act(ub.t[:], u.t[:], AF.Copy, [u.k], [ub.k])
                    yield
                    psga = PS.get()
                    mm(psga.t[:], wa_s[:, j, :], ub.t[:], True, True, ["wa_s", ub.k], [psga.k])
                    psgi = PS.get()
                    mm(psgi.t[:], wi_s[:, j, :], ub.t[:], True, True, ["wi_s", ub.k], [psgi.k])
                    ga = TMP.get()
                    act(ga.t[:], psga.t[:], AF.Sigmoid, [psga.k, "vecs"], [ga.k], bias=V(l, "ba", j))
                    gi = TMP.get()
                    act(gi.t[:], psgi.t[:], AF.Sigmoid, [psgi.k, "vecs"], [gi.k], bias=V(l, "bi", j))
                    aa = TMP.get()
                    act(aa.t[:], ga.t[:], AF.Exp, [ga.k, "der"], [aa.k], scale=sp8(j))
                    mlt = ga
                    stt(mlt.t[:], aa.t[:], -1.0, aa.t[:], ALU.mult, ALU.mult, [aa.k], [mlt.k])
                    act(mlt.t[:], mlt.t[:], AF.Sqrt, [mlt.k, "cst"], [mlt.k], bias=C("ones", 1))
                    ts(sm1[:, j:j + 1], mlt.t[:, 0:1], -1.0, 1.0, ALU.mult, ALU.add, [mlt.k], [("sm1", j)])
                    stt(mlt.t[:, 0:1], sm1[:, j:j + 1], FL("first", i), mlt.t[:, 0:1], ALU.mult, ALU.add,
                        [("sm1", j), mlt.k, "vecs"], [mlt.k])
                    tt(u.t[:], u.t[:], gi.t[:], ALU.mult, [u.k, gi.k], [u.k], eng=PL)
                    tt(u.t[:], u.t[:], mlt.t[:], ALU.mult, [u.k, mlt.k], [u.k])
                    yield
                    hb = gi
                    S.op("dve", lambda e, o=hb.t, a=aa.t, uu=u.t, j=j: e.tensor_tensor_scan(
                        out=o[:], data0=a[:], data1=uu[:], initial=cH[:, j:j + 1], op0=ALU.mult, op1=ALU.add),
                        [aa.k, u.k, ("cH", j)], [hb.k])
                    cp(cH[:, j:j + 1], hb.t[:, T - 1:T], [hb.k], [("cH", j)])
                    gx = TMP.get()
                    act(gx.t[:], psgb.t[:], AF.Copy, [psgb.k], [gx.k])
                    g2t = TMP.get()
                    act(g2t.t[:], psgb.t[:], AF.Square, [psgb.k], [g2t.k])
                    ts(g2t.t[:], g2t.t[:], 0.044715, 1.0, ALU.mult, ALU.add, [g2t.k], [g2t.k], eng=PL)
                    tt(g2t.t[:], g2t.t[:], gx.t[:], ALU.mult, [g2t.k, gx.k], [g2t.k], eng=PL)
                    act(g2t.t[:], g2t.t[:], AF.Sigmoid, [g2t.k], [g2t.k], scale=GELU_C)
                    tt(gx.t[:], gx.t[:], g2t.t[:], ALU.mult, [gx.k, g2t.k], [gx.k], eng=PL)
                    tt(gx.t[:], gx.t[:], hb.t[:], ALU.mult, [gx.k, hb.k], [gx.k])
                    yield
                    wvB, wkB = next_piece()
                    psg = proj(wvB, wkB, 0, hT, hkeys)
                    gB = TMP.get()
                    act(gB.t[:], psg.t[:], AF.Sigmoid, [psg.k, "vecs"], [gB.k], bias=V(l, "mbB", j))
                    tt(gx.t[:], gx.t[:], gB.t[:], ALU.mult, [gx.k, gB.k], [gx.k], eng=PL)
                    tt(Bp[:], Bp[:], gx.t[:], ALU.add, [K("Bp"), gx.k], [K("Bp")], eng=PL)
                    yield
                    psgC = proj(wvB, wkB, 1, hT, hkeys)
                    gCt = TMP.get()
                    act(gCt.t[:], psgC.t[:], AF.Sigmoid, [psgC.k, "vecs"], [gCt.k], bias=V(l, "mbC", j))
                    psr = proj(wvB, wkB, 2, hT, hkeys)
                    tshift(psr, 2 + j, V(l, "mu_r", j), RW["r"][:], ["r"])
                    yield
                    wvB, wkB = next_piece()
                    psk = proj(wvB, wkB, 0, hT, hkeys)
                    tshift(psk, 10 + j, V(l, "mu_k", j), RW["k"][:], ["k"])
                    yield
                    psv = proj(wvB, wkB, 1, hT, hkeys)
                    tshift(psv, 18 + j, V(l, "mu_v", j), RW["v"][:], ["v"])
                    yield
                    if i > 0:
                        S.dma(st4[:], rcv_c(8 + j)[:, 0:NSTAGE - 1, :], reads=[("rcv", 2 + j // 4)], writes=["st4"],
                              semkey="st4")
                        ts(vft[:], st4[:, 0, :], FL("sel", i), None, ALU.mult, None, ["st4", "vecs"], ["vft"], eng=PL)
                        for r in range(1, NSTAGE - 1):
                            stt(vft[:], st4[:, r, :], FL("sel", r * NT + i), vft[:], ALU.mult, ALU.add,
                                ["st4", "vft", "vecs"], ["vft"], eng=PL)
                    else:
                        S.op("dve", lambda e: e.memset(vft[:], 0.0), [], ["vft"])
                    ps = PS.get()
                    mm(ps.t[:], v2_s[0:32, js], hv1[:], True, True, ["v2_s", "hv1"], [ps.k])
                    sgv = TMP.get()
                    act(sgv.t[:], ps.t[:], AF.Sigmoid, [ps.k, "vecs"], [sgv.k], bias=V(l, "v0", j))
                    ts(sgv.t[:], sgv.t[:], FL("vres"), None, ALU.mult, None, [sgv.k, "vecs"], [sgv.k], eng=PL)
                    tt(vft[:], vft[:], RW["v"][:], ALU.subtract, ["vft", "v"], ["vft"], eng=PL)
                    vfo = TMP.get()
                    stt(vfo.t[:], vft[:], FL("vres"), RW["v"][:], ALU.mult, ALU.add, ["vft", "v", "vecs"], [vfo.k], eng=PL)
                    S.dma(snd_c(8 + j), vfo.t[:], reads=[vfo.k], writes=[("snd", 8 + j)], semkey=vfo.k)
                    if j % 4 == 3 and i < NT - 1:
                        allgather(2 + j // 4)
                    tt(vft[:], vft[:], sgv.t[:], ALU.mult, ["vft", sgv.k], ["vft"], eng=PL)
                    tt(RW["v"][:], RW["v"][:], vft[:], ALU.add, ["v", "vft"], ["v"], eng=PL)
                    yield
                    ps = PS.get()
                    mm(ps.t[:], w2a2_s[0:64, js], xwa[0:64, :], True, True, ["w2a2_s", "xwa"], [ps.k])
                    act(RW["sw"][:], ps.t[:], AF.Sigmoid, [ps.k, "vecs"], ["sw"], bias=V(l, "w0", j))
                    ps = PS.get()
                    mm(ps.t[:], w2a2_s[64:128, js], xwa[64:128, :], True, True, ["w2a2_s", "xwa"], [ps.k])
                    act(RW["ac"][:], ps.t[:], AF.Sigmoid, [ps.k, "vecs"], ["ac"], bias=V(l, "a0", j))
                    ps = PS.get()
                    mm(ps.t[:], g2_s[:, js], sgx[:], True, True, ["g2_s", "sgx"], [ps.k])
                    tt(G[:], ps.t[:], gCt.t[:], ALU.mult, [ps.k, gCt.k], [K("G")])
                    yield
                    ts(RW["kk"][:], RW["k"][:], V(l, "kkv", j), None, ALU.mult, None, ["k", "vecs"], ["kk"], eng=PL)
                    sq = SQ.get()
                    act(sq.t[:], RW["kk"][:], AF.Square, ["kk"], [sq.k])
                    ps = PS.get()
                    mm(ps.t[:], onesbd_b, sq.t[:], True, True, ["cstb", sq.k], [ps.k])
                    inv = TMP.get()
                    ts(inv.t[:], ps.t[:], 1e-24, None, ALU.max, None, [ps.k], [inv.k])
                    act(inv.t[:], inv.t[:], AF.Ln, [inv.k], [inv.k])
                    act(inv.t[:], inv.t[:], AF.Exp, [inv.k], [inv.k], scale=-0.5)
                    tt(RW["kk"][:], RW["kk"][:], inv.t[:], ALU.mult, ["kk", inv.k], ["kk"], eng=PL)
                    km = TMP.get()
                    ts(km.t[:], RW["ac"][:], V(l, "ka", j), omka(j), ALU.mult, ALU.add, ["ac", "vecs", "der"], [km.k], eng=PL)
                    tt(RW["k"][:], RW["k"][:], km.t[:], ALU.mult, ["k", km.k], ["k"], eng=PL)
                    tt(RW["b"][:], RW["kk"][:], RW["ac"][:], ALU.mult, ["kk", "ac"], ["b"], eng=PL)
                    yield
                    rkb = TB.get()
                    stt(rkb.t[:], RW["r"][:], V(l, "rk", j), RW["k"][:], ALU.mult, ALU.mult, ["r", "k", "vecs"], [rkb.k], eng=PL)
                    ps = PS.get()
                    mm(ps.t[:], onesbd_b, rkb.t[:], True, True, ["cstb", rkb.k], [ps.k])
                    bn = TMP.get()
                    tt(bn.t[:], ps.t[:], RW["v"][:], ALU.mult, [ps.k, "v"], [bn.k])
                    tt(bn.t[:], bn.t[:], G[:], ALU.mult, [bn.k, K("G")], [bn.k], eng=PL)
                    tt(Bp[:], Bp[:], bn.t[:], ALU.add, [K("Bp"), bn.k], [K("Bp")], eng=PL)
                    yield
                    S.op("dve", lambda e: e.tensor_tensor_scan(
                        out=RW["cum"][:], data0=C("cmask", 512), data1=RW["sw"][:], initial=0.0,
                        op0=ALU.mult, op1=ALU.add), ["cst", "sw"], ["cum"])
                    act(RW["epos"][:], RW["cum"][:], AF.Exp, ["cum"], ["epos"], scale=-DECAY_C)
                    act(RW["eneg"][:], RW["cum"][:], AF.Exp, ["cum"], ["eneg"], scale=DECAY_C)
                    cp(gam[:].unsqueeze(2), v3(RW["epos"][:])[:, :, CH - 1:CH], ["epos"], [K("gam")])
                    epv = TMP.get()
                    tt(epv.t[:], RW["cum"][:], RW["sw"][:], ALU.subtract, ["cum", "sw"], [epv.k], eng=PL)
                    act(epv.t[:], epv.t[:], AF.Exp, [epv.k], [epv.k], scale=-DECAY_C)
                    eht = TMP.get()
                    cum3 = v3(RW["cum"][:])
                    tt(v3(eht.t[:]), cum3[:, :, CH - 1:CH].broadcast_to([128, NCH, CH]),
                       cum3, ALU.subtract, ["cum"], [eht.k], eng=PL)
                    act(eht.t[:], eht.t[:], AF.Exp, [eht.k], [eht.k], scale=-DECAY_C)
                    yield

                    def bdv(t_, hh):
                        return t_[:].rearrange("p c (h t) -> p c h t", h=2)[:, :, hh, :]
                    for hh in range(2):
                        hm = cst[:, COFF["hmask"] + hh:COFF["hmask"] + hh + 1]
                        nhm = cst[:, COFF["nhmask"] + hh:COFF["nhmask"] + hh + 1]
                        stt(bdv(bd_a, hh), v3(RW["kk"][:]), nhm, v3(epv.t[:]), ALU.mult, ALU.mult,
                            ["kk", epv.k, "cst"], [K("bd_a")], eng=PL)
                        stt(bdv(BD["b"], hh), v3(RW["b"][:]), hm, v3(RW["eneg"][:]), ALU.mult, ALU.mult,
                            ["b", "eneg", "cst"], ["bd_b"], eng=PL)
                        stt(bdv(BD["k"], hh), v3(RW["k"][:]), hm, v3(RW["eneg"][:]), ALU.mult, ALU.mult,
                            ["k", "eneg", "cst"], ["bd_k"])
                        ts(bdv(BD["v"], hh), v3(RW["v"][:]), hm, None, ALU.mult, None, ["v", "cst"], ["bd_v"], eng=PL)
                    tt(rt[:], RW["r"][:], RW["epos"][:], ALU.mult, ["r", "epos"], [K("rt")])
                    yield

                    def transp(srcnm, dst, dkey):
                        ps = PS.get()
                        pv = ps.t[:].bitcast(BF16).rearrange("p (c t) -> p c t", c=NCH)
                        for c in range(NCH):
                            tr(pv[:, c, :], BD[srcnm][:, c, :], ident_b, ["bd_" + srcnm, "cstb"], [ps.k])
                        act(dst[:], pv, AF.Copy, [ps.k], [dkey])
                    for hh in range(2):
                        hm = cst[:, COFF["hmask"] + hh:COFF["hmask"] + hh + 1]
                        stt(bdv(BD["h"], hh), v3(RW["b"][:]), hm, v3(eht.t[:]), ALU.mult, ALU.mult,
                            ["b", eht.k, "cst"], ["bd_h"], eng=PL)
                    transp("h", BhT, K("BhT"))
                    yield
                    for hh in range(2):
                        hm = cst[:, COFF["hmask"] + hh:COFF["hmask"] + hh + 1]
                        stt(bdv(BD["h"], hh), v3(RW["k"][:]), hm, v3(eht.t[:]), ALU.mult, ALU.mult,
                            ["k", eht.k, "cst"], ["bd_h"])
                    transp("h", KhT, K("KhT"))
                    yield
                    transp("v", VT, K("VT"))
                    yield
                    for g in range(2):
                        ps = PS.get()
                        for cc in range(4):
                            c = g * 4 + cc
                            mm(pv4(ps)[:, cc, :], BD["k"][:, c, :], bd_a[:, c, :], True, True, ["bd_k", K("bd_a")], [ps.k])
                        tt(AakT[:, g * 4:(g + 1) * 4, :], pv4(ps), mSU4, ALU.mult, [ps.k, "cst"], [K("AakT")])
                    yield
                    for g in range(2):
                        psN = PS.get()
                        for cc in range(4):
                            c = g * 4 + cc
                            mm(pv4(psN)[:, cc, :], BD["b"][:, c, :], bd_a[:, c, :], True, True, ["bd_b", K("bd_a")], [psN.k])
                        tt(Nq[g][0][:], pv4(psN), mSU4, ALU.mult, [psN.k, "cst"], [("Nq", g, 0)])
                        psM = PS.get()
                        for cc in range(4):
                            c = g * 4 + cc
                            mm(pv4(psM)[:, cc, :], bd_a[:, c, :], BD["b"][:, c, :], True, True, ["bd_b", K("bd_a")], [psM.k])
                        tt(Mq[g][0][:], pv4(psM), mSL4, ALU.mult, [psM.k, "cst"], [("Mq", g, 0)])
                        tt(Xq[g][0][:], Nq[g][0][:], idn4, ALU.add, [("Nq", g, 0), "cst"], [("Xq", g, 0)], eng=PL)
                    yield
                    for lev in range(1, 6):
                        s_, d_ = (lev - 1) % 2, lev % 2
                        for g in range(2):
                            if lev < 5:
                                psN = PS.get()
                                for cc in range(4):
                                    mm(pv4(psN)[:, cc, :], Mq[g][s_][:, cc, :], Nq[g][s_][:, cc, :], True, True,
                                       [("Mq", g, s_), ("Nq", g, s_)], [psN.k])
                                act(Nq[g][d_][:], pv4(psN), AF.Copy, [psN.k], [("Nq", g, d_)])
                            psM = PS.get()
                            for cc in range(4):
                                mm(pv4(psM)[:, cc, :], Nq[g][s_][:, cc, :], Mq[g][s_][:, cc, :], True, True,
                                   [("Mq", g, s_), ("Nq", g, s_)], [psM.k])
                            act(Mq[g][d_][:], pv4(psM), AF.Copy, [psM.k], [("Mq", g, d_)])
                        yield
                        for g in range(2):
                            psX = PS.get()
                            for cc in range(4):
                                mm(pv4(psX)[:, cc, :], Mq[g][d_][:, cc, :], Xq[g][s_][:, cc, :], True, True,
                                   [("Mq", g, d_), ("Xq", g, s_)], [psX.k])
                            if lev < 5:
                                tt(Xq[g][d_][:], pv4(psX), Xq[g][s_][:], ALU.add, [psX.k, ("Xq", g, s_)], [("Xq", g, d_)])
                            else:
                                tt(Xf[:, g * 4:(g + 1) * 4, :], pv4(psX), Xq[g][s_][:], ALU.add,
                                   [psX.k, ("Xq", g, s_)], [K("Xf")])
                        yield
                    ps = PS.get()
                    pv8 = ps.t[:].rearrange("p (c t) -> p c t", c=NCH)
                    for c in range(NCH):
                        mm(pv8[:, c, :], BD["b"][:, c, :], rt[:, c * CH:(c + 1) * CH], True, True, ["bd_b", K("rt")], [ps.k])
                    tt(ArbT[:], pv8, mIU8, ALU.mult, [ps.k, "cst"], [K("ArbT")])
                    ps = PS.get()
                    pv8 = ps.t[:].rearrange("p (c t) -> p c t", c=NCH)
                    for c in range(NCH):
                        mm(pv8[:, c, :], BD["k"][:, c, :], rt[:, c * CH:(c + 1) * CH], True, True, ["bd_k", K("rt")], [ps.k])
                    tt(ArkT[:], pv8, mIU8, ALU.mult, [ps.k, "cst"], [K("ArkT")])
                    yield

                def advance(gen, n):
                    if gen is None:
                        return None
                    for _ in range(n):
                        try:
                            next(gen)
                        except StopIteration:
                            return None
                    return gen

                def chain(j, gen):
                    p = j % 2
                    bd_a, rt, BhT, KhT, VT = BDA[p], rtP[p], BhTP[p], KhTP[p], VTP[p]
                    Xf, AakT, ArbT, ArkT = XfP[p], AakTP[p], ArbTP[p], ArkTP[p]
                    G, Bp, gam = GP[p], BpP[p], gamP[p]

                    def K(nm):
                        return (nm, p)
                    psY = PSY
                    HfK, HbK = ("Hf", j), ("Hb", j)
                    for c in range(NCH):
                        cs = slice(c * CH, (c + 1) * CH)
                        psW = PQ.get()
                        mm(psW.t[:, 0:128], bd_a[:, c, :], Hb[:, j, :], True, False, [K("bd_a"), HbK], [psW.k])
                        mm(psW.t[:, 0:128], AakT[:, c, :], VT[:, c, :], False, True, [K("AakT"), K("VT")], [psW.k])
                        wb_ = WU.get()
                        act(wb_.t[:], psW.t[:, 0:128], AF.Copy, [psW.k], [wb_.k])
                        gen = advance(gen, 1)
                        psU = PQ.get()
                        mm(psU.t[:, 0:128], Xf[:, c, :], wb_.t[:], True, True, [K("Xf"), wb_.k], [psU.k])
                        ub_ = WU.get()
                        act(ub_.t[:], psU.t[:, 0:128], AF.Copy, [psU.k], [ub_.k])
                        gen = advance(gen, 1)
                        mm(psY.t[:, cs], Hb[:, j, :], rt[:, cs], True, False, [HbK, K("rt")], [psY.k])
                        mm(psY.t[:, cs], ub_.t[:], ArbT[:, c, :], False, False, [ub_.k, K("ArbT")], [psY.k])
                        mm(psY.t[:, cs], VT[:, c, :], ArkT[:, c, :], False, True, [K("VT"), K("ArkT")], [psY.k])
                        psH = PQ.get()
                        mm(psH.t[:, 0:128], BhT[:, c, :], ub_.t[:], True, False, [K("BhT"), ub_.k], [psH.k])
                        mm(psH.t[:, 0:128], KhT[:, c, :], VT[:, c, :], False, True, [K("KhT"), K("VT")], [psH.k])
                        stt(Hb[:, j, :], Hf[:, j, :], gam[:, c:c + 1], psH.t[:, 0:128],
                            ALU.mult, ALU.add, [HfK, K("gam"), psH.k], [HbK])
                        stt(Hf[:, j, :], Hf[:, j, :], gam[:, c:c + 1], psH.t[:, 0:128],
                            ALU.mult, ALU.add, [HfK, K("gam"), psH.k], [HfK])
                        gen = advance(gen, 2)
                    while gen is not None:
                        gen = advance(gen, 8)
                    ysb = TMP.get()
                    act(ysb.t[:], psY.t[:], AF.Copy, [psY.k], [ysb.k])
                    ysq = TMP.get()
                    act(ysq.t[:], psY.t[:], AF.Square, [psY.k], [ysq.k])
                    psm = PS.get()
                    mm(psm.t[:], C("ones_bd", 128), ysb.t[:], True, True, ["cst", ysb.k], [psm.k])
                    pss = PS.get()
                    mm(pss.t[:], C("ones_bd", 128), ysq.t[:], True, True, ["cst", ysq.k], [pss.k])
                    mu = TMP.get()
                    ts(mu.t[:], psm.t[:], 1.0 / 64, None, ALU.mult, None, [psm.k], [mu.k])
                    var = ysq
                    stt(var.t[:], mu.t[:], -1.0, mu.t[:], ALU.mult, ALU.mult, [mu.k], [var.k], eng=PL)
                    stt(var.t[:], pss.t[:], 1.0 / 64, var.t[:], ALU.mult, ALU.add, [pss.k, var.k], [var.k])
                    act(var.t[:], var.t[:], AF.Ln, [var.k, "cst"], [var.k], bias=C("lnxeps", 1))
                    act(var.t[:], var.t[:], AF.Exp, [var.k], [var.k], scale=-0.5)
                    tt(ysb.t[:], ysb.t[:], mu.t[:], ALU.subtract, [ysb.k, mu.k], [ysb.k], eng=PL)
                    tt(ysb.t[:], ysb.t[:], var.t[:], ALU.mult, [ysb.k, var.k], [ysb.k])
                    ts(ysb.t[:], ysb.t[:], V(l, "lg", j), V(l, "lb", j), ALU.mult, ALU.add, [ysb.k, "vecs"], [ysb.k], eng=PL)
                    tt(ysb.t[:], ysb.t[:], G[:], ALU.mult, [ysb.k, K("G")], [ysb.k])
                    tt(mb[:, j, :], Bp[:], ysb.t[:], ALU.add, [K("Bp"), ysb.k], [("mb", j)], eng=PL)

                g0 = prep(0)
                while advance(g0, 8) is not None:
                    pass
                for j in range(NJ):
                    chain(j, prep(j + 1) if j + 1 < NJ else None)

                mkeys = [("mb", j) for j in range(NJ)]
                for h in range(2):
                    wv, wk = next_piece()
                    for jo in range(4 * h, 4 * h + 4):
                        ps = proj(wv, wk, jo - 4 * h, mb, mkeys)
                        tt(xT[:, jo, :], xT[:, jo, :], ps.t[:], ALU.add, [XK(jo), ps.k], [XK(jo)])
                dbg("x_attn", xT[:, 0, :], XK(0), l == 0 and i == 0)
                dbg("x_attn7", xT[:, 7, :], XK(7), l == 0 and i == 0)
                rmsnorm_to(l, "g2n", hT, "hT")
                for q in range(4):
                    for h in range(2):
                        wv1, wk1 = next_piece()
                        for f in range(4 * h, 4 * h + 4):
                            ps = proj(wv1, wk1, f - 4 * h, hT, hkeys)
                            rl = TMP.get()
                            act(rl.t[:], ps.t[:], AF.Relu, [ps.k], [rl.k])
                            tt(hid[:, f, :], rl.t[:], rl.t[:], ALU.mult, [rl.k], [("mb", f)], eng="pool")
                    for jo in range(NJ):
                        if jo % 4 == 0:
                            wv2, wk2 = next_piece()
                        ps = PS.get()
                        for f in range(HG):
                            mm(ps.t[:], wv2[:, f, (jo % 4) * 128:(jo % 4 + 1) * 128], hid[:, f, :], f == 0, f == HG - 1,
                               [wk2, ("mb", f)], [ps.k])
                        tt(xT[:, jo, :], xT[:, jo, :], ps.t[:], ALU.add, [XK(jo), ps.k], [XK(jo)])
                dbg("x_mlp", xT[:, 0, :], XK(0), l == 0 and i == 0)
                for q in range(2):
                    S.dma(snd_t[q].ap().rearrange("p (c t) -> p c t", c=4), xT[:, 4 * q:4 * q + 4, :],
                          reads=[XK(j) for j in range(4 * q, 4 * q + 4)],
                          writes=[("snd", j) for j in range(4 * q, 4 * q + 4)], semkey=("xT_st", q))
                ps = PS.get()
                for j in range(NJ):
                    sq = SQ.get()
                    act(sq.t[:], xT[:, j, :], AF.Square, [XK(j)], [sq.k])
                    mm(ps.t[:], ones_b, sq.t[:], j == 0, j == NJ - 1, [sq.k, "cstb"], [ps.k])
                rstd = TMP.get()
                act(rstd.t[:], ps.t[:], AF.Ln, [ps.k, "cst"], [rstd.k], bias=C("eps", 1), scale=1.0 / D)
                act(rstd.t[:], rstd.t[:], AF.Exp, [rstd.k], [rstd.k], scale=-0.5)
                for j in range(NJ):
                    g = vecs[:, NVEC + j:NVEC + j + 1]
                    o = TMP.get()
                    stt(o.t[:], xT[:, j, :], g, rstd.t[:], ALU.mult, ALU.mult, [XK(j), rstd.k, "vecs"], [o.k])
                    S.dma(out_d[:, j, t0:t0 + T], o.t[:], reads=[o.k], writes=[("od", i, j)], semkey=o.k)
                if i < NT - 1:
                    allgather(0)
                    allgather(1)
        S.final_wait([("od", i, j) for i in range(NT) for j in range(NJ)] + [("dbg", n) for n in range(len(DBG_NAMES))])
        with nc.Block() as block:
            S.emit(block)
        print("program: ops=%d sems=%d sbuf_left=%d" % (S.nops, len(S.handles), nc.sbuf_bytes_remaining))
    return nc


_PROG_CACHE = {}


def run_model(inputs, S_total, DEPTH, debug=False):
    assert DEPTH == NSTAGE
    x = np.asarray(inputs["x"], np.float32)
    B = x.shape[0]
    assert B == NGROUP
    NTS = S_total // T
    NTICK = NTS + NSTAGE - 1
    FOFF, NFLAG = flag_layout(NTICK)
    perm = _cperm()
    cst = _consts()
    nc = build_program(NTICK, debug=debug)
    lw = [_pack_layer_w(inputs, l, perm)[None] for l in range(DEPTH)]
    lv = [_pack_layer_vecs(inputs, l, DEPTH) for l in range(DEPTH)]
    fg = _fm(inputs["final_g"])
    in_maps = []
    for c in range(NGROUP * NSTAGE):
        b, st = c // NSTAGE, c % NSTAGE
        xfm = x[b].reshape(S_total, NJ, 128).transpose(2, 1, 0)
        reps = -(-NTICK * T // S_total)
        xb = np.ascontiguousarray(np.concatenate([xfm] * reps, axis=2)[:, :, 0:NTICK * T])
        fl = np.zeros((128, NFLAG), np.float32)
        for k in range(NTICK):
            own = (st == 0) or (k < st)
            fl[:, FOFF["selx"] + k] = 1.0 if own else 0.0
            if not own:
                fl[:, FOFF["sel"] + (st - 1) * NTICK + k] = 1.0
        if st > 0:
            fl[:, FOFF["vres"]] = 1.0
        fl[:, FOFF["keep"]:FOFF["keep"] + NTICK] = 1.0
        fl[:, FOFF["keep"] + st] = 0.0
        fl[:, FOFF["first"] + st] = 1.0
        vec_all = np.concatenate([lv[st], fg, fl], axis=1)
        in_maps.append({"x": xb, "vecs": np.ascontiguousarray(vec_all), "wts": lw[st], "consts": cst})
    res = run_bass_kernel_spmd(nc, in_maps, core_ids=list(range(NGROUP * NSTAGE)))
    _PROG_CACHE["res"] = res if debug else None
    outs = []
    for b in range(B):
        o = np.asarray(res.results[b * NSTAGE + NSTAGE - 1]["out"])
        o = o[:, :, (NSTAGE - 1) * T:(NSTAGE - 1) * T + S_total]
        outs.append(o.transpose(2, 1, 0).reshape(S_total, D))
    if debug:
        return np.stack(outs, axis=0).astype(np.float32), {n: np.asarray(res.results[0]["dbg"])[i]
                                                           for i, n in enumerate(DBG_NAMES)}
    return np.stack(outs, axis=0).astype(np.float32)


def kernel(**inputs):
    return run_model(inputs, 16384, 4)
```

```python
import numpy as np
from contextlib import ExitStack
import concourse.bass as bass
import concourse.mybir as mybir
from concourse.bass_utils import run_bass_kernel_spmd

F32 = mybir.dt.float32
BF16 = mybir.dt.bfloat16
AF = mybir.ActivationFunctionType
ALU = mybir.AluOpType

D = 1024
NJ = 8
T = 512
CH = 64
NCH = T // CH
N_IN = 11520
D_FF = 4096
EPS = 1e-6
LNX_EPS = 64 * 1e-5
DECAY_C = 0.6065306597126334
GELU_C = 1.5957691216057308

_VNAMES = [("g1", 8), ("g2n", 8), ("mbA", 8), ("mbB", 8), ("mbC", 8), ("cA0", 8), ("cA1", 8), ("cA2", 8),
           ("cB0", 8), ("cB1", 8), ("cB2", 8), ("cB3", 8), ("cBb", 8), ("ba", 8), ("bi", 8), ("ap", 8),
           ("mu_lr", 2), ("mu_r", 8), ("mu_k", 8), ("mu_v", 8), ("w0", 8), ("a0", 8), ("kkv", 8), ("ka", 8),
           ("rk", 8), ("lg", 8), ("lb", 8), ("v0", 8)]
VOFF = {}
_o = 0
for _n, _w in _VNAMES:
    VOFF[_n] = _o
    _o += _w
NVEC = _o

WOFF = {}
_o = 0
for _n, _w in [("win", 8 * N_IN), ("wout", 8 * D), ("w1", 8 * D_FF), ("w2", 32 * D), ("w2a2", D), ("g2", D),
               ("v1", 8 * 32), ("v2", D), ("wa", 8 * 128), ("wi", 8 * 128)]:
    WOFF[_n] = _o
    _o += _w
WTOT = _o
CVT = 4096

COFF = {}
_o = 0
for _n, _w in [("ident", 128), ("ones_bd", 128), ("mSU", 128), ("mSL", 128), ("mIU", 64), ("cmask", 512),
               ("hmask", 2), ("nhmask", 2), ("ones", 128), ("eps", 1), ("lnxeps", 1)]:
    COFF[_n] = _o
    _o += _w
NCONST = _o


def _cperm():
    c0 = 8192
    perm = list(range(c0 + 3072, c0 + 3328))
    for j in range(NJ):
        s = slice(j * 128, (j + 1) * 128)
        for base in (0, 1024, 2048, 3072, 4096, 5120, 6144, 7168, c0, c0 + 1024, c0 + 2048):
            perm += list(range(base + s.start, base + s.stop))
    return np.array(perm, dtype=np.int64)


def _fm(v):
    return np.ascontiguousarray(np.asarray(v, np.float32).reshape(-1, 128).T)


def _pack_layer_vecs(inp, l, L):
    out = np.zeros((128, NVEC), np.float32)

    def put(name, arr):
        a = _fm(arr)
        out[:, VOFF[name]:VOFF[name] + a.shape[1]] = a

    put("g1", inp["norm1_g"][l]); put("g2n", inp["norm2_g"][l])
    mb = inp["merge_b"][l]
    put("mbA", mb[0:1024]); put("mbB", mb[1024:2048]); put("mbC", mb[2048:3072])
    for t in range(3):
        put("cA%d" % t, inp["conv_a_w"][l, t])
    for t in range(4):
        put("cB%d" % t, inp["lru_conv_w"][l, t])
    put("cBb", inp["lru_conv_b"][l]); put("ba", inp["lru_ba"][l]); put("bi", inp["lru_bi"][l])
    put("ap", inp["lru_a_param"][l])
    mu = inp["rwkv_mu"][l]
    put("mu_lr", mu[3072:3328]); put("mu_r", mu[0:1024]); put("mu_k", mu[1024:2048]); put("mu_v", mu[2048:3072])
    put("w0", inp["rwkv_w0"][l]); put("a0", inp["rwkv_a0"][l]); put("kkv", inp["rwkv_kk"][l])
    put("ka", inp["rwkv_ka"][l]); put("rk", inp["rwkv_rk"][l].reshape(-1))
    put("lg", inp["rwkv_lnx_g"][l]); put("lb", inp["rwkv_lnx_b"][l])
    if l > 0:
        put("v0", inp["rwkv_v0"][l - 1])
    return out


def _pack_layer_w(inp, l, perm):
    out = np.zeros((128, WTOT), np.float32)

    def put(name, a):
        a = np.asarray(a, np.float32).reshape(a.shape[0], -1)
        out[:a.shape[0], WOFF[name]:WOFF[name] + a.shape[1]] = a

    def kc(w):
        K, N = w.shape
        return np.ascontiguousarray(np.asarray(w, np.float32).reshape(K // 128, 128, N).transpose(1, 0, 2))

    put("win", kc(inp["w_in"][l][:, perm]))
    put("wout", kc(inp["w_out"][l]))
    put("w1", kc(inp["mlp_w1"][l]))
    put("w2", kc(inp["mlp_w2"][l]))
    put("w2a2", np.concatenate([inp["rwkv_w2"][l], inp["rwkv_a2"][l]], axis=0))
    put("g2", inp["rwkv_g2"][l])
    if l > 0:
        put("v1", kc(inp["rwkv_v1"][l - 1]))
        put("v2", inp["rwkv_v2"][l - 1])
    for nm, key in (("wa", "lru_wa"), ("wi", "lru_wi")):
        bd = np.zeros((128, 8, 128), np.float32)
        w = np.asarray(inp[key][l], np.float32)
        for j in range(8):
            bd[0:64, j, 0:64] = w[2 * j]
            bd[64:128, j, 64:128] = w[2 * j + 1]
        put(nm, bd)
    return out


def _consts():
    c = np.zeros((128, NCONST), np.float32)
    p = np.arange(128)
    hp, sp = p // 64, p % 64
    c[:, COFF["ident"]:COFF["ident"] + 128] = np.eye(128)
    same = (hp[:, None] == hp[None, :])
    c[:, COFF["ones_bd"]:COFF["ones_bd"] + 128] = same
    c[:, COFF["mSU"]:COFF["mSU"] + 128] = same & (sp[:, None] < sp[None, :])
    c[:, COFF["mSL"]:COFF["mSL"] + 128] = same & (sp[:, None] > sp[None, :])
    c[:, COFF["mIU"]:COFF["mIU"] + 64] = (sp[:, None] <= np.arange(64)[None, :])
    c[:, COFF["cmask"]:COFF["cmask"] + 512] = (np.arange(512) % 64 != 0)[None, :]
    for hh in range(2):
        c[:, COFF["hmask"] + hh] = (hp == hh)
        c[:, COFF["nhmask"] + hh] = -(hp == hh).astype(np.float32)
    c[:, COFF["ones"]:COFF["ones"] + 128] = 1.0
    c[:, COFF["eps"]] = EPS
    c[:, COFF["lnxeps"]] = LNX_EPS
    return c


class _Ctr:
    def __init__(self, S, name):
        self.S, self.name, self.gen = S, name, 0
        self.sid = S.new_sem(name)
        self.v = 0

    def bump(self, inc):
        if self.v + inc > 30000:
            self.gen += 1
            self.sid = self.S.new_sem("%s_%d" % (self.name, self.gen))
            self.v = 0
        self.v += inc
        return self.sid, self.v


class Sched:
    ENG = ("pe", "act", "dve", "pool", "sp")

    def __init__(self, nc, stack):
        self.nc, self.stack = nc, stack
        self.handles = []
        self.streams = {e: [] for e in self.ENG}
        self.ctr = {e: _Ctr(self, "e_" + e) for e in self.ENG if e != "sp"}
        self.waited = {e: {} for e in self.ENG}
        self.lw, self.rd, self.dctr = {}, {}, {}
        self.nops = 0

    def new_sem(self, name):
        h = self.stack.enter_context(self.nc.semaphore(name))
        self.handles.append(h)
        return len(self.handles) - 1

    def _deps(self, eng, reads, writes):
        need = {}

        def add(ev):
            sid, val, src = ev
            if self.waited[eng].get(sid, 0) >= val:
                return
            if need.get(sid, 0) < val:
                need[sid] = val

        for k in reads:
            ev = self.lw.get(k)
            if ev is not None and not (eng == "pe" and ev[2] == "pe"):
                add(ev)
        for k in writes:
            ev = self.lw.get(k)
            if ev is not None and ev[2] != eng:
                add(ev)
            for sid, (val, src) in self.rd.get(k, {}).items():
                if src != eng:
                    add((sid, val, src))
        for sid, val in need.items():
            self.waited[eng][sid] = val
            self.streams[eng].append(("w", sid, val))

    def _record(self, me, reads, writes):
        for k in writes:
            self.lw[k] = me
            self.rd[k] = {}
        for k in reads:
            d = self.rd.setdefault(k, {})
            if d.get(me[0], (0, None))[0] < me[1]:
                d[me[0]] = (me[1], me[2])

    def op(self, eng, fn, reads=(), writes=()):
        self._deps(eng, reads, writes)
        sid, val = self.ctr[eng].bump(1)
        self.streams[eng].append(("o", fn, sid))
        self._record((sid, val, eng), reads, writes)
        self.nops += 1

    def dma(self, out_ap, in_ap, reads=(), writes=(), semkey=None, q="sp"):
        self._deps(q, reads, writes)
        c = self.dctr.get(semkey)
        if c is None:
            c = self.dctr[semkey] = _Ctr(self, "d_" + str(semkey))
        sid, val = c.bump(16)
        self.streams[q].append(("d", out_ap, in_ap, sid))
        self._record((sid, val, "dma"), reads, writes)
        self.nops += 1

    def cc(self, fn, reads=(), writes=()):
        self._deps("pool", reads, writes)
        c = self.dctr.get("cc")
        if c is None:
            c = self.dctr["cc"] = _Ctr(self, "d_cc")
        sid, val = c.bump(1)
        self.streams["pool"].append(("o", fn, sid))
        self._record((sid, val, "dma"), reads, writes)
        self.nops += 1

    def barrier(self):
        evs = []
        for e, c in self.ctr.items():
            if c.v > 0:
                evs.append((c.sid, c.v))
        for c in self.dctr.values():
            if c.v > 0:
                evs.append((c.sid, c.v))
        for eng in self.ENG:
            for sid, val in evs:
                if self.waited[eng].get(sid, 0) < val:
                    self.waited[eng][sid] = val
                    self.streams[eng].append(("w", sid, val))

    def final_wait(self, keys):
        self._deps("sp", list(keys), [])

    def emit(self, block):
        H = self.handles

        def mk(name):
            items = self.streams[name]

            def body(e):
                for it in items:
                    if it[0] == "w":
                        e.wait_ge(H[it[1]], it[2])
                    elif it[0] == "o":
                        it[1](e).then_inc(H[it[2]], 1)
                    else:
                        e.dma_start(out=it[1], in_=it[2]).then_inc(H[it[3]], 16)
            return body

        block.tensor(mk("pe"))
        block.scalar(mk("act"))
        block.vector(mk("dve"))
        block.gpsimd(mk("pool"))
        block.sync(mk("sp"))


class Buf:
    def __init__(self, t, k):
        self.t, self.k = t, k


class Pool:
    def __init__(self, bufs, name):
        self.bufs = bufs
        self.name = name
        self.i = 0

    def get(self):
        b = self.bufs[self.i % len(self.bufs)]
        self.i += 1
        return b


DBG_NAMES = []


NSTAGE = 4
NGROUP = 2


def flag_layout(NTICK):
    off = {"selx": 0, "sel": NTICK, "vres": (1 + NSTAGE) * NTICK, "keep": (1 + NSTAGE) * NTICK + 1,
           "first": (2 + NSTAGE) * NTICK + 1}
    return off, (3 + NSTAGE) * NTICK + 1


def build_program(NT, debug=False):
    DEPTH = 1
    NTOK = NT * T
    FOFF, NFLAG = flag_layout(NT)
    NV_ALL = NVEC + 8 + NFLAG
    nc = bass.Bass("TRN2", target_bir_lowering=False)
    x_in = nc.dram_tensor("x", [128, NJ, NTOK], F32, kind="ExternalInput").ap()
    vecs_d = nc.dram_tensor("vecs", [128, NV_ALL], F32, kind="ExternalInput").ap()
    wts_d = nc.dram_tensor("wts", [DEPTH, 128, WTOT], F32, kind="ExternalInput").ap()
    cst_d = nc.dram_tensor("consts", [128, NCONST], F32, kind="ExternalInput").ap()
    out_d = nc.dram_tensor("out", [128, NJ, NTOK], F32, kind="ExternalOutput").ap()
    snd_t = [nc.dram_tensor("snd%d" % q, [128, 4 * T], F32) for q in range(4)]
    rcv_t = [nc.dram_tensor("rcv%d" % q, [NSTAGE * 128, 4 * T], F32) for q in range(4)]

    def snd_c(c):
        return snd_t[c // 4].ap().rearrange("p (c t) -> p c t", c=4)[:, c % 4, :]

    def rcv_c(c):
        return rcv_t[c // 4].ap().rearrange("(r p) (c t) -> p r c t", r=NSTAGE, c=4)[:, :, c % 4, :]
    wb_d = nc.dram_tensor("wb", [DEPTH, 128, WTOT], BF16).ap()
    dbg_d = nc.dram_tensor("dbg", [24, 128, T], F32, kind="ExternalOutput").ap() if debug else None
    del DBG_NAMES[:]

    with ExitStack() as stack:
        S = Sched(nc, stack)

        def sb(name, shape, dt):
            return stack.enter_context(nc.sbuf_tensor("s_" + name, shape, dt))

        def mkpool(name, n, shape, dt):
            return Pool([Buf(sb("%s%d" % (name, i), shape, dt), (name, i)) for i in range(n)], name)

        def act(out, in_, func, reads, writes, bias=None, scale=None):
            kw = {}
            if bias is not None:
                kw["bias"] = bias
            if scale is not None:
                kw["scale"] = scale
            S.op("act", lambda e: e.activation(out=out, in_=in_, func=func, **kw), reads, writes)

        def tt(out, a, b, op, reads, writes, eng="dve"):
            S.op(eng, lambda e: e.tensor_tensor(out=out, in0=a, in1=b, op=op), reads, writes)

        def ts(out, a, s1, s2, op0, op1, reads, writes, eng="dve"):
            eng = "dve"
            if op1 is None:
                S.op(eng, lambda e: e.tensor_scalar(out=out, in0=a, scalar1=s1, scalar2=None, op0=op0), reads, writes)
            else:
                S.op(eng, lambda e: e.tensor_scalar(out=out, in0=a, scalar1=s1, scalar2=s2, op0=op0, op1=op1),
                     reads, writes)

        def stt(out, a, s, b, op0, op1, reads, writes, eng="dve"):
            eng = "dve"
            S.op(eng, lambda e: e.scalar_tensor_tensor(out=out, in0=a, scalar=s, in1=b, op0=op0, op1=op1),
                 reads, writes)

        def cp(out, in_, reads, writes, eng="dve"):
            S.op(eng, lambda e: e.tensor_copy(out=out, in_=in_), reads, writes)

        def rcp(out, in_, reads, writes):
            S.op("dve", lambda e: e.reciprocal(out=out, in_=in_), reads, writes)

        def mm(out, lhsT, rhs, start, stop, reads, writes):
            S.op("pe", lambda e: e.matmul(out, lhsT=lhsT, rhs=rhs, start=start, stop=stop), reads, writes)

        def tr(out, in_, ident, reads, writes):
            S.op("pe", lambda e: e.transpose(out, in_, ident), reads, writes)

        def dbg(name, ap, key, cond=True):
            if debug and cond:
                n = len(DBG_NAMES)
                DBG_NAMES.append(name)
                S.dma(dbg_d[n], ap, reads=[key], writes=[("dbg", n)], semkey=("dbg", n))

        cst = sb("cst", [128, NCONST], F32)
        cstb = sb("cstb", [128, 384], BF16)
        vecs = sb("vecs", [128, NV_ALL], F32)
        der = sb("der", [128, DEPTH * 16], F32)
        S.dma(cst[:], cst_d, writes=["cst"], semkey="cst")
        S.dma(vecs[:], vecs_d, writes=["vecs"], semkey="vecs")
        cp(cstb[:, 0:256], cst[:, COFF["ident"]:COFF["ident"] + 256], ["cst"], ["cstb"])
        cp(cstb[:, 256:384], cst[:, COFF["ones"]:COFF["ones"] + 128], ["cst"], ["cstb"])
        ident_b = cstb[:, 0:128]
        onesbd_b = cstb[:, 128:256]
        ones_b = cstb[:, 256:384]

        def C(name, w):
            return cst[:, COFF[name]:COFF[name] + w]

        def FL(name, k=0):
            o = NVEC + 8 + FOFF[name] + k
            return vecs[:, o:o + 1]

        def V(l, name, j):
            o = l * NVEC + VOFF[name] + j
            return vecs[:, o:o + 1]

        with ExitStack() as pstack:
            stg = [pstack.enter_context(nc.sbuf_tensor("stg%d" % i, [128, CVT], F32)) for i in range(3)]
            stb = [pstack.enter_context(nc.sbuf_tensor("stb%d" % i, [128, CVT], BF16)) for i in range(3)]
            n = 0
            for l in range(DEPTH):
                o = l * 16
                act(der[:, o:o + 8], vecs[:, l * NVEC + VOFF["ap"]:l * NVEC + VOFF["ap"] + 8], AF.Exp,
                    ["vecs"], ["der"])
                ts(der[:, o:o + 8], der[:, o:o + 8], 1.0, None, ALU.add, None, ["der"], ["der"])
                act(der[:, o:o + 8], der[:, o:o + 8], AF.Ln, ["der"], ["der"])
                ts(der[:, o:o + 8], der[:, o:o + 8], -8.0, None, ALU.mult, None, ["der"], ["der"])
                ts(der[:, o + 8:o + 16], vecs[:, l * NVEC + VOFF["ka"]:l * NVEC + VOFF["ka"] + 8], -1.0, 1.0,
                   ALU.mult, ALU.add, ["vecs"], ["der"])
                for c0 in range(0, WTOT, CVT):
                    w = min(CVT, WTOT - c0)
                    s = n % 3
                    S.dma(stg[s][:, 0:w], wts_d[l][:, c0:c0 + w], writes=[("stg", s)], semkey=("stg", s))
                    if n % 2 == 0:
                        cp(stb[s][:, 0:w], stg[s][:, 0:w], [("stg", s)], [("stb", s)])
                    else:
                        act(stb[s][:, 0:w], stg[s][:, 0:w], AF.Copy, [("stg", s)], [("stb", s)])
                    S.dma(wb_d[l][:, c0:c0 + w], stb[s][:, 0:w], reads=[("stb", s)], writes=["wbd"],
                          semkey=("stb", s))
                    n += 1
        S.barrier()

        xT = sb("xT", [128, NJ, T], F32)
        hT = sb("hT", [128, NJ, T], BF16)
        mb = sb("mb", [128, NJ, T], BF16)
        HG = 8
        hid = mb
        WB = mkpool("wb", 4, [128, 4096], BF16)
        wa_s = sb("wa_s", [128, 8, 128], BF16)
        wi_s = sb("wi_s", [128, 8, 128], BF16)
        w2a2_s = sb("w2a2_s", [128, D], BF16)
        g2_s = sb("g2_s", [128, D], BF16)
        v1_s = sb("v1_s", [128, 8, 32], BF16)
        v2_s = sb("v2_s", [128, D], BF16)
        TMP = mkpool("tmp", 9, [128, T], F32)
        PB = mkpool("pb", 3, [128, T + 3], F32)
        SQ = mkpool("sq", 2, [128, T], BF16)
        TB = mkpool("tb", 3, [128, T], BF16)
        _psb = [Buf(stack.enter_context(nc.psum_tensor("ps%d" % i, [128, 512], F32)), ("ps", i))
                for i in range(8)]
        PS = Pool(_psb[0:5], "ps")
        PQ = Pool(_psb[5:7], "pq")
        PSY = _psb[7]
        cTS = sb("cTS", [128, 26], F32)
        cA = sb("cA", [128, NJ, 2], F32)
        cB = sb("cB", [128, NJ, 3], F32)
        cH = sb("cH", [128, NJ], F32)
        Hf = sb("Hf", [128, NJ, 128], F32)
        Hb = sb("Hb", [128, NJ, 128], BF16)
        xwa = sb("xwa", [128, T], BF16)
        sgx = sb("sgx", [128, T], BF16)
        hv1 = sb("hv1", [32, T], BF16)
        RW = {nm: sb("rw_" + nm, [128, T], F32) for nm in
              ("r", "k", "v", "kk", "b", "ac", "sw", "cum", "epos", "eneg")}
        BD = {nm: sb("bd_" + nm, [128, NCH, 128], BF16) for nm in ("b", "k", "h", "v")}
        BDA = [sb("bd_a%d" % p, [128, NCH, 128], BF16) for p in range(2)]
        rtP = [sb("rt%d" % p, [128, T], BF16) for p in range(2)]
        BhTP = [sb("BhT%d" % p, [128, NCH, 128], BF16) for p in range(2)]
        KhTP = [sb("KhT%d" % p, [128, NCH, 128], BF16) for p in range(2)]
        VTP = [sb("VT%d" % p, [128, NCH, 128], BF16) for p in range(2)]
        GP = [sb("G%d" % p, [128, T], F32) for p in range(2)]
        BpP = [sb("Bp%d" % p, [128, T], F32) for p in range(2)]
        gamP = [sb("gam%d" % p, [128, NCH], F32) for p in range(2)]
        Nq = [[sb("Nq%d_%d" % (g, i), [128, 4, 128], BF16) for i in range(2)] for g in range(2)]
        Mq = [[sb("Mq%d_%d" % (g, i), [128, 4, 128], BF16) for i in range(2)] for g in range(2)]
        Xq = [[sb("Xq%d_%d" % (g, i), [128, 4, 128], BF16) for i in range(2)] for g in range(2)]
        XfP = [sb("Xf%d" % p, [128, NCH, 128], BF16) for p in range(2)]
        AakTP = [sb("AakT%d" % p, [128, NCH, 128], BF16) for p in range(2)]
        ArbTP = [sb("ArbT%d" % p, [128, NCH, 64], BF16) for p in range(2)]
        ArkTP = [sb("ArkT%d" % p, [128, NCH, 64], BF16) for p in range(2)]
        WU = mkpool("wu", 4, [128, 128], BF16)
        vft = sb("vft", [128, T], F32)
        st4 = sb("st4", [128, NSTAGE - 1, T], F32)
        sm1 = sb("sm1", [128, NJ], F32)

        pieces = []
        for l in range(DEPTH):
            win3 = wb_d[l][:, WOFF["win"]:WOFF["win"] + 8 * N_IN].rearrange("p (k c) -> p k c", k=8)
            wout3 = wb_d[l][:, WOFF["wout"]:WOFF["wout"] + 8 * D].rearrange("p (k c) -> p k c", k=8)
            w13 = wb_d[l][:, WOFF["w1"]:WOFF["w1"] + 8 * D_FF].rearrange("p (k c) -> p k c", k=8)
            w23 = wb_d[l][:, WOFF["w2"]:WOFF["w2"] + 32 * D].rearrange("p (k c) -> p k c", k=32)
            for i in range(NT):
                pieces.append((win3[:, :, 0:256], (8, 256)))
                for j in range(NJ):
                    c0 = 256 + j * 1408
                    pieces.append((win3[:, :, c0:c0 + 384], (8, 384)))
                    pieces.append((win3[:, :, c0 + 384:c0 + 768], (8, 384)))
                    pieces.append((win3[:, :, c0 + 768:c0 + 1152], (8, 384)))
                    pieces.append((win3[:, :, c0 + 1152:c0 + 1408], (8, 256)))
                for h in range(2):
                    pieces.append((wout3[:, :, h * 512:(h + 1) * 512], (8, 512)))
                for q in range(4):
                    for h in range(2):
                        pieces.append((w13[:, :, q * 1024 + h * 512:q * 1024 + (h + 1) * 512], (8, 512)))
                    for h in range(2):
                        pieces.append((w23[:, q * 8:(q + 1) * 8, h * 512:(h + 1) * 512], (8, 512)))
        pstate = {"issued": 0, "next": 0, "slots": {}}

        def _issue(n):
            src, (k, c) = pieces[n]
            b = WB.get()
            view = b.t[:, 0:k * c].rearrange("p (k c) -> p k c", k=k)
            S.dma(view, src, reads=["wbd"], writes=[b.k], semkey=b.k)
            pstate["slots"][n] = (view, b.k)

        def next_piece():
            n = pstate["next"]
            pstate["next"] += 1
            while pstate["issued"] < min(len(pieces), n + 4):
                _issue(pstate["issued"])
                pstate["issued"] += 1
            return pstate["slots"].pop(n)

        def XK(j):
            return ("xT", j)

        def allgather(q):
            S.cc(lambda e: e.collective_compute(
                "AllGather", ALU.bypass,
                replica_groups=[list(range(g * NSTAGE, (g + 1) * NSTAGE)) for g in range(NGROUP)],
                ins=[snd_t[q].ap().opt()], outs=[rcv_t[q].ap().opt()]),
                reads=[("snd", c) for c in range(4 * q, 4 * q + 4)], writes=[("rcv", q)])

        def proj(wv, wk, cidx, rhsT, rkeys):
            ps = PS.get()
            for kc in range(8):
                mm(ps.t[:], wv[:, kc, cidx * 128:(cidx + 1) * 128], rhsT[:, kc, :], kc == 0, kc == 7,
                   [wk] + rkeys, [ps.k])
            return ps

        def rmsnorm_to(l, gname, dst, dkeyname):
            ps = PS.get()
            for j in range(NJ):
                sq = SQ.get()
                act(sq.t[:], xT[:, j, :], AF.Square, [XK(j)], [sq.k])
                mm(ps.t[:], ones_b, sq.t[:], j == 0, j == NJ - 1, [sq.k, "cstb"], [ps.k])
            rstd = TMP.get()
            act(rstd.t[:], ps.t[:], AF.Ln, [ps.k, "cst"], [rstd.k], bias=C("eps", 1), scale=1.0 / D)
            act(rstd.t[:], rstd.t[:], AF.Exp, [rstd.k], [rstd.k], scale=-0.5)
            for j in range(NJ):
                if gname == "final":
                    g = vecs[:, DEPTH * NVEC + j:DEPTH * NVEC + j + 1]
                else:
                    g = V(l, gname, j)
                stt(dst[:, j, :], xT[:, j, :], g, rstd.t[:], ALU.mult, ALU.mult,
                    [XK(j), rstd.k, "vecs"], [(dkeyname, j)])

        def tshift(ps, cidx, mu_ap, out_ap, okeys):
            pb = PB.get()
            cp(pb.t[:, 0:1], cTS[:, cidx:cidx + 1], [("cTS", cidx)], [pb.k])
            act(pb.t[:, 1:T + 1], ps.t[:], AF.Copy, [ps.k], [pb.k])
            cp(cTS[:, cidx:cidx + 1], pb.t[:, T:T + 1], [pb.k], [("cTS", cidx)])
            d = TMP.get()
            tt(d.t[:], pb.t[:, 0:T], pb.t[:, 1:T + 1], ALU.subtract, [pb.k], [d.k], eng="pool")
            stt(out_ap, d.t[:], mu_ap, pb.t[:, 1:T + 1], ALU.mult, ALU.add, [d.k, pb.k, "vecs"], okeys, eng="pool")

        for l in range(DEPTH):
            for buf, key in ((cTS, None), (cA, None), (cB, None), (cH, None), (Hf, None), (Hb, None)):
                pass
            S.op("dve", lambda e: e.memset(cTS[:], 0.0), [], [("cTS", c) for c in range(26)])
            S.op("dve", lambda e: e.memset(cA[:], 0.0), [], [("cA", j) for j in range(NJ)])
            S.op("dve", lambda e: e.memset(cB[:], 0.0), [], [("cB", j) for j in range(NJ)])
            S.op("dve", lambda e: e.memset(cH[:], 0.0), [], [("cH", j) for j in range(NJ)])
            S.op("dve", lambda e: e.memset(Hf[:], 0.0), [], [("Hf", j) for j in range(NJ)])
            S.op("dve", lambda e: e.memset(Hb[:], 0.0), [], [("Hb", j) for j in range(NJ)])
            wl = wb_d[l]
            S.dma(wa_s[:], wl[:, WOFF["wa"]:WOFF["wa"] + 1024].rearrange("p (k c) -> p k c", k=8), ["wbd"],
                  ["wa_s"], semkey="wa_s")
            S.dma(wi_s[:], wl[:, WOFF["wi"]:WOFF["wi"] + 1024].rearrange("p (k c) -> p k c", k=8), ["wbd"],
                  ["wi_s"], semkey="wi_s")
            S.dma(w2a2_s[:], wl[:, WOFF["w2a2"]:WOFF["w2a2"] + D], ["wbd"], ["w2a2_s"], semkey="w2a2_s")
            S.dma(g2_s[:], wl[:, WOFF["g2"]:WOFF["g2"] + D], ["wbd"], ["g2_s"], semkey="g2_s")
            if True:
                S.dma(v1_s[:], wl[:, WOFF["v1"]:WOFF["v1"] + 256].rearrange("p (k c) -> p k c", k=8), ["wbd"],
                      ["v1_s"], semkey="v1_s")
                S.dma(v2_s[:], wl[:, WOFF["v2"]:WOFF["v2"] + D], ["wbd"], ["v2_s"], semkey="v2_s")
            sp8 = lambda j, l=l: der[:, l * 16 + j:l * 16 + j + 1]
            omka = lambda j, l=l: der[:, l * 16 + 8 + j:l * 16 + 8 + j + 1]
            HK = ["hT%d" % 0]
            hkeys = [("hT", j) for j in range(NJ)]

            for i in range(NT):
                t0 = i * T
                keep = FL("keep", i)
                for stt_, kn, n_ in ((cTS[:], "cTS", 26), (cA[:].rearrange("p a b -> p (a b)"), "cA", NJ),
                                     (cB[:].rearrange("p a b -> p (a b)"), "cB", NJ), (cH[:], "cH", NJ),
                                     (Hf[:].rearrange("p a b -> p (a b)"), "Hf", NJ),
                                     (Hb[:].rearrange("p a b -> p (a b)"), "Hb", NJ)):
                    ks_ = [(kn, c) for c in range(n_)]
                    ts(stt_, stt_, keep, None, ALU.mult, None, ks_ + ["vecs"], ks_)
                for j in range(NJ):
                    xi = TMP.get()
                    S.dma(xi.t[:], x_in[:, j, t0:t0 + T], writes=[xi.k], semkey=xi.k)
                    ts(xT[:, j, :], xi.t[:], FL("selx", i), None, ALU.mult, None, [xi.k, "vecs"], [XK(j)])
                    if i > 0:
                        for r in range(NSTAGE - 1):
                            xc = TMP.get()
                            S.dma(xc.t[:], rcv_c(j)[:, r, :], reads=[("rcv", j // 4)], writes=[xc.k], semkey=xc.k)
                            stt(xT[:, j, :], xc.t[:], FL("sel", r * NT + i), xT[:, j, :], ALU.mult, ALU.add,
                                [xc.k, XK(j), "vecs"], [XK(j)])
                rmsnorm_to(l, "g1", hT, "hT")

                wv, wk = next_piece()
                ps = proj(wv, wk, 0, hT, hkeys)
                xs_t = TMP.get()
                tshift(ps, 0, V(l, "mu_lr", 0), xs_t.t[:], [xs_t.k])
                act(xwa[0:64, :], xs_t.t[0:64, :], AF.Tanh, [xs_t.k], ["xwa"])
                act(xwa[64:128, :], xs_t.t[64:128, :], AF.Copy, [xs_t.k], ["xwa"])
                ps = proj(wv, wk, 1, hT, hkeys)
                xg_t = TMP.get()
                tshift(ps, 1, V(l, "mu_lr", 1), xg_t.t[:], [xg_t.k])
                act(sgx[:], xg_t.t[:], AF.Sigmoid, [xg_t.k], ["sgx"])
                if True:
                    ps = PS.get()
                    for kc in range(8):
                        mm(ps.t[0:32, :], v1_s[:, kc, :], hT[:, kc, :], kc == 0, kc == 7, ["v1_s"] + hkeys, [ps.k])
                    act(hv1[:], ps.t[0:32, :], AF.Copy, [ps.k], ["hv1"])

                PL = "pool"

                def v3(ap):
                    return ap.rearrange("p (c t) -> p c t", c=NCH)

                def pv4(ps):
                    return ps.t[:].rearrange("p (c t) -> p c t", c=4)

                mSU4 = C("mSU", 128).unsqueeze(1).broadcast_to([128, 4, 128])
                mSL4 = C("mSL", 128).unsqueeze(1).broadcast_to([128, 4, 128])
                idn4 = C("ident", 128).unsqueeze(1).broadcast_to([128, 4, 128])
                mIU8 = C("mIU", 64).unsqueeze(1).broadcast_to([128, NCH, 64])

                def prep(j):
                    p = j % 2
                    js = slice(j * 128, (j + 1) * 128)
                    bd_a, rt, BhT, KhT, VT = BDA[p], rtP[p], BhTP[p], KhTP[p], VTP[p]
                    Xf, AakT, ArbT, ArkT = XfP[p], AakTP[p], ArbTP[p], ArkTP[p]
                    G, Bp, gam = GP[p], BpP[p], gamP[p]

                    def K(nm):
                        return (nm, p)
                    wvA, wkA = next_piece()
                    psB = proj(wvA, wkA, 0, hT, hkeys)
                    psC = proj(wvA, wkA, 1, hT, hkeys)
                    psX = proj(wvA, wkA, 2, hT, hkeys)
                    csb = TMP.get()
                    act(csb.t[:], psC.t[:], AF.Copy, [psC.k], [csb.k])
                    cx = PB.get()
                    cp(cx.t[:, 0:2], cA[:, j, :], [("cA", j)], [cx.k])
                    tt(cx.t[:, 2:T + 2], csb.t[:], psX.t[:], ALU.mult, [csb.k, psX.k], [cx.k])
                    cp(cA[:, j, :], cx.t[:, T:T + 2], [cx.k], [("cA", j)])
                    acc = TMP.get()
                    ts(acc.t[:], cx.t[:, 2:T + 2], V(l, "cA2", j), None, ALU.mult, None, [cx.k, "vecs"], [acc.k], eng=PL)
                    stt(acc.t[:], cx.t[:, 1:T + 1], V(l, "cA1", j), acc.t[:], ALU.mult, ALU.add,
                        [cx.k, acc.k], [acc.k], eng=PL)
                    stt(acc.t[:], cx.t[:, 0:T], V(l, "cA0", j), acc.t[:], ALU.mult, ALU.add,
                        [cx.k, acc.k], [acc.k], eng=PL)
                    tt(acc.t[:], acc.t[:], psB.t[:], ALU.mult, [acc.k, psB.k], [acc.k])
                    yield
                    wvA, wkA = next_piece()
                    psg = proj(wvA, wkA, 2, hT, hkeys)
                    gA = TMP.get()
                    act(gA.t[:], psg.t[:], AF.Sigmoid, [psg.k, "vecs"], [gA.k], bias=V(l, "mbA", j))
                    tt(Bp[:], acc.t[:], gA.t[:], ALU.mult, [acc.k, gA.k], [K("Bp")])
                    yield
                    psxb = proj(wvA, wkA, 0, hT, hkeys)
                    psgb = proj(wvA, wkA, 1, hT, hkeys)
                    xbb = PB.get()
                    cp(xbb.t[:, 0:3], cB[:, j, :], [("cB", j)], [xbb.k])
                    act(xbb.t[:, 3:T + 3], psxb.t[:], AF.Copy, [psxb.k], [xbb.k])
                    cp(cB[:, j, :], xbb.t[:, T:T + 3], [xbb.k], [("cB", j)])
                    u = TMP.get()
                    ts(u.t[:], xbb.t[:, 3:T + 3], V(l, "cB3", j), V(l, "cBb", j), ALU.mult, ALU.add,
                       [xbb.k, "vecs"], [u.k], eng=PL)
                    for tap in range(3):
                        stt(u.t[:], xbb.t[:, tap:tap + T], V(l, "cB%d" % tap, j), u.t[:], ALU.mult, ALU.add,
                            [xbb.k, u.k], [u.k], eng=PL)
                    ub = TB.get()
                    act(ub.t[:], u.t[:], AF.Copy, [u.k], [ub.k])
                    yield
                    psga = PS.get()
                    mm(psga.t[:], wa_s[:, j, :], ub.t[:], True, True, ["wa_s", ub.k], [psga.k])
                    psgi = PS.get()
                    mm(psgi.t[:], wi_s[:, j, :], ub.t[:], True, True, ["wi_s", ub.k], [psgi.k])
                    ga = TMP.get()
                    act(ga.t[:], psga.t[:], AF.Sigmoid, [psga.k, "vecs"], [ga.k], bias=V(l, "ba", j))
                    gi = TMP.get()
                    act(gi.t[:], psgi.t[:], AF.Sigmoid, [psgi.k, "vecs"], [gi.k], bias=V(l, "bi", j))
                    aa = TMP.get()
                    act(aa.t[:], ga.t[:], AF.Exp, [ga.k, "der"], [aa.k], scale=sp8(j))
                    mlt = ga
                    stt(mlt.t[:], aa.t[:], -1.0, aa.t[:], ALU.mult, ALU.mult, [aa.k], [mlt.k])
                    act(mlt.t[:], mlt.t[:], AF.Sqrt, [mlt.k, "cst"], [mlt.k], bias=C("ones", 1))
                    ts(sm1[:, j:j + 1], mlt.t[:, 0:1], -1.0, 1.0, ALU.mult, ALU.add, [mlt.k], [("sm1", j)])
                    stt(mlt.t[:, 0:1], sm1[:, j:j + 1], FL("first", i), mlt.t[:, 0:1], ALU.mult, ALU.add,
                        [("sm1", j), mlt.k, "vecs"], [mlt.k])
                    tt(u.t[:], u.t[:], gi.t[:], ALU.mult, [u.k, gi.k], [u.k], eng=PL)
                    tt(u.t[:], u.t[:], mlt.t[:], ALU.mult, [u.k, mlt.k], [u.k])
                    yield
                    hb = gi
                    S.op("dve", lambda e, o=hb.t, a=aa.t, uu=u.t, j=j: e.tensor_tensor_scan(
                        out=o[:], data0=a[:], data1=uu[:], initial=cH[:, j:j + 1], op0=ALU.mult, op1=ALU.add),
                        [aa.k, u.k, ("cH", j)], [hb.k])
                    cp(cH[:, j:j + 1], hb.t[:, T - 1:T], [hb.k], [("cH", j)])
                    gx = TMP.get()
                    act(gx.t[:], psgb.t[:], AF.Copy, [psgb.k], [gx.k])
                    g2t = TMP.get()
                    act(g2t.t[:], psgb.t[:], AF.Square, [psgb.k], [g2t.k])
                    ts(g2t.t[:], g2t.t[:], 0.044715, 1.0, ALU.mult, ALU.add, [g2t.k], [g2t.k], eng=PL)
                    tt(g2t.t[:], g2t.t[:], gx.t[:], ALU.mult, [g2t.k, gx.k], [g2t.k], eng=PL)
                    act(g2t.t[:], g2t.t[:], AF.Sigmoid, [g2t.k], [g2t.k], scale=GELU_C)
                    tt(gx.t[:], gx.t[:], g2t.t[:], ALU.mult, [gx.k, g2t.k], [gx.k], eng=PL)
                    tt(gx.t[:], gx.t[:], hb.t[:], ALU.mult, [gx.k, hb.k], [gx.k])
                    yield
                    wvB, wkB = next_piece()
                    psg = proj(wvB, wkB, 0, hT, hkeys)
                    gB = TMP.get()
                    act(gB.t[:], psg.t[:], AF.Sigmoid, [psg.k, "vecs"], [gB.k], bias=V(l, "mbB", j))
                    tt(gx.t[:], gx.t[:], gB.t[:], ALU.mult, [gx.k, gB.k], [gx.k], eng=PL)
                    tt(Bp[:], Bp[:], gx.t[:], ALU.add, [K("Bp"), gx.k], [K("Bp")], eng=PL)
                    yield
                    psgC = proj(wvB, wkB, 1, hT, hkeys)
                    gCt = TMP.get()
                    act(gCt.t[:], psgC.t[:], AF.Sigmoid, [psgC.k, "vecs"], [gCt.k], bias=V(l, "mbC", j))
                    psr = proj(wvB, wkB, 2, hT, hkeys)
                    tshift(psr, 2 + j, V(l, "mu_r", j), RW["r"][:], ["r"])
                    yield
                    wvB, wkB = next_piece()
                    psk = proj(wvB, wkB, 0, hT, hkeys)
                    tshift(psk, 10 + j, V(l, "mu_k", j), RW["k"][:], ["k"])
                    yield
                    psv = proj(wvB, wkB, 1, hT, hkeys)
                    tshift(psv, 18 + j, V(l, "mu_v", j), RW["v"][:], ["v"])
                    yield
                    if i > 0:
                        S.dma(st4[:], rcv_c(8 + j)[:, 0:NSTAGE - 1, :], reads=[("rcv", 2 + j // 4)], writes=["st4"],
                              semkey="st4")
                        ts(vft[:], st4[:, 0, :], FL("sel", i), None, ALU.mult, None, ["st4", "vecs"], ["vft"], eng=PL)
                        for r in range(1, NSTAGE - 1):
                            stt(vft[:], st4[:, r, :], FL("sel", r * NT + i), vft[:], ALU.mult, ALU.add,
                                ["st4", "vft", "vecs"], ["vft"], eng=PL)
                    else:
                        S.op("dve", lambda e: e.memset(vft[:], 0.0), [], ["vft"])
                    ps = PS.get()
                    mm(ps.t[:], v2_s[0:32, js], hv1[:], True, True, ["v2_s", "hv1"], [ps.k])
                    sgv = TMP.get()
                    act(sgv.t[:], ps.t[:], AF.Sigmoid, [ps.k, "vecs"], [sgv.k], bias=V(l, "v0", j))
                    ts(sgv.t[:], sgv.t[:], FL("vres"), None, ALU.mult, None, [sgv.k, "vecs"], [sgv.k], eng=PL)
                    tt(vft[:], vft[:], RW["v"][:], ALU.subtract, ["vft", "v"], ["vft"], eng=PL)
                    vfo = TMP.get()
                    stt(vfo.t[:], vft[:], FL("vres"), RW["v"][:], ALU.mult, ALU.add, ["vft", "v", "vecs"], [vfo.k], eng=PL)
                    S.dma(snd_c(8 + j), vfo.t[:], reads=[vfo.k], writes=[("snd", 8 + j)], semkey=vfo.k)
                    if j % 4 == 3 and i < NT - 1:
                        allgather(2 + j // 4)
                    tt(vft[:], vft[:], sgv.t[:], ALU.mult, ["vft", sgv.k], ["vft"], eng=PL)
                    tt(RW["v"][:], RW["v"][:], vft[:], ALU.add, ["v", "vft"], ["v"], eng=PL)
                    yield
                    ps = PS.get()
                    mm(ps.t[:], w2a2_s[0:64, js], xwa[0:64, :], True, True, ["w2a2_s", "xwa"], [ps.k])
                    act(RW["sw"][:], ps.t[:], AF.Sigmoid, [ps.k, "vecs"], ["sw"], bias=V(l, "w0", j))
                    ps = PS.get()
                    mm(ps.t[:], w2a2_s[64:128, js], xwa[64:128, :], True, True, ["w2a2_s", "xwa"], [ps.k])
                    act(RW["ac"][:], ps.t[:], AF.Sigmoid, [ps.k, "vecs"], ["ac"], bias=V(l, "a0", j))
                    ps = PS.get()
                    mm(ps.t[:], g2_s[:, js], sgx[:], True, True, ["g2_s", "sgx"], [ps.k])
                    tt(G[:], ps.t[:], gCt.t[:], ALU.mult, [ps.k, gCt.k], [K("G")])
                    yield
                    ts(RW["kk"][:], RW["k"][:], V(l, "kkv", j), None, ALU.mult, None, ["k", "vecs"], ["kk"], eng=PL)
                    sq = SQ.get()
                    act(sq.t[:], RW["kk"][:], AF.Square, ["kk"], [sq.k])
                    ps = PS.get()
                    mm(ps.t[:], onesbd_b, sq.t[:], True, True, ["cstb", sq.k], [ps.k])
                    inv = TMP.get()
                    ts(inv.t[:], ps.t[:], 1e-24, None, ALU.max, None, [ps.k], [inv.k])
                    act(inv.t[:], inv.t[:], AF.Ln, [inv.k], [inv.k])
                    act(inv.t[:], inv.t[:], AF.Exp, [inv.k], [inv.k], scale=-0.5)
                    tt(RW["kk"][:], RW["kk"][:], inv.t[:], ALU.mult, ["kk", inv.k], ["kk"], eng=PL)
                    km = TMP.get()
                    ts(km.t[:], RW["ac"][:], V(l, "ka", j), omka(j), ALU.mult, ALU.add, ["ac", "vecs", "der"], [km.k], eng=PL)
                    tt(RW["k"][:], RW["k"][:], km.t[:], ALU.mult, ["k", km.k], ["k"], eng=PL)
                    tt(RW["b"][:], RW["kk"][:], RW["ac"][:], ALU.mult, ["kk", "ac"], ["b"], eng=PL)
                    yield
                    rkb = TB.get()
                    stt(rkb.t[:], RW["r"][:], V(l, "rk", j), RW["k"][:], ALU.mult, ALU.mult, ["r", "k", "vecs"], [rkb.k], eng=PL)
                    ps = PS.get()
                    mm(ps.t[:], onesbd_b, rkb.t[:], True, True, ["cstb", rkb.k], [ps.k])
                    bn = TMP.get()
                    tt(bn.t[:], ps.t[:], RW["v"][:], ALU.mult, [ps.k, "v"], [bn.k])
                    tt(bn.t[:], bn.t[:], G[:], ALU.mult, [bn.k, K("G")], [bn.k], eng=PL)
                    tt(Bp[:], Bp[:], bn.t[:], ALU.add, [K("Bp"), bn.k], [K("Bp")], eng=PL)
                    yield
                    S.op("dve", lambda e: e.tensor_tensor_scan(
                        out=RW["cum"][:], data0=C("cmask", 512), data1=RW["sw"][:], initial=0.0,
                        op0=ALU.mult, op1=ALU.add), ["cst", "sw"], ["cum"])
                    act(RW["epos"][:], RW["cum"][:], AF.Exp, ["cum"], ["epos"], scale=-DECAY_C)
                    act(RW["eneg"][:], RW["cum"][:], AF.Exp, ["cum"], ["eneg"], scale=DECAY_C)
                    cp(gam[:].unsqueeze(2), v3(RW["epos"][:])[:, :, CH - 1:CH], ["epos"], [K("gam")])
                    epv = TMP.get()
                    tt(epv.t[:], RW["cum"][:], RW["sw"][:], ALU.subtract, ["cum", "sw"], [epv.k], eng=PL)
                    act(epv.t[:], epv.t[:], AF.Exp, [epv.k], [epv.k], scale=-DECAY_C)
                    eht = TMP.get()
                    cum3 = v3(RW["cum"][:])
                    tt(v3(eht.t[:]), cum3[:, :, CH - 1:CH].broadcast_to([128, NCH, CH]),
                       cum3, ALU.subtract, ["cum"], [eht.k], eng=PL)
                    act(eht.t[:], eht.t[:], AF.Exp, [eht.k], [eht.k], scale=-DECAY_C)
                    yield

                    def bdv(t_, hh):
                        return t_[:].rearrange("p c (h t) -> p c h t", h=2)[:, :, hh, :]
                    for hh in range(2):
                        hm = cst[:, COFF["hmask"] + hh:COFF["hmask"] + hh + 1]
                        nhm = cst[:, COFF["nhmask"] + hh:COFF["nhmask"] + hh + 1]
                        stt(bdv(bd_a, hh), v3(RW["kk"][:]), nhm, v3(epv.t[:]), ALU.mult, ALU.mult,
                            ["kk", epv.k, "cst"], [K("bd_a")], eng=PL)
                        stt(bdv(BD["b"], hh), v3(RW["b"][:]), hm, v3(RW["eneg"][:]), ALU.mult, ALU.mult,
                            ["b", "eneg", "cst"], ["bd_b"], eng=PL)
                        stt(bdv(BD["k"], hh), v3(RW["k"][:]), hm, v3(RW["eneg"][:]), ALU.mult, ALU.mult,
                            ["k", "eneg", "cst"], ["bd_k"])
                        ts(bdv(BD["v"], hh), v3(RW["v"][:]), hm, None, ALU.mult, None, ["v", "cst"], ["bd_v"], eng=PL)
                    tt(rt[:], RW["r"][:], RW["epos"][:], ALU.mult, ["r", "epos"], [K("rt")])
                    yield

                    def transp(srcnm, dst, dkey):
                        ps = PS.get()
                        pv = ps.t[:].bitcast(BF16).rearrange("p (c t) -> p c t", c=NCH)
                        for c in range(NCH):
                            tr(pv[:, c, :], BD[srcnm][:, c, :], ident_b, ["bd_" + srcnm, "cstb"], [ps.k])
                        act(dst[:], pv, AF.Copy, [ps.k], [dkey])
                    for hh in range(2):
                        hm = cst[:, COFF["hmask"] + hh:COFF["hmask"] + hh + 1]
                        stt(bdv(BD["h"], hh), v3(RW["b"][:]), hm, v3(eht.t[:]), ALU.mult, ALU.mult,
                            ["b", eht.k, "cst"], ["bd_h"], eng=PL)
                    transp("h", BhT, K("BhT"))
                    yield
                    for hh in range(2):
                        hm = cst[:, COFF["hmask"] + hh:COFF["hmask"] + hh + 1]
                        stt(bdv(BD["h"], hh), v3(RW["k"][:]), hm, v3(eht.t[:]), ALU.mult, ALU.mult,
                            ["k", eht.k, "cst"], ["bd_h"])
                    transp("h", KhT, K("KhT"))
                    yield
                    transp("v", VT, K("VT"))
                    yield
                    for g in range(2):
                        ps = PS.get()
                        for cc in range(4):
                            c = g * 4 + cc
                            mm(pv4(ps)[:, cc, :], BD["k"][:, c, :], bd_a[:, c, :], True, True, ["bd_k", K("bd_a")], [ps.k])
                        tt(AakT[:, g * 4:(g + 1) * 4, :], pv4(ps), mSU4, ALU.mult, [ps.k, "cst"], [K("AakT")])
                    yield
                    for g in range(2):
                        psN = PS.get()
                        for cc in range(4):
                            c = g * 4 + cc
                            mm(pv4(psN)[:, cc, :], BD["b"][:, c, :], bd_a[:, c, :], True, True, ["bd_b", K("bd_a")], [psN.k])
                        tt(Nq[g][0][:], pv4(psN), mSU4, ALU.mult, [psN.k, "cst"], [("Nq", g, 0)])
                        psM = PS.get()
                        for cc in range(4):
                            c = g * 4 + cc
                            mm(pv4(psM)[:, cc, :], bd_a[:, c, :], BD["b"][:, c, :], True, True, ["bd_b", K("bd_a")], [psM.k])
                        tt(Mq[g][0][:], pv4(psM), mSL4, ALU.mult, [psM.k, "cst"], [("Mq", g, 0)])
                        tt(Xq[g][0][:], Nq[g][0][:], idn4, ALU.add, [("Nq", g, 0), "cst"], [("Xq", g, 0)], eng=PL)
                    yield
                    for lev in range(1, 6):
                        s_, d_ = (lev - 1) % 2, lev % 2
                        for g in range(2):
                            if lev < 5:
                                psN = PS.get()
                                for cc in range(4):
                                    mm(pv4(psN)[:, cc, :], Mq[g][s_][:, cc, :], Nq[g][s_][:, cc, :], True, True,
                                       [("Mq", g, s_), ("Nq", g, s_)], [psN.k])
                                act(Nq[g][d_][:], pv4(psN), AF.Copy, [psN.k], [("Nq", g, d_)])
                            psM = PS.get()
                            for cc in range(4):
                                mm(pv4(psM)[:, cc, :], Nq[g][s_][:, cc, :], Mq[g][s_][:, cc, :], True, True,
                                   [("Mq", g, s_), ("Nq", g, s_)], [psM.k])
                            act(Mq[g][d_][:], pv4(psM), AF.Copy, [psM.k], [("Mq", g, d_)])
                            yield
                        for g in range(2):
                            psX = PS.get()
                            for cc in range(4):
                                mm(pv4(psX)[:, cc, :], Mq[g][d_][:, cc, :], Xq[g][s_][:, cc, :], True, True,
                                   [("Mq", g, d_), ("Xq", g, s_)], [psX.k])
                            if lev < 5:
                                tt(Xq[g][d_][:], pv4(psX), Xq[g][s_][:], ALU.add, [psX.k, ("Xq", g, s_)], [("Xq", g, d_)])
                            else:
                                tt(Xf[:, g * 4:(g + 1) * 4, :], pv4(psX), Xq[g][s_][:], ALU.add,
                                   [psX.k, ("Xq", g, s_)], [K("Xf")])
                            yield
                    ps = PS.get()
                    pv8 = ps.t[:].rearrange("p (c t) -> p c t", c=NCH)
                    for c in range(NCH):
                        mm(pv8[:, c, :], BD["b"][:, c, :], rt[:, c * CH:(c + 1) * CH], True, True, ["bd_b", K("rt")], [ps.k])
                    tt(ArbT[:], pv8, mIU8, ALU.mult, [ps.k, "cst"], [K("ArbT")])
                    ps = PS.get()
                    pv8 = ps.t[:].rearrange("p (c t) -> p c t", c=NCH)
                    for c in range(NCH):
                        mm(pv8[:, c, :], BD["k"][:, c, :], rt[:, c * CH:(c + 1) * CH], True, True, ["bd_k", K("rt")], [ps.k])
                    tt(ArkT[:], pv8, mIU8, ALU.mult, [ps.k, "cst"], [K("ArkT")])
                    yield

                def advance(gen, n):
                    if gen is None:
                        return None
                    for _ in range(n):
                        try:
                            next(gen)
                        except StopIteration:
                            return None
                    return gen

                def chain(j, gen):
                    p = j % 2
                    bd_a, rt, BhT, KhT, VT = BDA[p], rtP[p], BhTP[p], KhTP[p], VTP[p]
                    Xf, AakT, ArbT, ArkT = XfP[p], AakTP[p], ArbTP[p], ArkTP[p]
                    G, Bp, gam = GP[p], BpP[p], gamP[p]

                    def K(nm):
                        return (nm, p)
                    psY = PSY
                    HfK, HbK = ("Hf", j), ("Hb", j)
                    for c in range(NCH):
                        cs = slice(c * CH, (c + 1) * CH)
                        psW = PQ.get()
                        mm(psW.t[:, 0:128], bd_a[:, c, :], Hb[:, j, :], True, False, [K("bd_a"), HbK], [psW.k])
                        mm(psW.t[:, 0:128], AakT[:, c, :], VT[:, c, :], False, True, [K("AakT"), K("VT")], [psW.k])
                        wb_ = WU.get()
                        act(wb_.t[:], psW.t[:, 0:128], AF.Copy, [psW.k], [wb_.k])
                        gen = advance(gen, 1)
                        psU = PQ.get()
                        mm(psU.t[:, 0:128], Xf[:, c, :], wb_.t[:], True, True, [K("Xf"), wb_.k], [psU.k])
                        ub_ = WU.get()
                        act(ub_.t[:], psU.t[:, 0:128], AF.Copy, [psU.k], [ub_.k])
                        gen = advance(gen, 1)
                        mm(psY.t[:, cs], Hb[:, j, :], rt[:, cs], True, False, [HbK, K("rt")], [psY.k])
                        mm(psY.t[:, cs], ub_.t[:], ArbT[:, c, :], False, False, [ub_.k, K("ArbT")], [psY.k])
                        mm(psY.t[:, cs], VT[:, c, :], ArkT[:, c, :], False, True, [K("VT"), K("ArkT")], [psY.k])
                        psH = PQ.get()
                        mm(psH.t[:, 0:128], BhT[:, c, :], ub_.t[:], True, False, [K("BhT"), ub_.k], [psH.k])
                        mm(psH.t[:, 0:128], KhT[:, c, :], VT[:, c, :], False, True, [K("KhT"), K("VT")], [psH.k])
                        stt(Hb[:, j, :], Hf[:, j, :], gam[:, c:c + 1], psH.t[:, 0:128],
                            ALU.mult, ALU.add, [HfK, K("gam"), psH.k], [HbK])
                        stt(Hf[:, j, :], Hf[:, j, :], gam[:, c:c + 1], psH.t[:, 0:128],
                            ALU.mult, ALU.add, [HfK, K("gam"), psH.k], [HfK])
                        gen = advance(gen, 2)
                    while gen is not None:
                        gen = advance(gen, 8)
                    ysb = TMP.get()
                    act(ysb.t[:], psY.t[:], AF.Copy, [psY.k], [ysb.k])
                    ysq = TMP.get()
                    act(ysq.t[:], psY.t[:], AF.Square, [psY.k], [ysq.k])
                    psm = PS.get()
                    mm(psm.t[:], C("ones_bd", 128), ysb.t[:], True, True, ["cst", ysb.k], [psm.k])
                    pss = PS.get()
                    mm(pss.t[:], C("ones_bd", 128), ysq.t[:], True, True, ["cst", ysq.k], [pss.k])
                    mu = TMP.get()
                    ts(mu.t[:], psm.t[:], 1.0 / 64, None, ALU.mult, None, [psm.k], [mu.k])
                    var = ysq
                    stt(var.t[:], mu.t[:], -1.0, mu.t[:], ALU.mult, ALU.mult, [mu.k], [var.k], eng=PL)
                    stt(var.t[:], pss.t[:], 1.0 / 64, var.t[:], ALU.mult, ALU.add, [pss.k, var.k], [var.k])
                    act(var.t[:], var.t[:], AF.Ln, [var.k, "cst"], [var.k], bias=C("lnxeps", 1))
                    act(var.t[:], var.t[:], AF.Exp, [var.k], [var.k], scale=-0.5)
                    tt(ysb.t[:], ysb.t[:], mu.t[:], ALU.subtract, [ysb.k, mu.k], [ysb.k], eng=PL)
                    tt(ysb.t[:], ysb.t[:], var.t[:], ALU.mult, [ysb.k, var.k], [ysb.k])
                    ts(ysb.t[:], ysb.t[:], V(l, "lg", j), V(l, "lb", j), ALU.mult, ALU.add, [ysb.k, "vecs"], [ysb.k], eng=PL)
                    tt(ysb.t[:], ysb.t[:], G[:], ALU.mult, [ysb.k, K("G")], [ysb.k])
                    tt(mb[:, j, :], Bp[:], ysb.t[:], ALU.add, [K("Bp"), ysb.k], [("mb", j)], eng=PL)

                g0 = prep(0)
                while advance(g0, 8) is not None:
                    pass
                for j in range(NJ):
                    chain(j, prep(j + 1) if j + 1 < NJ else None)

                mkeys = [("mb", j) for j in range(NJ)]
                for h in range(2):
                    wv, wk = next_piece()
                    for jo in range(4 * h, 4 * h + 4):
                        ps = proj(wv, wk, jo - 4 * h, mb, mkeys)
                        tt(xT[:, jo, :], xT[:, jo, :], ps.t[:], ALU.add, [XK(jo), ps.k], [XK(jo)])
                dbg("x_attn", xT[:, 0, :], XK(0), l == 0 and i == 0)
                dbg("x_attn7", xT[:, 7, :], XK(7), l == 0 and i == 0)
                rmsnorm_to(l, "g2n", hT, "hT")
                for q in range(4):
                    for h in range(2):
                        wv1, wk1 = next_piece()
                        for f in range(4 * h, 4 * h + 4):
                            ps = proj(wv1, wk1, f - 4 * h, hT, hkeys)
                            rl = TMP.get()
                            act(rl.t[:], ps.t[:], AF.Relu, [ps.k], [rl.k])
                            tt(hid[:, f, :], rl.t[:], rl.t[:], ALU.mult, [rl.k], [("mb", f)], eng="pool")
                    for jo in range(NJ):
                        if jo % 4 == 0:
                            wv2, wk2 = next_piece()
                        ps = PS.get()
                        for f in range(HG):
                            mm(ps.t[:], wv2[:, f, (jo % 4) * 128:(jo % 4 + 1) * 128], hid[:, f, :], f == 0, f == HG - 1,
                               [wk2, ("mb", f)], [ps.k])
                        tt(xT[:, jo, :], xT[:, jo, :], ps.t[:], ALU.add, [XK(jo), ps.k], [XK(jo)])
                dbg("x_mlp", xT[:, 0, :], XK(0), l == 0 and i == 0)
                for q in range(2):
                    S.dma(snd_t[q].ap().rearrange("p (c t) -> p c t", c=4), xT[:, 4 * q:4 * q + 4, :],
                          reads=[XK(j) for j in range(4 * q, 4 * q + 4)],
                          writes=[("snd", j) for j in range(4 * q, 4 * q + 4)], semkey=("xT_st", q))
                ps = PS.get()
                for j in range(NJ):
                    sq = SQ.get()
                    act(sq.t[:], xT[:, j, :], AF.Square, [XK(j)], [sq.k])
                    mm(ps.t[:], ones_b, sq.t[:], j == 0, j == NJ - 1, [sq.k, "cstb"], [ps.k])
                rstd = TMP.get()
                act(rstd.t[:], ps.t[:], AF.Ln, [ps.k, "cst"], [rstd.k], bias=C("eps", 1), scale=1.0 / D)
                act(rstd.t[:], rstd.t[:], AF.Exp, [rstd.k], [rstd.k], scale=-0.5)
                for j in range(NJ):
                    g = vecs[:, NVEC + j:NVEC + j + 1]
                    o = TMP.get()
                    stt(o.t[:], xT[:, j, :], g, rstd.t[:], ALU.mult, ALU.mult, [XK(j), rstd.k, "vecs"], [o.k])
                    S.dma(out_d[:, j, t0:t0 + T], o.t[:], reads=[o.k], writes=[("od", i, j)], semkey=o.k)
                if i < NT - 1:
                    allgather(0)
                    allgather(1)
        S.final_wait([("od", i, j) for i in range(NT) for j in range(NJ)] + [("dbg", n) for n in range(len(DBG_NAMES))])
        with nc.Block() as block:
            S.emit(block)
        print("program: ops=%d sems=%d sbuf_left=%d" % (S.nops, len(S.handles), nc.sbuf_bytes_remaining))
    return nc


_PROG_CACHE = {}


def run_model(inputs, S_total, DEPTH, debug=False):
    assert DEPTH == NSTAGE
    x = np.asarray(inputs["x"], np.float32)
    B = x.shape[0]
    assert B == NGROUP
    NTS = S_total // T
    NTICK = NTS + NSTAGE - 1
    FOFF, NFLAG = flag_layout(NTICK)
    perm = _cperm()
    cst = _consts()
    nc = build_program(NTICK, debug=debug)
    lw = [_pack_layer_w(inputs, l, perm)[None] for l in range(DEPTH)]
    lv = [_pack_layer_vecs(inputs, l, DEPTH) for l in range(DEPTH)]
    fg = _fm(inputs["final_g"])
    in_maps = []
    for c in range(NGROUP * NSTAGE):
        b, st = c // NSTAGE, c % NSTAGE
        xfm = x[b].reshape(S_total, NJ, 128).transpose(2, 1, 0)
        reps = -(-NTICK * T // S_total)
        xb = np.ascontiguousarray(np.concatenate([xfm] * reps, axis=2)[:, :, 0:NTICK * T])
        fl = np.zeros((128, NFLAG), np.float32)
        for k in range(NTICK):
            own = (st == 0) or (k < st)
            fl[:, FOFF["selx"] + k] = 1.0 if own else 0.0
            if not own:
                fl[:, FOFF["sel"] + (st - 1) * NTICK + k] = 1.0
        if st > 0:
            fl[:, FOFF["vres"]] = 1.0
        fl[:, FOFF["keep"]:FOFF["keep"] + NTICK] = 1.0
        fl[:, FOFF["keep"] + st] = 0.0
        fl[:, FOFF["first"] + st] = 1.0
        vec_all = np.concatenate([lv[st], fg, fl], axis=1)
        in_maps.append({"x": xb, "vecs": np.ascontiguousarray(vec_all), "wts": lw[st], "consts": cst})
    res = run_bass_kernel_spmd(nc, in_maps, core_ids=list(range(NGROUP * NSTAGE)))
    _PROG_CACHE["res"] = res if debug else None
    outs = []
    for b in range(B):
        o = np.asarray(res.results[b * NSTAGE + NSTAGE - 1]["out"])
        o = o[:, :, (NSTAGE - 1) * T:(NSTAGE - 1) * T + S_total]
        outs.append(o.transpose(2, 1, 0).reshape(S_total, D))
    if debug:
        return np.stack(outs, axis=0).astype(np.float32), {n: np.asarray(res.results[0]["dbg"])[i]
                                                           for i, n in enumerate(DBG_NAMES)}
    return np.stack(outs, axis=0).astype(np.float32)


def kernel(**inputs):
    return run_model(inputs, 16384, 4)
```
